# Optimizing a Trainium2 kernel written in Bass

```python
import jax, jax.numpy as jnp
from jax import lax
import numpy as np

D_MODEL = 4096
BATCH = 4
SEQ = 2048
DEPTH = 2
DEC_BATCH = 128
DEC_SEQ = 8
PAST_LEN = 16384
PAGE_SIZE = 128

N_AB_LAYERS = (DEPTH + 1) // 2
N_C_LAYERS = DEPTH // 2
RET_WIDTH = D_MODEL // 2
ML_WIDTH = D_MODEL - RET_WIDTH
RET_HEADS = 8
RET_DK = RET_WIDTH // RET_HEADS
RET_DV = RET_WIDTH // RET_HEADS
ML_HEADS = 8
ML_DK = ML_WIDTH // ML_HEADS
ML_DV = ML_WIDTH // ML_HEADS
HG_DK = 128
HG_HEADS = D_MODEL // HG_DK
HG_DV = D_MODEL // HG_HEADS
HG_WIDTH = HG_HEADS * HG_DK
D_FF = ((8 * D_MODEL + 3 * 256 - 1) // (3 * 256)) * 256
RET_CHUNK = 128
ML_CHUNK = 128
HG_CHUNK = 32
ROPE_BASE = 10000.0
EPS = 1e-6
AB_COLS = 4 * RET_WIDTH + 4 * ML_WIDTH + 2 * ML_HEADS
C_COLS = 2 * HG_WIDTH + 2 * HG_HEADS * HG_DV

kernel_name = 'hybrid_retnet_mlstm_hgrn2_step'


def _rmsnorm(x, w):
    xf = x.astype(jnp.float32)
    y = xf * lax.rsqrt(jnp.mean(xf * xf, axis=-1, keepdims=True) + EPS)
    return (y * w.astype(jnp.float32)).astype(x.dtype)


def _head_norm(x):
    return x * lax.rsqrt(jnp.mean(x * x, axis=-1, keepdims=True) + EPS)


def _to_heads(x, n_heads):
    b, t, _ = x.shape
    return x.reshape(b, t, n_heads, -1).transpose(0, 2, 1, 3).astype(jnp.float32)


def _from_heads(x):
    b, h, t, d = x.shape
    return x.transpose(0, 2, 1, 3).reshape(b, t, h * d)


def _chunk_len(t, c):
    return c if t % c == 0 else t


def _chunks(x, l):
    b, h, t = x.shape[:3]
    return jnp.moveaxis(x.reshape((b, h, t // l, l) + x.shape[3:]), 2, 0)


def _unchunk(x):
    n, b, h, l = x.shape[:4]
    return jnp.moveaxis(x, 0, 2).reshape((b, h, n * l) + x.shape[4:])


def _rotary(x, pos):
    half = x.shape[-1] // 2
    inv_freq = ROPE_BASE ** (-jnp.arange(half, dtype=jnp.float32) / half)
    ang = pos[:, None] * inv_freq[None, :]
    cos, sin = jnp.cos(ang), jnp.sin(ang)
    x1, x2 = x[..., :half], x[..., half:]
    return jnp.concatenate([x1 * cos - x2 * sin, x1 * sin + x2 * cos], axis=-1)


def _retention(q, k, v, s0):
    t = q.shape[2]
    l = _chunk_len(t, RET_CHUNK)
    log_gamma = jnp.log1p(-jnp.exp2(-5.0 - jnp.arange(RET_HEADS, dtype=jnp.float32)))[:, None]
    idx = jnp.arange(l, dtype=jnp.float32)
    diff = idx[:, None] - idx[None, :]
    decay = jnp.exp(jnp.where(diff >= 0, log_gamma[:, :, None] * diff, -jnp.inf))
    q_decay = jnp.exp(log_gamma * (idx + 1.0))
    k_decay = jnp.exp(log_gamma * (l - 1.0 - idx))
    chunk_decay = jnp.exp(log_gamma[:, 0] * l)

    def step(s, inp):
        qc, kc, vc = inp
        att = jnp.einsum('bhtd,bhsd->bhts', qc, kc) * decay
        o = (jnp.einsum('bhts,bhse->bhte', att, vc)
             + jnp.einsum('bhtd,bhde->bhte', qc, s) * q_decay[:, :, None])
        s_new = (chunk_decay[:, None, None] * s
                 + jnp.einsum('bhsd,bhse->bhde', kc * k_decay[:, :, None], vc))
        return s_new, o

    s_end, o = lax.scan(step, s0, (_chunks(q, l), _chunks(k, l), _chunks(v, l)))
    return _unchunk(o), s_end


def _mlstm(q, k, v, ig, lf, c0, n0, m0):
    t = q.shape[2]
    l = _chunk_len(t, ML_CHUNK)
    causal = jnp.tril(jnp.ones((l, l), dtype=bool))

    def step(carry, inp):
        c, n, m = carry
        qc, kc, vc, igc, lfc = inp
        f_cum = jnp.cumsum(lfc, axis=-1)
        a = igc - f_cum
        m_t = f_cum + jnp.maximum(m[..., None], lax.cummax(a, axis=2))
        log_d = jnp.where(causal, (f_cum - m_t)[..., :, None] + a[..., None, :], -jnp.inf)
        s = jnp.einsum('bhtd,bhsd->bhts', qc, kc) * jnp.exp(log_d)
        inter = jnp.exp(f_cum + m[..., None] - m_t)
        num = (inter[..., None] * jnp.einsum('bhtd,bhde->bhte', qc, c)
               + jnp.einsum('bhts,bhse->bhte', s, vc))
        den = inter * jnp.einsum('bhtd,bhd->bht', qc, n) + jnp.sum(s, axis=-1)
        h = num / jnp.maximum(jnp.abs(den), jnp.exp(-m_t))[..., None]
        m_end = m_t[..., -1]
        w = jnp.exp(f_cum[..., -1:] - m_end[..., None] + a)
        carry_decay = jnp.exp(f_cum[..., -1] + m - m_end)
        c_new = carry_decay[..., None, None] * c + jnp.einsum('bhs,bhsd,bhse->bhde', w, kc, vc)
        n_new = carry_decay[..., None] * n + jnp.einsum('bhs,bhsd->bhd', w, kc)
        return (c_new, n_new, m_end), h

    (c_end, n_end, m_end), h = lax.scan(
        step, (c0, n0, m0),
        (_chunks(q, l), _chunks(k, l), _chunks(v, l), _chunks(ig, l), _chunks(lf, l)))
    return _unchunk(h), c_end, n_end, m_end


def _hgrn2(q, k, logf, v, s0):
    t = q.shape[2]
    l = _chunk_len(t, HG_CHUNK)
    causal = jnp.tril(jnp.ones((l, l), dtype=bool))[:, :, None]

    def step(s, inp):
        qc, kc, gc, vc = inp
        g_cum = jnp.cumsum(gc, axis=2)
        log_w = jnp.where(causal, g_cum[:, :, :, None, :] - g_cum[:, :, None, :, :], -jnp.inf)
        att = jnp.einsum('bhtd,bhsd,bhtsd->bhts', qc, kc, jnp.exp(log_w))
        o = (jnp.einsum('bhtd,bhde->bhte', qc * jnp.exp(g_cum), s)
             + jnp.einsum('bhts,bhse->bhte', att, vc))
        g_end = g_cum[:, :, -1]
        s_new = (jnp.exp(g_end)[..., None] * s
                 + jnp.einsum('bhsd,bhse->bhde', kc * jnp.exp(g_end[:, :, None] - g_cum), vc))
        return s_new, o

    s_end, o = lax.scan(step, s0, (_chunks(q, l), _chunks(k, l), _chunks(logf, l), _chunks(v, l)))
    return _unchunk(o), s_end


def _ab_mixer(h, pos, w_in, b_if, ml_norm_w, w_out, s_ret, c_ml, n_ml, m_ml):
    f32 = jnp.float32
    proj = jnp.einsum('btd,dc->btc', h, w_in)
    rw, mw = RET_WIDTH, ML_WIDTH
    offs = [int(o) for o in np.cumsum([rw, rw, rw, rw, mw, mw, mw, mw])]
    rq, rk, rv, rg, mq, mk, mv, mo, gates = jnp.split(proj, offs, axis=-1)
    rq = _rotary(_to_heads(rq, RET_HEADS), pos)
    rk = _rotary(_to_heads(rk, RET_HEADS), pos) * RET_DK ** -0.5
    r_o, s_ret_new = _retention(rq, rk, _to_heads(rv, RET_HEADS), s_ret.astype(f32))
    r_out = jax.nn.silu(rg.astype(f32)) * _from_heads(_head_norm(r_o))
    gates = gates.astype(f32) + b_if.astype(f32)
    ig = jnp.swapaxes(gates[..., :ML_HEADS], 1, 2)
    lf = jax.nn.log_sigmoid(jnp.swapaxes(gates[..., ML_HEADS:], 1, 2))
    m_h, c_new, n_new, m_new = _mlstm(
        _to_heads(mq, ML_HEADS), _to_heads(mk, ML_HEADS) * ML_DK ** -0.5, _to_heads(mv, ML_HEADS),
        ig, lf, c_ml.astype(f32), n_ml.astype(f32), m_ml.astype(f32))
    m_out = jax.nn.sigmoid(mo.astype(f32)) * (_from_heads(_head_norm(m_h)) * ml_norm_w.astype(f32))
    mixed = jnp.concatenate([r_out, m_out], axis=-1).astype(h.dtype)
    return jnp.einsum('btc,cd->btd', mixed, w_out), (s_ret_new, c_new, n_new, m_new)


def _c_mixer(h, lb, w_in, g_norm_w, w_out, s_hg):
    f32 = jnp.float32
    proj = jnp.einsum('btd,dc->btc', h, w_in)
    q, f, i, g = jnp.split(proj, [HG_WIDTH, 2 * HG_WIDTH, 2 * HG_WIDTH + HG_HEADS * HG_DV], axis=-1)
    fg = _to_heads(lb + (1.0 - lb) * jax.nn.sigmoid(f.astype(f32)), HG_HEADS)
    q = jax.nn.silu(_to_heads(q, HG_HEADS))
    o, s_new = _hgrn2(q, 1.0 - fg, jnp.log(fg), _to_heads(i, HG_HEADS), s_hg.astype(f32))
    o = _from_heads(_head_norm(o) * g_norm_w.astype(f32)) * jax.nn.silu(g.astype(f32))
    return jnp.einsum('btc,cd->btd', o.astype(h.dtype), w_out), s_new


def _swiglu(h, w_gate, w_up, w_down):
    a = jnp.einsum('btd,df->btf', h, w_gate)
    b = jnp.einsum('btd,df->btf', h, w_up)
    return jnp.einsum('btf,fd->btd', jax.nn.silu(a) * b, w_down)


def _trunk(x, pos, ret0, mc0, mn0, mm0, hg0, norm_mix_w, w_in_ab, b_if_ab, ml_norm_w, w_out_ab,
           w_in_c, lb_logits, hg_norm_w, w_out_c, norm_ffn_w, w_gate, w_up, w_down, norm_final_w):
    cum = jnp.cumsum(jax.nn.softmax(lb_logits.astype(jnp.float32), axis=0), axis=0)
    lower_bounds = cum - cum[0]
    ret_new, mc_new, mn_new, mm_new, hg_new = [], [], [], [], []
    for layer in range(DEPTH):
        j = layer // 2
        hn = _rmsnorm(x, norm_mix_w[layer])
        if layer % 2 == 0:
            mix, (sr, sc, sn, sm) = _ab_mixer(hn, pos, w_in_ab[j], b_if_ab[j], ml_norm_w[j], w_out_ab[j],
                                              ret0[j], mc0[j], mn0[j], mm0[j])
            ret_new.append(sr)
            mc_new.append(sc)
            mn_new.append(sn)
            mm_new.append(sm)
        else:
            mix, sh = _c_mixer(hn, lower_bounds[layer], w_in_c[j], hg_norm_w[j], w_out_c[j], hg0[j])
            hg_new.append(sh)
        x = x + mix.astype(x.dtype)
        x = x + _swiglu(_rmsnorm(x, norm_ffn_w[layer]), w_gate[layer], w_up[layer], w_down[layer]).astype(x.dtype)
    dt = x.dtype
    return (_rmsnorm(x, norm_final_w), jnp.stack(ret_new).astype(dt), jnp.stack(mc_new).astype(dt),
            jnp.stack(mn_new).astype(dt), jnp.stack(mm_new).astype(dt), jnp.stack(hg_new).astype(dt))


def setup_inputs(seed: int = 0) -> dict:
    key = jax.random.key(seed)
    ks = jax.random.split(key, 24)
    f32 = jnp.float32

    def nrm(k, shape, scale):
        return jax.random.normal(k, shape, f32) * scale

    d = D_MODEL
    b_if_ab = jnp.concatenate(
        [nrm(ks[9], (N_AB_LAYERS, ML_HEADS), 0.1),
         jnp.linspace(3.0, 6.0, ML_HEADS, dtype=f32)[None, :] + nrm(ks[10], (N_AB_LAYERS, ML_HEADS), 0.1)],
        axis=-1)
    return {
        'x_prompt': nrm(ks[0], (BATCH, SEQ, d), 1.0),
        'x_sample': nrm(ks[1], (DEC_BATCH, DEC_SEQ, d), 1.0),
        'state_ret': nrm(ks[2], (N_AB_LAYERS, DEC_BATCH, RET_HEADS, RET_DK, RET_DV), 0.5),
        'state_mlstm_C': nrm(ks[3], (N_AB_LAYERS, DEC_BATCH, ML_HEADS, ML_DK, ML_DV), 0.5),
        'state_mlstm_n': nrm(ks[4], (N_AB_LAYERS, DEC_BATCH, ML_HEADS, ML_DK), 0.5),
        'state_mlstm_m': nrm(ks[5], (N_AB_LAYERS, DEC_BATCH, ML_HEADS), 1.0),
        'state_hgrn': nrm(ks[6], (N_C_LAYERS, DEC_BATCH, HG_HEADS, HG_DK, HG_DV), 0.5),
        'norm_mix_w': 1.0 + nrm(ks[7], (DEPTH, d), 0.01),
        'w_in_ab': nrm(ks[8], (N_AB_LAYERS, d, AB_COLS), d ** -0.5),
        'b_if_ab': b_if_ab,
        'ml_norm_w': 1.0 + nrm(ks[11], (N_AB_LAYERS, ML_WIDTH), 0.01),
        'w_out_ab': nrm(ks[12], (N_AB_LAYERS, RET_WIDTH + ML_WIDTH, d), (RET_WIDTH + ML_WIDTH) ** -0.5),
        'w_in_c': nrm(ks[13], (N_C_LAYERS, d, C_COLS), d ** -0.5),
        'lb_logits': nrm(ks[14], (DEPTH, HG_WIDTH), 0.1),
        'hg_norm_w': 1.0 + nrm(ks[15], (N_C_LAYERS, HG_DV), 0.01),
        'w_out_c': nrm(ks[16], (N_C_LAYERS, HG_HEADS * HG_DV, d), (HG_HEADS * HG_DV) ** -0.5),
        'norm_ffn_w': 1.0 + nrm(ks[17], (DEPTH, d), 0.01),
        'w_gate': nrm(ks[18], (DEPTH, d, D_FF), d ** -0.5),
        'w_up': nrm(ks[19], (DEPTH, d, D_FF), d ** -0.5),
        'w_down': nrm(ks[20], (DEPTH, D_FF, d), D_FF ** -0.5),
        'norm_final_w': 1.0 + nrm(ks[21], (d,), 0.01),
    }


def reference(x_prompt, x_sample, state_ret, state_mlstm_C, state_mlstm_n, state_mlstm_m, state_hgrn,
              norm_mix_w, w_in_ab, b_if_ab, ml_norm_w, w_out_ab, w_in_c, lb_logits, hg_norm_w, w_out_c,
              norm_ffn_w, w_gate, w_up, w_down, norm_final_w):
    f32 = jnp.float32
    weights = (norm_mix_w, w_in_ab, b_if_ab, ml_norm_w, w_out_ab, w_in_c, lb_logits, hg_norm_w, w_out_c,
               norm_ffn_w, w_gate, w_up, w_down, norm_final_w)
    b, t = x_prompt.shape[0], x_prompt.shape[1]
    pos_p = jnp.arange(t, dtype=f32)
    pos_s = PAST_LEN + jnp.arange(x_sample.shape[1], dtype=f32)
    ret0 = jnp.zeros((N_AB_LAYERS, b, RET_HEADS, RET_DK, RET_DV), f32)
    mc0 = jnp.zeros((N_AB_LAYERS, b, ML_HEADS, ML_DK, ML_DV), f32)
    mn0 = jnp.zeros((N_AB_LAYERS, b, ML_HEADS, ML_DK), f32)
    mm0 = jnp.zeros((N_AB_LAYERS, b, ML_HEADS), f32)
    hg0 = jnp.zeros((N_C_LAYERS, b, HG_HEADS, HG_DK, HG_DV), f32)
    y_prompt, ret_p, mc_p, mn_p, mm_p, hg_p = _trunk(x_prompt, pos_p, ret0, mc0, mn0, mm0, hg0, *weights)
    y_sample, ret_s, mc_s, mn_s, mm_s, hg_s = _trunk(
        x_sample, pos_s, state_ret, state_mlstm_C, state_mlstm_n, state_mlstm_m, state_hgrn, *weights)
    return (y_prompt, y_sample, ret_p, mc_p, mn_p, mm_p, hg_p, ret_s, mc_s, mn_s, mm_s, hg_s)
```

```python
import numpy as np
import ml_dtypes
from contextlib import ExitStack
import concourse.bass as bass
import concourse.mybir as mybir
from concourse.bass_utils import run_bass_kernel_spmd

F32 = mybir.dt.float32
BF16 = mybir.dt.bfloat16
AF = mybir.ActivationFunctionType
ALU = mybir.AluOpType
AX = mybir.AxisListType
EPS = 1e-6


class Cfg:
    def __init__(self, D=4096, RH=8, MH=8, DFF=11008, NT=1, ncores=8, batch=4, seq=2048, dec_batch=128,
                 past_len=16384, xch=True):
        self.xch = xch
        assert (NT == 1) if xch else True
        self.groups = [[2 * i, 2 * i + 1] for i in range(ncores // 2)]
        self.D = D
        self.KC = D // 128
        self.RH, self.MH = RH, MH
        self.RW = D // 2
        self.MW = D - self.RW
        assert self.RW // RH == 256 and self.MW // MH == 256
        self.HH = D // 128
        self.DFF = DFF
        self.FC = DFF // 128
        assert DFF % 128 == 0
        self.NT = NT
        self.ncores = ncores
        self.T = 1152
        self.TP = 1024
        self.NS = 16
        self.batch, self.seq, self.dec_batch, self.past_len = batch, seq, dec_batch, past_len
        self.ABC = 4 * self.RW + 4 * self.MW + 2 * MH
        self.CC = 4 * D
        nq = 4 if self.FC >= 8 else 1
        base = self.FC // nq
        rem = self.FC % nq
        self.fq = []
        s = 0
        for i in range(nq):
            n = base + (1 if i < rem else 0)
            self.fq.append((s, n))
            s += n


class Ev:
    __slots__ = ("sem", "val", "key", "dkey")

    def __init__(self, sem, val, key, dkey=None):
        self.sem, self.val, self.key, self.dkey = sem, val, key, dkey


class Prog:
    EPOCH = 30000

    def __init__(self, nc):
        self.nc = nc
        self.eng = {"pe": nc.tensor, "act": nc.scalar, "dve": nc.vector, "pool": nc.gpsimd, "sp": nc.sync}
        self.sem = {}
        self.cnt = {}
        self.nsem = 0
        for e in self.eng:
            self._new_epoch(e)
        self.known = {e: {} for e in self.eng}
        self.lastw = {}
        self.readers = {}
        self.dsem = {}
        self.dcnt = {}
        self.n_ins = 0
        self.n_wait = 0
        self.filler = None
        self.in_fill = False

    def _new_epoch(self, e):
        self.nsem += 1
        self.sem[e] = self.nc.alloc_semaphore(name="s_%s_%d" % (e, self.nsem))
        self.cnt[e] = 0

    def _wait(self, e, ev):
        if e == "pe" and ev.key == id(self.sem["pe"]):
            return
        k = self.known[e]
        val = ev.val
        if ev.dkey is not None:
            val = max(val, self.dcnt[ev.dkey])
        if k.get(ev.key, 0) >= val:
            return
        self.eng[e].wait_ge(ev.sem, val)
        self.n_wait += 1
        k[ev.key] = val

    def _deps(self, e, reads, writes):
        for r in reads:
            ev = self.lastw.get(r)
            if ev is not None:
                self._wait(e, ev)
        for w in writes:
            ev = self.lastw.get(w)
            if ev is not None:
                self._wait(e, ev)
            for ev in self.readers.get(w, ()):
                self._wait(e, ev)

    def _commit(self, ev, reads, writes):
        for r in reads:
            self.readers.setdefault(r, []).append(ev)
        for w in writes:
            self.lastw[w] = ev
            self.readers[w] = []

    def fill(self, n=1):
        if self.filler is None or self.in_fill:
            return
        self.in_fill = True
        try:
            for _ in range(n):
                try:
                    next(self.filler)
                except StopIteration:
                    self.filler = None
                    break
        finally:
            self.in_fill = False

    def op(self, e, fn, reads=(), writes=()):
        if e == "pe":
            self.fill()
        self._deps(e, reads, writes)
        ins = fn(self.eng[e])
        if self.cnt[e] >= self.EPOCH:
            self._new_epoch(e)
        self.cnt[e] += 1
        s = self.sem[e]
        ins.then_inc(s, 1)
        ev = Ev(s, self.cnt[e], id(s))
        self._commit(ev, reads, writes)
        self.n_ins += 1
        return ev

    def group(self, e, fns, reads=(), writes=()):
        if e == "pe":
            self.fill()
        self._deps(e, reads, writes)
        ins = None
        for fn in fns:
            ins = fn(self.eng[e])
            self.n_ins += 1
        if self.cnt[e] >= self.EPOCH:
            self._new_epoch(e)
        self.cnt[e] += 1
        s = self.sem[e]
        ins.then_inc(s, 1)
        ev = Ev(s, self.cnt[e], id(s))
        self._commit(ev, reads, writes)
        return ev

    def dma(self, q, out, in_, reads=(), writes=(), semkey=None, **kw):
        if semkey is None:
            semkey = writes[0] if writes else reads[0]
        if semkey not in self.dsem:
            self.nsem += 1
            self.dsem[semkey] = self.nc.alloc_semaphore(name="d_%d" % self.nsem)
            self.dcnt[semkey] = 0
        self._deps(q, reads, writes)
        ins = self.eng[q].dma_start(out=out, in_=in_, **kw)
        s = self.dsem[semkey]
        self.dcnt[semkey] += 16
        ins.then_inc(s, 16)
        ev = Ev(s, self.dcnt[semkey], id(s), semkey)
        self._commit(ev, reads, writes)
        self.n_ins += 1
        return ev

    def coll(self, fn, reads=(), writes=()):
        if not hasattr(self, "csem"):
            self.nsem += 1
            self.csem = self.nc.alloc_semaphore(name="c_%d" % self.nsem)
            self.ccnt = 0
        self._deps("pool", reads, writes)
        ins = fn(self.eng["pool"])
        self.ccnt += 1
        ins.then_inc(self.csem, 1)
        ev = Ev(self.csem, self.ccnt, id(self.csem))
        self._commit(ev, reads, writes)
        self.n_ins += 1
        return ev

    def barrier(self):
        best = {}
        for ev in list(self.lastw.values()) + [ev for lst in self.readers.values() for ev in lst]:
            if ev.key not in best or best[ev.key].val < ev.val:
                best[ev.key] = ev
        for e in self.eng:
            for ev in best.values():
                self._wait(e, ev)

    def finish(self):
        evs = set()
        for ev in self.lastw.values():
            evs.add((ev.key, ev.val, ev))
        for lst in self.readers.values():
            for ev in lst:
                evs.add((ev.key, ev.val, ev))
        best = {}
        for key, val, ev in evs:
            if key not in best or best[key].val < val:
                best[key] = ev
        for ev in best.values():
            self._wait("sp", ev)


def _tables(cfg, core, pos0=0):
    T, TP = cfg.T, cfg.TP
    f32 = np.float32
    tb = {}
    tb["ident_bf"] = np.eye(128, dtype=f32).astype(ml_dtypes.bfloat16)
    tb["ident_f"] = np.eye(128, dtype=f32)
    tb["ones_bf"] = np.ones((128, 128), f32).astype(ml_dtypes.bfloat16)
    tb["ones_f"] = np.ones((128, 128), f32)
    half = 128
    inv_freq = (10000.0 ** (-np.arange(half, dtype=f32) / f32(half))).astype(f32)
    rot = np.zeros((cfg.NT, 2, 128, T), f32)
    for ti in range(cfg.NT):
        pos = np.concatenate([np.arange(TP, dtype=f32) + f32(ti * TP + pos0),
                              np.tile(f32(cfg.past_len) + np.arange(8, dtype=f32), cfg.NS)]).astype(f32)
        ang = (pos[None, :] * inv_freq[:, None]).astype(f32)
        rot[ti, 0] = np.cos(ang)
        rot[ti, 1] = np.sin(ang)
    tb["rot"] = rot
    idx = np.arange(128)
    grp = idx // 8
    causal_p = (idx[:, None] <= idx[None, :])
    causal_s = causal_p & (grp[:, None] == grp[None, :])
    lg = np.log1p(-np.exp2(-5.0 - np.arange(cfg.RH, dtype=np.float64)))
    dec = np.zeros((cfg.RH, 2, 128, 128), f32)
    qdec = np.zeros((128, cfg.RH, 2), f32)
    kdec = np.zeros((128, cfg.RH, 2), f32)
    diff = (idx[None, :] - idx[:, None]).astype(np.float64)
    for h in range(cfg.RH):
        dec[h, 0] = np.where(causal_p, np.exp(lg[h] * diff), 0.0) / 16.0
        dec[h, 1] = np.where(causal_s, np.exp(lg[h] * diff), 0.0) / 16.0
        qdec[:, h, 0] = np.exp(lg[h] * (idx + 1.0))
        qdec[:, h, 1] = np.exp(lg[h] * ((idx % 8) + 1.0))
        kdec[:, h, 0] = np.exp(lg[h] * (127.0 - idx)) / 16.0
        kdec[:, h, 1] = np.exp(lg[h] * (7.0 - (idx % 8))) / 16.0
    tb["ret_dec"] = dec
    tb["ret_qdec"] = qdec
    tb["ret_kdec"] = kdec
    cfg.ret_cd = [(float(np.exp(lg[h] * 128.0)), float(np.exp(lg[h] * 8.0))) for h in range(cfg.RH)]
    mk = np.zeros((4, 128, 128), f32)
    mk[0] = np.where(causal_p, 0.0, -30000.0)
    mk[1] = np.where(causal_s, 0.0, -30000.0)
    mk[2] = causal_p.astype(f32)
    mk[3] = causal_s.astype(f32)
    tb["masks"] = mk
    qm = np.zeros((128, 16, 128), f32)
    for b in range(16):
        qm[:, b, b * 8:(b + 1) * 8] = 1.0
    tb["qmask"] = qm.astype(ml_dtypes.bfloat16)
    rm = np.zeros((128, 16), f32)
    rm[idx, grp] = 1.0
    tb["rowmask"] = rm
    rst = np.ones((2, 128, T), f32)
    rst[0, :, TP::8] = 0.0
    rst[1, :, 0:TP:128] = 0.0
    rst[1, :, TP::8] = 0.0
    tb["rst"] = rst
    return tb


class Builder:
    def __init__(self, cfg):
        self.cfg = cfg
        nc = bass.Bass("TRN2", target_bir_lowering=False)
        self.nc = nc
        self.P = Prog(nc)
        self.din = {}
        self.dout = {}

    def inp(self, name, shape, dt=F32):
        t = self.nc.dram_tensor(name, list(shape), dt, kind="ExternalInput").ap()
        self.din[name] = t
        return t

    def outp(self, name, shape, dt=F32):
        t = self.nc.dram_tensor(name, list(shape), dt, kind="ExternalOutput").ap()
        self.dout[name] = t
        return t

    def scratch(self, name, shape, dt=F32):
        return self.nc.dram_tensor(name, list(shape), dt, kind="Internal").ap()

    def sb(self, es, name, shape, dt):
        self._uid = getattr(self, "_uid", 0) + 1
        return es.enter_context(self.nc.sbuf_tensor("%s_%d" % (name, self._uid), list(shape), dt))

    def build(self):
        cfg, nc, P = self.cfg, self.nc, self.P
        D, KC, T, NT = cfg.D, cfg.KC, cfg.T, cfg.NT
        RH, MH, HH = cfg.RH, cfg.MH, cfg.HH
        self.xT = self.inp("xT", [NT, KC, 128, T])
        self.s_ret = self.inp("s_ret", [NT, 16, RH, 256, 256])
        self.s_mc = self.inp("s_mc", [NT, 16, MH, 256, 256])
        self.s_mn = self.inp("s_mn", [NT, 16, MH, 256])
        self.s_mm = self.inp("s_mm", [NT, MH, 16])
        self.s_hg = self.inp("s_hg", [NT, 16, HH, 128, 128])
        self.w_in_ab = self.inp("w_in_ab", [D, cfg.ABC])
        self.w_out_ab = self.inp("w_out_ab", [D, D])
        self.w_in_c = self.inp("w_in_c", [D, cfg.CC])
        self.w_out_c = self.inp("w_out_c", [D, D])
        self.w_gate = self.inp("w_gate", [2, D, cfg.DFF])
        self.w_up = self.inp("w_up", [2, D, cfg.DFF])
        self.w_down = self.inp("w_down", [2, cfg.DFF, D])
        self.nw = self.inp("nw", [128, 5, KC])
        self.mlw = self.inp("mlw", [128, MH * 2])
        self.hgw = self.inp("hgw", [128, 1])
        self.bif = self.inp("bif", [128, 2 * MH])
        self.lbt = self.inp("lbt", [128, 2, HH])
        self.cmask = self.inp("cmask", [128, 1])
        self.snd = self.scratch("snd", [128, 520])
        self.rcv = self.scratch("rcv", [256, 520])
        self.snd_c = self.scratch("snd_c", [128, 128])
        self.rcv_c = self.scratch("rcv_c", [256, 128])
        tb = _tables(cfg, 0)
        self.tbl = {}
        for k, v in tb.items():
            dt = BF16 if v.dtype == ml_dtypes.bfloat16 else F32
            self.tbl[k] = self.inp("tb_" + k, v.shape, dt)
        self.yT = self.outp("yT", [NT, KC, 128, T])
        self.o_ret_p = self.outp("o_ret_p", [RH, 256, 256])
        self.o_mc_p = self.outp("o_mc_p", [MH, 256, 256])
        self.o_mn_p = self.outp("o_mn_p", [MH, 256])
        self.o_mm_p = self.outp("o_mm_p", [1, MH])
        self.o_hg_p = self.outp("o_hg_p", [HH, 128, 128])
        self.o_ret_s = self.outp("o_ret_s", [NT, 16, RH, 256, 256])
        self.o_mc_s = self.outp("o_mc_s", [NT, 16, MH, 256, 256])
        self.o_mn_s = self.outp("o_mn_s", [NT, 16, MH, 256])
        self.o_mm_s = self.outp("o_mm_s", [NT, MH, 16])
        self.o_hg_s = self.outp("o_hg_s", [NT, 16, HH, 128, 128])
        self.rT = self.scratch("rT", [KC, 128, T])
        self.mT = self.scratch("mT", [KC, 128, T], BF16)

        with ExitStack() as es:
            self.es = es
            self.pb = [es.enter_context(nc.psum_tensor("pb%d" % i, [128, 512], F32)) for i in range(7)]
            self.pt = [es.enter_context(nc.psum_tensor("pt%d" % i, [128, 1024], BF16)) for i in range(1)]
            c = {}
            for k in ("ident_bf", "ones_bf"):
                c[k] = self.sb(es, "c_" + k, [128, 128], BF16)
            for k in ("ident_f", "ones_f"):
                c[k] = self.sb(es, "c_" + k, [128, 128], F32)
            c["masks"] = self.sb(es, "c_masks", [128, 4, 128], F32)
            c["qmask"] = self.sb(es, "c_qmask", [128, 16, 128], BF16)
            c["rowmask"] = self.sb(es, "c_rowmask", [128, 16], F32)
            c["ret_qdec"] = self.sb(es, "c_qdec", [128, RH, 2], F32)
            c["ret_kdec"] = self.sb(es, "c_kdec", [128, RH, 2], F32)
            c["nw"] = self.sb(es, "c_nw", [128, 5, KC], F32)
            c["mlw"] = self.sb(es, "c_mlw", [128, MH * 2], F32)
            c["hgw"] = self.sb(es, "c_hgw", [128, 1], F32)
            c["bif"] = self.sb(es, "c_bif", [128, 2 * MH], F32)
            c["lbt"] = self.sb(es, "c_lbt", [128, 2, HH], F32)
            c["lb"] = self.sb(es, "c_lb", [128, HH], F32)
            c["oml"] = self.sb(es, "c_oml", [128, HH], F32)
            c["eps"] = self.sb(es, "c_eps", [128, 1], F32)
            c["one"] = self.sb(es, "c_one", [128, 1], F32)
            c["cmask"] = self.sb(es, "c_cmask", [128, 1], F32)
            self.c = c
            init_evs = []
            ld = lambda k, src: init_evs.append(P.dma("sp", c[k][:], src, reads=(), writes=(("c", k),), semkey="init"))
            ld("ident_bf", self.tbl["ident_bf"])
            ld("ones_bf", self.tbl["ones_bf"])
            ld("ident_f", self.tbl["ident_f"])
            ld("ones_f", self.tbl["ones_f"])
            ld("masks", self.tbl["masks"].rearrange("m p t -> p m t"))
            ld("qmask", self.tbl["qmask"])
            ld("rowmask", self.tbl["rowmask"])
            ld("ret_qdec", self.tbl["ret_qdec"])
            ld("ret_kdec", self.tbl["ret_kdec"])
            ld("nw", self.nw)
            ld("mlw", self.mlw)
            ld("hgw", self.hgw)
            ld("bif", self.bif)
            ld("lbt", self.lbt)
            ld("cmask", self.cmask)
            P.op("dve", lambda e: e.memset(c["eps"][:], EPS), writes=(("c", "eps"),))
            P.op("dve", lambda e: e.memset(c["one"][:], 1.0), writes=(("c", "one"),))
            self.lower_bounds()
            self.xn = self.sb(es, "xn", [128, KC, T], BF16)
            self.NW = 2
            self.wb = [self.sb(es, "wb%d" % i, [128, KC, 256], BF16) for i in range(self.NW)]
            self.wi = 0
            self.rci = 0
            for ti in range(NT):
                self.tile(ti)
            P.finish()
        return nc

    def lower_bounds(self):
        P, c = self.P, self.c
        HH = self.cfg.HH
        P.op("dve", lambda e: e.tensor_tensor(out=c["lb"][:], in0=c["lbt"][:, 1, :], in1=c["lbt"][:, 0, :],
                                              op=ALU.subtract), reads=(("c", "lbt"),), writes=(("c", "lb"),))
        P.op("act", lambda e: e.activation(out=c["lb"][:], in_=c["lb"][:], func=AF.Sigmoid),
             reads=(("c", "lb"),), writes=(("c", "lb"),))
        P.op("dve", lambda e: e.tensor_scalar(out=c["oml"][:], in0=c["lb"][:], scalar1=-1.0, scalar2=1.0,
                                              op0=ALU.mult, op1=ALU.add),
             reads=(("c", "lb"),), writes=(("c", "oml"),))

    def sreg(self, j, n):
        return self.pb[6][:, 0:n] if j == 0 else self.pb[3][:, 132:132 + n]

    def skey(self, j):
        return ("pb", 6) if j == 0 else ("pb", 3)

    def wslot(self):
        i = self.wi
        self.wi = (self.wi + 1) % self.NW
        return i

    def load_w(self, W, r0, nk, c0, ncols):
        i = self.wslot()
        src = W[r0:r0 + nk * 128, c0:c0 + ncols].rearrange("(k p) n -> p k n", p=128)
        self.P.dma("pool", self.wb[i][:, 0:nk, 0:ncols], src, reads=(("coll",),), writes=(("wb", i),))
        return i

    def dense_fm_gen(self, wi, nk, cb, act, act_res, banks, consumer, slice_k=4):
        P = self.P
        wb = self.wb[wi]
        k0 = 0
        while k0 < nk:
            k1 = min(nk, k0 + slice_k)
            fns = []
            for k in range(k0, k1):
                for j in range(3):
                    def fn(e, k=k, j=j):
                        return e.matmul(self.pb[banks[j]][:, 0:384], lhsT=wb[:, k, cb * 128:(cb + 1) * 128],
                                        rhs=act(k)[:, j * 384:(j + 1) * 384], start=(k == 0), stop=(k == nk - 1))
                    fns.append(fn)
            P.group("pe", fns, reads=(("wb", wi),) + tuple(act_res), writes=tuple(("pb", b) for b in banks))
            k0 = k1
            if k0 < nk:
                yield
        consumer([self.pb[b][:, 0:384] for b in banks], [("pb", b) for b in banks])
        yield

    def dense_fm(self, wi, nk, cb, act, act_res, banks, consumer):
        save = self.P.in_fill
        self.P.in_fill = True
        try:
            for _ in self.dense_fm_gen(wi, nk, cb, act, act_res, banks, consumer, slice_k=nk):
                pass
        finally:
            self.P.in_fill = save

    def norm(self, src_dram, widx, out_fp32_dram=None):
        cfg, P, c = self.cfg, self.P, self.c
        KC, T = cfg.KC, cfg.T
        banks = (3, 4, 5)
        sqb = self.sq
        for k in range(KC):
            r = self.rci
            self.rci ^= 1
            P.dma("sp", self.rc[r][:], src_dram[k], reads=(("dram", "r", k),), writes=(("rc", r),))
            P.op("act", lambda e, r=r, k=k: e.activation(out=sqb[k % 2][:], in_=self.rc[r][:], func=AF.Square),
                 reads=(("rc", r),), writes=(("sq", k % 2),))
            fns = []
            for j in range(3):
                fns.append(lambda e, j=j, k=k: e.matmul(self.pb[banks[j]][:, 0:384], lhsT=c["ones_bf"][:],
                                                        rhs=sqb[k % 2][:, j * 384:(j + 1) * 384],
                                                        start=(k == 0), stop=(k == KC - 1)))
            P.group("pe", fns, reads=(("sq", k % 2), ("c", "ones_bf")), writes=tuple(("pb", b) for b in banks))
        for j in range(3):
            sl = slice(j * 384, (j + 1) * 384)
            P.op("act", lambda e, j=j, sl=sl: e.activation(out=self.rstd[:, sl], in_=self.pb[banks[j]][:, 0:384],
                                                          func=AF.Sqrt, scale=1.0 / cfg.D, bias=c["eps"][:]),
                 reads=(("pb", banks[j]), ("c", "eps")), writes=(("rstd",),))
        P.op("dve", lambda e: e.reciprocal(out=self.rstd[:], in_=self.rstd[:]), reads=(("rstd",),),
             writes=(("rstd",),))
        for k in range(KC):
            r = self.rci
            self.rci ^= 1
            P.dma("sp", self.rc[r][:], src_dram[k], reads=(("dram", "r", k),), writes=(("rc", r),))
            if out_fp32_dram is None:
                P.op("dve", lambda e, r=r, k=k: e.scalar_tensor_tensor(
                    out=self.xn[:, k, :], in0=self.rc[r][:], scalar=c["nw"][:, widx, k:k + 1], in1=self.rstd[:],
                    op0=ALU.mult, op1=ALU.mult), reads=(("rc", r), ("rstd",), ("c", "nw")), writes=(("xn", k),))
            else:
                P.op("dve", lambda e, r=r, k=k: e.scalar_tensor_tensor(
                    out=self.rc[r][:], in0=self.rc[r][:], scalar=c["nw"][:, widx, k:k + 1], in1=self.rstd[:],
                    op0=ALU.mult, op1=ALU.mult), reads=(("rc", r), ("rstd",), ("c", "nw")), writes=(("rc", r),))
                P.dma("sp", out_fp32_dram[k], self.rc[r][:], reads=(("rc", r),), writes=(("dram", "y", k),),
                      semkey=("rc", r))

    def add_residual(self, k, src_dram, ps, ps_res):
        P = self.P
        r = self.rci
        self.rci ^= 1
        P.dma("sp", self.rc[r][:], src_dram[k], reads=(("dram", "r", k),), writes=(("rc", r),))
        for j in range(3):
            sl = slice(j * 384, (j + 1) * 384)
            P.op("dve", lambda e, j=j, sl=sl, r=r: e.tensor_tensor(out=self.rc[r][:, sl], in0=ps[j],
                                                                   in1=self.rc[r][:, sl], op=ALU.add),
                 reads=(ps_res[j], ("rc", r)), writes=(("rc", r),))
        P.dma("sp", self.rT[k], self.rc[r][:], reads=(("rc", r),), writes=(("dram", "r", k),), semkey=("rc", r))

    def out_proj(self, W, src_dram):
        cfg, P = self.cfg, self.P
        KC = cfg.KC
        for k in range(KC):
            P.dma("sp", self.xn[:, k, :], self.mT[k], reads=(("dram", "m", k),), writes=(("xn", k),),
                  semkey=("xnload", k % 4))
        xn_res = tuple(("xn", k) for k in range(KC))
        for cb2 in range(KC // 2):
            wi = self.load_w(W, 0, KC, cb2 * 256, 256)
            for s in range(2):
                kk = cb2 * 2 + s
                self.dense_fm(wi, KC, s, lambda k: self.xn[:, k, :], xn_res, (0, 1, 2),
                              lambda ps, pr, kk=kk: self.add_residual(kk, src_dram, ps, pr))

    def ffn(self, layer):
        cfg, P = self.cfg, self.P
        KC, T = cfg.KC, cfg.T
        xn_res = tuple(("xn", k) for k in range(KC))
        Wg, Wu, Wd = self.w_gate[layer], self.w_up[layer], self.w_down[layer]
        with ExitStack() as es:
            nfmax = max(n for _, n in cfg.fq)
            hT = self.sb(es, "hT", [128, nfmax, T], BF16)
            sil = [self.sb(es, "sil%d" % i, [128, 384], F32) for i in range(2)]
            cnt = [0]
            for (f0, nf) in cfg.fq:
                fb = 0
                while fb < nf:
                    nb = min(2, nf - fb)
                    wg = self.load_w(Wg, 0, KC, (f0 + fb) * 128, nb * 128)
                    wu = self.load_w(Wu, 0, KC, (f0 + fb) * 128, nb * 128)
                    for s in range(nb):
                        fl = fb + s
                        store = {}

                        def cons_g(ps, pr, store=store):
                            store["g"] = (ps, pr)

                        self.dense_fm(wg, KC, s, lambda k: self.xn[:, k, :], xn_res, (0, 1, 2), cons_g)

                        def cons_u(ps, pr, store=store, fl=fl):
                            gps, gpr = store["g"]
                            for j in range(3):
                                sl = slice(j * 384, (j + 1) * 384)
                                si = cnt[0] % 2
                                cnt[0] += 1
                                P.op("act", lambda e, j=j, si=si: e.activation(out=sil[si][:], in_=gps[j], func=AF.Silu),
                                     reads=(gpr[j],), writes=(("sil", si),))
                                P.op("dve", lambda e, j=j, si=si, sl=sl: e.tensor_tensor(
                                    out=hT[:, fl, sl], in0=ps[j], in1=sil[si][:], op=ALU.mult),
                                    reads=(pr[j], ("sil", si)), writes=(("hT", fl),))

                        self.dense_fm(wu, KC, s, lambda k: self.xn[:, k, :], xn_res, (3, 4, 5), cons_u)
                    fb += nb
                h_res = tuple(("hT", f) for f in range(nf))
                for cb2 in range(KC // 2):
                    wd = self.load_w(Wd, f0 * 128, nf, cb2 * 256, 256)
                    for s in range(2):
                        kk = cb2 * 2 + s
                        self.dense_fm(wd, nf, s, lambda k: hT[:, k, :], h_res, (0, 1, 2),
                                      lambda ps, pr, kk=kk: self.add_residual(kk, self.rT, ps, pr))
            P.barrier()

    def nbufs(self):
        es = ExitStack()
        T = self.cfg.T
        self.rstd = self.sb(es, "rstd", [128, T], F32)
        self.rc = [self.sb(es, "rc%d" % i, [128, T], F32) for i in range(2)]
        self.sq = [self.sb(es, "sq%d" % i, [128, T], BF16) for i in range(2)]
        return es

    def tile(self, ti):
        cfg, P = self.cfg, self.P
        self.ti = ti
        with self.nbufs():
            self.norm(self.xT[ti], 0)
            P.barrier()
        from_x = self.xT[ti]
        self.mixer_ab()
        with self.nbufs():
            self.out_proj(self.w_out_ab, from_x)
            self.norm(self.rT, 1)
            self.ffn(0)
            self.norm(self.rT, 2)
            P.barrier()
        self.mixer_c()
        with self.nbufs():
            self.out_proj(self.w_out_c, self.rT)
            self.norm(self.rT, 3)
            self.ffn(1)
            self.norm(self.rT, 4, out_fp32_dram=self.yT[ti])
            P.barrier()

    def mixer_ab(self):
        mixer_ab(self)

    def mixer_c(self):
        mixer_c(self)


def _zero_mT(b):
    P, cfg = b.P, b.cfg
    with ExitStack() as es:
        z = b.sb(es, "zst", [128, cfg.T], BF16)
        P.op("dve", lambda e: e.memset(z[:], 0.0), writes=(("zst",),))
        for k in range(cfg.KC):
            P.dma("sp", b.mT[k], z[:], reads=(("zst",),), writes=(("dram", "m", k),), semkey=("zst",))
        P.barrier()


STUB_AB = False
STUB_C = False


def mixer_ab(b):
    if STUB_AB:
        return _zero_mT(b)
    return mixer_ab_real(b)


def mixer_c(b):
    if STUB_C:
        return _zero_mT(b)
    return mixer_c_real(b)


def core_maps(cfg):
    pm, sm = [], []
    if cfg.xch:
        for c in range(cfg.ncores):
            pm.append((c // 2, c % 2))
            sm.append([c * 16])
        return pm, sm
    if cfg.ncores == 8:
        for c in range(8):
            pm.append(c if c < 4 else None)
            sm.append([None] * cfg.NT if c < 4 else [((c - 4) * cfg.NT + ti) * 16 for ti in range(cfg.NT)])
    else:
        pm.append(0)
        sm.append([ti * 16 for ti in range(cfg.NT)])
    return pm, sm


def make_in_maps(cfg, inp):
    T, TP, KC, NT = cfg.T, cfg.TP, cfg.KC, cfg.NT
    f32 = np.float32
    pm, sm = core_maps(cfg)
    tb = _tables(cfg, 0)
    shared = {}
    shared["w_in_ab"] = np.ascontiguousarray(inp["w_in_ab"][0])
    shared["w_out_ab"] = np.ascontiguousarray(inp["w_out_ab"][0])
    shared["w_in_c"] = np.ascontiguousarray(inp["w_in_c"][0])
    shared["w_out_c"] = np.ascontiguousarray(inp["w_out_c"][0])
    shared["w_gate"] = np.ascontiguousarray(inp["w_gate"])
    shared["w_up"] = np.ascontiguousarray(inp["w_up"])
    shared["w_down"] = np.ascontiguousarray(inp["w_down"])
    nws = np.stack([inp["norm_mix_w"][0], inp["norm_ffn_w"][0], inp["norm_mix_w"][1], inp["norm_ffn_w"][1],
                    inp["norm_final_w"]], 0)
    shared["nw"] = np.ascontiguousarray(nws.reshape(5, KC, 128).transpose(2, 0, 1))
    shared["mlw"] = np.ascontiguousarray(inp["ml_norm_w"][0].reshape(cfg.MH * 2, 128).T)
    shared["hgw"] = np.ascontiguousarray(inp["hg_norm_w"][0].reshape(128, 1))
    shared["bif"] = np.ascontiguousarray(np.broadcast_to(inp["b_if_ab"][0][None, :], (128, 2 * cfg.MH)))
    shared["lbt"] = np.ascontiguousarray(inp["lb_logits"].reshape(2, cfg.HH, 128).transpose(2, 0, 1))
    for k, v in tb.items():
        shared["tb_" + k] = v
    maps = []
    for c in range(cfg.ncores):
        m = dict(shared)
        if cfg.xch:
            m["tb_rot"] = _tables(cfg, c, pos0=(c % 2) * TP)["rot"]
            m["cmask"] = np.full((128, 1), float(c % 2), f32)
        else:
            m["cmask"] = np.zeros((128, 1), f32)
        x = np.zeros((NT, T, cfg.D), f32)
        s_ret = np.zeros((NT, 16, cfg.RH, 256, 256), f32)
        s_mc = np.zeros((NT, 16, cfg.MH, 256, 256), f32)
        s_mn = np.zeros((NT, 16, cfg.MH, 256), f32)
        s_mm = np.zeros((NT, cfg.MH, 16), f32)
        s_hg = np.zeros((NT, 16, cfg.HH, 128, 128), f32)
        for ti in range(NT):
            if cfg.xch:
                sq_, hf_ = pm[c]
                x[ti, :TP] = inp["x_prompt"][sq_, hf_ * TP:(hf_ + 1) * TP]
            elif pm[c] is not None:
                x[ti, :TP] = inp["x_prompt"][pm[c], ti * TP:(ti + 1) * TP]
            if sm[c][ti] is not None:
                b0 = sm[c][ti]
                x[ti, TP:] = inp["x_sample"][b0:b0 + 16].reshape(128, cfg.D)
                s_ret[ti] = inp["state_ret"][0, b0:b0 + 16]
                s_mc[ti] = inp["state_mlstm_C"][0, b0:b0 + 16]
                s_mn[ti] = inp["state_mlstm_n"][0, b0:b0 + 16]
                s_mm[ti] = inp["state_mlstm_m"][0, b0:b0 + 16].T
                s_hg[ti] = inp["state_hgrn"][0, b0:b0 + 16]
        m["xT"] = np.ascontiguousarray(x.transpose(0, 2, 1).reshape(NT, KC, 128, T))
        m["s_ret"], m["s_mc"], m["s_mn"], m["s_mm"], m["s_hg"] = s_ret, s_mc, s_mn, s_mm, s_hg
        maps.append(m)
    return maps


def assemble(cfg, res):
    T, TP, KC, NT, D = cfg.T, cfg.TP, cfg.KC, cfg.NT, cfg.D
    f32 = np.float32
    pm, sm = core_maps(cfg)
    B, DB = cfg.batch, cfg.dec_batch
    y_p = np.zeros((B, cfg.seq, D), f32)
    y_s = np.zeros((DB, 8, D), f32)
    ret_p = np.zeros((1, B, cfg.RH, 256, 256), f32)
    mc_p = np.zeros((1, B, cfg.MH, 256, 256), f32)
    mn_p = np.zeros((1, B, cfg.MH, 256), f32)
    mm_p = np.zeros((1, B, cfg.MH), f32)
    hg_p = np.zeros((1, B, cfg.HH, 128, 128), f32)
    ret_s = np.zeros((1, DB, cfg.RH, 256, 256), f32)
    mc_s = np.zeros((1, DB, cfg.MH, 256, 256), f32)
    mn_s = np.zeros((1, DB, cfg.MH, 256), f32)
    mm_s = np.zeros((1, DB, cfg.MH), f32)
    hg_s = np.zeros((1, DB, cfg.HH, 128, 128), f32)
    for c in range(cfg.ncores):
        r = res[c]
        y = np.asarray(r["yT"]).reshape(NT, D, T).transpose(0, 2, 1)
        for ti in range(NT):
            if cfg.xch:
                sq_, hf_ = pm[c]
                y_p[sq_, hf_ * TP:(hf_ + 1) * TP] = y[ti, :TP]
            elif pm[c] is not None:
                y_p[pm[c], ti * TP:(ti + 1) * TP] = y[ti, :TP]
            if sm[c][ti] is not None:
                b0 = sm[c][ti]
                y_s[b0:b0 + 16] = y[ti, TP:].reshape(16, 8, D)
                ret_s[0, b0:b0 + 16] = r["o_ret_s"][ti]
                mc_s[0, b0:b0 + 16] = r["o_mc_s"][ti]
                mn_s[0, b0:b0 + 16] = r["o_mn_s"][ti]
                mm_s[0, b0:b0 + 16] = np.asarray(r["o_mm_s"][ti]).T
                hg_s[0, b0:b0 + 16] = r["o_hg_s"][ti]
        psel = None
        if cfg.xch:
            if pm[c][1] == 1:
                psel = pm[c][0]
        elif pm[c] is not None:
            psel = pm[c]
        if psel is not None:
            ret_p[0, psel] = r["o_ret_p"]
            mc_p[0, psel] = r["o_mc_p"]
            mn_p[0, psel] = r["o_mn_p"]
            mm_p[0, psel] = np.asarray(r["o_mm_p"]).reshape(cfg.MH)
            hg_p[0, psel] = r["o_hg_p"]
    return (y_p, y_s, ret_p, mc_p, mn_p, mm_p, hg_p, ret_s, mc_s, mn_s, mm_s, hg_s)


def run_cfg(cfg, inp):
    b = Builder(cfg)
    nc = b.build()
    maps = make_in_maps(cfg, inp)
    res = run_bass_kernel_spmd(nc, maps, core_ids=list(range(cfg.ncores)))
    return assemble(cfg, res.results)


def kernel(**inputs):
    cfg = Cfg()
    inp = {k: np.asarray(v) for k, v in inputs.items()}
    return run_cfg(cfg, inp)


class _AB:
    pass


def _xchg(b, src_ap, snd, rcv, res):
    P, c, cfg = b.P, b.c, b.cfg
    key = ("dram", id(snd))
    P.dma("sp", snd, src_ap, reads=(res,), writes=(key + ("s",),), semkey=("xs",))
    P.coll(lambda g: g.collective_compute("AllGather", ALU.bypass, replica_groups=cfg.groups, ins=[snd], outs=[rcv]),
           reads=(key + ("s",),) + tuple(("wb", i) for i in range(len(b.wb))), writes=(key + ("r",), ("coll",)))
    P.dma("sp", src_ap, rcv[0:128, :], reads=(key + ("r",),), writes=(res,), semkey=("xr",))
    P.op("dve", lambda e: e.tensor_scalar(out=src_ap, in0=src_ap, scalar1=c["cmask"][:, 0:1], scalar2=None, op0=ALU.mult),
         reads=(res, ("c", "cmask")), writes=(res,))


def _proj_fm_gen(b, W, col0, evac, bankset):
    cfg = b.cfg
    KC = cfg.KC
    xn_res = tuple(("xn", k) for k in range(KC))
    wi = b.load_w(W, 0, KC, col0, 256)
    for s in range(2):
        banks = bankset[s % len(bankset)]

        def cons(ps, pr, s=s):
            for j in range(3):
                evac(s, j, ps[j], pr[j])

        yield from b.dense_fm_gen(wi, KC, s, lambda k: b.xn[:, k, :], xn_res, banks, cons)


def _proj_tm_gen(b, W, col0, ncols, dst, dst_res, dcol0=0):
    cfg, P = b.cfg, b.P
    KC = cfg.KC
    xn_res = tuple(("xn", k) for k in range(KC))
    wi = b.load_w(W, 0, KC, col0, ncols)
    for tb in range(9):
        bank = tb % 2
        fns = []
        for k in range(KC):
            fns.append(lambda e, k=k, tb=tb, bank=bank: e.matmul(
                b.pb[bank][:, 0:ncols], lhsT=b.xn[:, k, tb * 128:(tb + 1) * 128], rhs=b.wb[wi][:, k, 0:ncols],
                start=(k == 0), stop=(k == KC - 1)))
        P.group("pe", fns, reads=(("wb", wi),) + xn_res, writes=(("pb", bank),))
        P.op("act", lambda e, tb=tb, bank=bank: e.copy(out=dst[:, tb, dcol0:dcol0 + ncols], in_=b.pb[bank][:, 0:ncols]),
             reads=(("pb", bank),), writes=(dst_res,))
        yield


def _evac_to(b, dst, dres, func=None):
    P = b.P

    def ev(s, j, ps, pr):
        if func is None:
            P.op("act", lambda e: e.copy(out=dst[:, s, j * 384:(j + 1) * 384], in_=ps), reads=(pr,), writes=(dres,))
        else:
            P.op("act", lambda e: e.activation(out=dst[:, s, j * 384:(j + 1) * 384], in_=ps, func=func),
                 reads=(pr,), writes=(dres,))
    return ev


def _pipeline(b, n, proj_gen, rec):
    P = b.P
    _drain(b, proj_gen(0))
    for i in range(n):
        nxt = proj_gen(i + 1) if i + 1 < n else None
        P.filler = nxt
        rec(i)
        P.filler = None
        _drain(b, nxt)


def _drain(b, gen):
    if gen is None:
        return
    P = b.P
    save = P.in_fill
    P.in_fill = True
    try:
        for _ in gen:
            pass
    finally:
        P.in_fill = save


class _View:
    DB = ("qT", "kT", "gT", "v", "Q32", "F32")

    def __init__(self, base, par):
        self.__dict__["_base"] = base
        self.__dict__["_par"] = par

    def __getattr__(self, n):
        base, par = self.__dict__["_base"], self.__dict__["_par"]
        if n in _View.DB:
            return getattr(base, n + "_db")[par]
        if n.startswith("k_") and n[2:] in _View.DB:
            return (n[2:], par)
        return getattr(base, n)

    def __setattr__(self, n, v):
        setattr(self.__dict__["_base"], n, v)


def _transpose_pair(b, src_fn, src_res, pt_i, n=2):
    P, c = b.P, b.c
    fns = []
    for j in range(n):
        fns.append(lambda e, j=j: e.transpose(out=b.pt[0][:, pt_i * 256 + j * 128:pt_i * 256 + (j + 1) * 128], in_=src_fn(j),
                                              identity=c["ident_bf"][:]))
    P.group("pe", fns, reads=tuple(src_res) + (("c", "ident_bf"),), writes=(("pt", 0),))


def mixer_ab_real(b):
    cfg, P, c = b.cfg, b.P, b.c
    P.barrier()
    KC, T, TP = cfg.KC, cfg.T, cfg.TP
    RW, MW, RH, MH = cfg.RW, cfg.MW, cfg.RH, cfg.MH
    W = b.w_in_ab
    ti = b.ti
    with ExitStack() as es:
        A = _AB()
        A.raw = b.sb(es, "raw", [128, 2, T], F32)
        A.qT_db = [b.sb(es, "qT", [128, 2, T], BF16) for _ in range(2)]
        A.kT_db = [b.sb(es, "kT", [128, 2, T], BF16) for _ in range(2)]
        A.gT_db = [b.sb(es, "gT", [128, 2, T], BF16) for _ in range(2)]
        A.v_db = [b.sb(es, "v", [128, 9, 260], BF16) for _ in range(2)]
        A.rot = b.sb(es, "rot", [128, 2, T], F32)
        A.t1 = b.sb(es, "t1", [128, T], F32)
        A.t2 = b.sb(es, "t2", [128, T], F32)
        A.mst = b.sb(es, "mst", [128, 2, T], BF16)
        A.S = b.sb(es, "S", [128, 2, 260], F32)
        A.Sbf = b.sb(es, "Sbf", [128, 2, 260], BF16)
        A.Sst = b.sb(es, "Sst", [128, 4, 2, 260], F32)
        A.Sstb = b.sb(es, "Sstb", [128, 4, 2, 260], BF16)
        A.qmx = b.sb(es, "qmx", [128, 2, 4, 128], BF16)
        A.kdm = b.sb(es, "kdm", [128, 4, 256], BF16)
        A.dec = b.sb(es, "dec", [128, 2, 128], F32)
        A.attm = b.sb(es, "attm", [128, 128], BF16)
        A.osb = b.sb(es, "osb", [128, 260], F32)
        A.otot_db = [b.sb(es, "otot", [128, 260], F32) for _ in range(2)]
        A.junk = A.osb
        A.on = b.sb(es, "on", [128, 256], BF16)
        A.kd = b.sb(es, "kd", [128, 256], BF16)
        A.sm = b.sb(es, "sm", [128, 16], F32)
        A.cols = b.sb(es, "cols", [128, 4], F32)
        A.D = b.sb(es, "Dm", [128, 128], F32)
        A.nrow = b.sb(es, "nrow", [16, 256], F32)
        A.ncol = b.sb(es, "ncol", [128, 2, 16], F32)
        A.mrow = b.sb(es, "mrow", [128, 16], F32)
        A.qi = b.sb(es, "qi", [128, 2, 128], BF16)
        A.wrep = b.wb[1][:, 0:KC, 128:256]
        A.mend = b.sb(es, "mend", [128, 16], F32)
        for par in range(2):
            P.op("dve", lambda e, par=par: e.memset(A.v_db[par][:, :, 256:257], 1.0), writes=(("v1",),))
        P.op("dve", lambda e: e.memset(A.Sst[:], 0.0), writes=(("Sst",),))
        P.op("dve", lambda e: e.memset(A.S[:], 0.0), writes=(("S",),))
        P.dma("sp", A.rot[:], b.tbl["rot"][ti].rearrange("a p t -> p a t"), writes=(("rot",),))
        views = [_View(A, 0), _View(A, 1)]

        def proj_gen(i):
            if i < RH:
                return _ret_proj_gen(b, views[i % 2], i)
            return _ml_proj_gen(b, views[i % 2], i - RH)

        def rec(i):
            if i < RH:
                _ret_rec(b, views[i % 2], i)
            else:
                _ml_rec(b, views[i % 2], i - RH)

        _pipeline(b, RH + MH, proj_gen, rec)
        b.NW = 2
        b.wi = 0
        P.barrier()


def _store_mixed(b, A, hidx):
    P = b.P
    for j in range(2):
        P.dma("sp", b.mT[hidx * 2 + j], A.mst[:, j, :], reads=(("mst",),), writes=(("dram", "m", hidx * 2 + j),),
              semkey=("mst",))


def _ret_proj_gen(b, A, h):
    cfg, P, c = b.cfg, b.P, b.c
    T, TP, RW = cfg.T, cfg.TP, cfg.RW
    W = b.w_in_ab
    ti = b.ti
    sets = [(0, 1, 2)]

    def evac_raw(s, j, ps, pr):
        P.op("act", lambda e: e.copy(out=A.raw[:, s, j * 384:(j + 1) * 384], in_=ps), reads=(pr,), writes=(("raw",),))

    def rotary(dst, dres):
        x1, x2 = A.raw[:, 0, :], A.raw[:, 1, :]
        cs, sn = A.rot[:, 0, :], A.rot[:, 1, :]
        rr = (("raw",), ("rot",))
        P.op("dve", lambda e: e.tensor_tensor(out=A.t1[:], in0=x1, in1=cs, op=ALU.mult), reads=rr, writes=(("t1",),))
        P.op("dve", lambda e: e.tensor_tensor(out=A.t2[:], in0=x2, in1=sn, op=ALU.mult), reads=rr, writes=(("t2",),))
        P.op("dve", lambda e: e.tensor_tensor(out=dst[:, 0, :], in0=A.t1[:], in1=A.t2[:], op=ALU.subtract),
             reads=(("t1",), ("t2",)), writes=(dres,))
        P.op("dve", lambda e: e.tensor_tensor(out=A.t1[:], in0=x1, in1=sn, op=ALU.mult), reads=rr, writes=(("t1",),))
        P.op("dve", lambda e: e.tensor_tensor(out=A.t2[:], in0=x2, in1=cs, op=ALU.mult), reads=rr, writes=(("t2",),))
        P.op("dve", lambda e: e.tensor_tensor(out=dst[:, 1, :], in0=A.t1[:], in1=A.t2[:], op=ALU.add),
             reads=(("t1",), ("t2",)), writes=(dres,))

    yield from _proj_fm_gen(b, W, 0 * RW + h * 256, evac_raw, sets)
    rotary(A.qT, A.k_qT)
    yield from _proj_fm_gen(b, W, 1 * RW + h * 256, evac_raw, sets)
    rotary(A.kT, A.k_kT)
    yield from _proj_tm_gen(b, W, 2 * RW + h * 256, 256, A.v, A.k_v)

    def evac_g(s, j, ps, pr):
        P.op("act", lambda e: e.activation(out=A.gT[:, s, j * 384:(j + 1) * 384], in_=ps, func=AF.Silu),
             reads=(pr,), writes=(A.k_gT,))

    yield from _proj_fm_gen(b, W, 3 * RW + h * 256, evac_g, sets)


def _ret_rec(b, A, h):
    cfg, P, c = b.cfg, b.P, b.c
    T, TP, RW = cfg.T, cfg.TP, cfg.RW
    ti = b.ti
    P.dma("sp", A.dec[:], b.tbl["ret_dec"][h].rearrange("a p t -> p a t"), writes=(("dec",),))
    cdp, cds = cfg.ret_cd[h]
    if ti == 0:
        P.op("dve", lambda e: e.memset(A.S[:], 0.0), writes=(("S",),))
    else:
        P.dma("sp", A.S[:, :, 0:256], b.o_ret_p[h].rearrange("(j p) e -> p j e", p=128),
              reads=(("dram", "retp", h),), writes=(("S",),))
    if cfg.xch:
        for cidx in range(8):
            tok = slice(cidx * 128, (cidx + 1) * 128)
            _transpose_pair(b, lambda j: A.kT[:, j, tok], (A.k_kT,), 1)
            P.op("dve", lambda e: e.tensor_scalar(out=A.kd[:], in0=b.pt[0][:, 256:512], scalar1=c["ret_kdec"][:, h, 0:1],
                                                  scalar2=None, op0=ALU.mult),
                 reads=(("pt", 0), ("c", "ret_kdec")), writes=(("kd",),))
            for j in range(2):
                bank = j
                P.op("pe", lambda e, j=j, bank=bank: e.matmul(b.sreg(bank, 256), lhsT=A.kd[:, j * 128:(j + 1) * 128],
                                                              rhs=A.v[:, cidx, 0:256], start=True, stop=True),
                     reads=(("kd",), A.k_v), writes=(b.skey(bank),))
                P.op("dve", lambda e, j=j, bank=bank: e.scalar_tensor_tensor(
                    out=A.S[:, j, 0:256], in0=A.S[:, j, 0:256], scalar=cdp, in1=b.sreg(bank, 256),
                    op0=ALU.mult, op1=ALU.add), reads=(("S",), b.skey(bank)), writes=(("S",),))
        _xchg(b, A.S[:].rearrange("p j e -> p (j e)"), b.snd[:, :], b.rcv[:, :], ("S",))
    P.op("act", lambda e: e.copy(out=A.Sbf[:], in_=A.S[:]), reads=(("S",),), writes=(("Sbf",),))
    pending = None
    for cidx in range(9):
        smp = 1 if cidx == 8 else 0
        tok = slice(cidx * 128, (cidx + 1) * 128)
        otot, kot = A.otot_db[cidx % 2], ("otot", cidx % 2)
        P.group("pe", [lambda e, j=j: e.matmul(b.pb[3][:, 0:128], lhsT=A.kT[:, j, tok], rhs=A.qT[:, j, tok],
                                               start=(j == 0), stop=(j == 1)) for j in range(2)],
                reads=(A.k_kT, A.k_qT), writes=(("pb", 3),))
        P.op("dve", lambda e: e.tensor_tensor(out=A.attm[:], in0=b.pb[3][:, 0:128], in1=A.dec[:, smp, :], op=ALU.mult),
             reads=(("pb", 3), ("dec",)), writes=(("attm",),))
        P.op("pe", lambda e: e.matmul(b.pb[4][:, 0:256], lhsT=A.attm[:], rhs=A.v[:, cidx, 0:256], start=True, stop=True),
             reads=(("attm",), A.k_v), writes=(("pb", 4),))
        P.op("act", lambda e: e.copy(out=A.osb[:, 0:256], in_=b.pb[4][:, 0:256]), reads=(("pb", 4),), writes=(("osb",),))
        _transpose_pair(b, lambda j: A.kT[:, j, tok], (A.k_kT,), 1)
        P.op("dve", lambda e: e.tensor_scalar(out=A.kd[:], in0=b.pt[0][:, 256:512], scalar1=c["ret_kdec"][:, h, smp:smp + 1],
                                              scalar2=None, op0=ALU.mult),
             reads=(("pt", 0), ("c", "ret_kdec")), writes=(("kd",),))
        if not smp:
            P.group("pe", [lambda e, j=j: e.matmul(b.pb[5][:, 0:256], lhsT=A.qT[:, j, tok], rhs=A.Sbf[:, j, 0:256],
                                                   start=(j == 0), stop=(j == 1)) for j in range(2)],
                    reads=(A.k_qT, ("Sbf",)), writes=(("pb", 5),))
            for j in range(2):
                bank = j
                P.op("pe", lambda e, j=j, bank=bank: e.matmul(b.sreg(bank, 256), lhsT=A.kd[:, j * 128:(j + 1) * 128],
                                                              rhs=A.v[:, cidx, 0:256], start=True, stop=True),
                     reads=(("kd",), A.k_v), writes=(b.skey(bank),))
                P.op("dve", lambda e, j=j, bank=bank: e.scalar_tensor_tensor(
                    out=A.S[:, j, 0:256], in0=A.S[:, j, 0:256], scalar=cdp, in1=b.sreg(bank, 256),
                    op0=ALU.mult, op1=ALU.add), reads=(("S",), b.skey(bank)), writes=(("S",),))
            P.op("act", lambda e: e.copy(out=A.Sbf[:], in_=A.S[:]), reads=(("S",),), writes=(("Sbf",),))
        else:
            _sample_states(b, A, tok, b.s_ret[ti, :, h], b.o_ret_s[ti, :, h], 256, cds, None)
        P.op("dve", lambda e: e.scalar_tensor_tensor(out=otot[:, 0:256], in0=b.pb[5][:, 0:256],
                                                     scalar=c["ret_qdec"][:, h, smp:smp + 1], in1=A.osb[:, 0:256],
                                                     op0=ALU.mult, op1=ALU.add),
             reads=(("pb", 5), ("osb",), ("c", "ret_qdec")), writes=(kot,))
        if pending is not None:
            pending()
        pending = (lambda tok=tok, otot=otot, kot=kot: _norm_gate_store(b, A, tok, None, None, otot, kot))
    pending()
    if True:
        for j in range(2):
            P.dma("sp", b.o_ret_p[h, j * 128:(j + 1) * 128, :], A.S[:, j, 0:256], reads=(("S",),),
                  writes=(("dram", "retp", h),), semkey=("Sout",))
    _store_mixed(b, A, h)


def _norm_gate_store(b, A, tok, rec_col, wcol_fn, otot, kot):
    P, c = b.P, b.c
    P.op("act", lambda e: e.activation(out=A.junk[:, 0:256], in_=otot[:, 0:256], func=AF.Square,
                                       accum_out=A.sm[:, 0:1]),
         reads=(kot,), writes=(("osb",), ("sm", 0)))
    if rec_col is None:
        P.op("act", lambda e: e.activation(out=A.sm[:, 1:2], in_=A.sm[:, 0:1], func=AF.Sqrt, scale=1.0 / 256.0,
                                           bias=c["eps"][:]), reads=(("sm", 0), ("c", "eps")), writes=(("sm", 1),))
        P.op("dve", lambda e: e.reciprocal(out=A.sm[:, 2:3], in_=A.sm[:, 1:2]), reads=(("sm", 1),), writes=(("sm", 2),))
    else:
        P.op("dve", lambda e: e.tensor_tensor(out=A.sm[:, 3:4], in0=rec_col, in1=rec_col, op=ALU.mult),
             reads=(("sm", 8),), writes=(("sm", 3),))
        P.op("dve", lambda e: e.tensor_tensor(out=A.sm[:, 3:4], in0=A.sm[:, 3:4], in1=A.sm[:, 0:1], op=ALU.mult),
             reads=(("sm", 3), ("sm", 0)), writes=(("sm", 3),))
        P.op("act", lambda e: e.activation(out=A.sm[:, 1:2], in_=A.sm[:, 3:4], func=AF.Sqrt, scale=1.0 / 256.0,
                                           bias=c["eps"][:]), reads=(("sm", 3), ("c", "eps")), writes=(("sm", 1),))
        P.op("dve", lambda e: e.reciprocal(out=A.sm[:, 2:3], in_=A.sm[:, 1:2]), reads=(("sm", 1),), writes=(("sm", 2),))
        P.op("dve", lambda e: e.tensor_tensor(out=A.sm[:, 2:3], in0=A.sm[:, 2:3], in1=rec_col, op=ALU.mult),
             reads=(("sm", 2), ("sm", 8)), writes=(("sm", 2),))
    P.op("dve", lambda e: e.tensor_scalar(out=A.on[:], in0=otot[:, 0:256], scalar1=A.sm[:, 2:3], scalar2=None,
                                          op0=ALU.mult), reads=(kot, ("sm", 2)), writes=(("on",),))
    _transpose_pair(b, lambda j: A.on[:, j * 128:(j + 1) * 128], (("on",),), 0)
    for j in range(2):
        if wcol_fn is None:
            P.op("dve", lambda e, j=j: e.tensor_tensor(out=A.mst[:, j, tok], in0=b.pt[0][:, j * 128:(j + 1) * 128],
                                                       in1=A.gT[:, j, tok], op=ALU.mult),
                 reads=(("pt", 0), A.k_gT), writes=(("mst",),))
        else:
            P.op("dve", lambda e, j=j: e.scalar_tensor_tensor(out=A.mst[:, j, tok], in0=b.pt[0][:, j * 128:(j + 1) * 128],
                                                              scalar=wcol_fn(j), in1=A.gT[:, j, tok],
                                                              op0=ALU.mult, op1=ALU.mult),
                 reads=(("pt", 0), A.k_gT, ("c", "mlw")), writes=(("mst",),))


def _sample_states(b, A, tok, s_in, s_out, ncol, decay, ml):
    P, c = b.P, b.c
    qsrc = A.qT if ml is None else A.qi
    for g in range(4):
        for bb in range(4):
            P.dma("sp", A.Sst[:, bb, :, 0:256], s_in[g * 4 + bb].rearrange("(j p) e -> p j e", p=128),
                  writes=(("Sst",),))
        if ml is not None:
            for bb in range(4):
                for j in range(2):
                    P.op("act", lambda e, bb=bb, j=j: e.copy(out=A.Sst[:, bb, j, 256:257],
                                                            in_=A.ncol[:, j, g * 4 + bb:g * 4 + bb + 1]),
                         reads=(("ncol",), ("Sst",)), writes=(("Sst",),))
        P.op("act", lambda e: e.copy(out=A.Sstb[:], in_=A.Sst[:]), reads=(("Sst",),), writes=(("Sstb",),))
        for j in range(2):
            if ml is None:
                src = A.qT[:, j, tok]
            else:
                src = A.qi[:, j, :]
            P.op("dve", lambda e, j=j, src=src: e.tensor_tensor(
                out=A.qmx[:, j, :, :], in0=c["qmask"][:, g * 4:(g + 1) * 4, :],
                in1=src.unsqueeze(1).broadcast_to([128, 4, 128]), op=ALU.mult),
                reads=(A.k_qT, ("qi",), ("c", "qmask")), writes=(("qmx",),))
        fns = []
        for bb in range(4):
            for j in range(2):
                first = (g == 0 and bb == 0 and j == 0)
                last = (g == 3 and bb == 3 and j == 1)
                fns.append(lambda e, bb=bb, j=j, first=first, last=last: e.matmul(
                    b.pb[5][:, 0:ncol], lhsT=A.qmx[:, j, bb, :], rhs=A.Sstb[:, bb, j, 0:ncol], start=first, stop=last))
        P.group("pe", fns, reads=(("qmx",), ("Sstb",)), writes=(("pb", 5),))
        P.op("dve", lambda e: e.tensor_tensor(
            out=A.kdm[:], in0=A.kd[:].unsqueeze(1).broadcast_to([128, 4, 256]),
            in1=c["rowmask"][:, g * 4:(g + 1) * 4].unsqueeze(2).broadcast_to([128, 4, 256]), op=ALU.mult),
            reads=(("kd",), ("c", "rowmask")), writes=(("kdm",),))
        for bb in range(4):
            for j in range(2):
                bank = (bb * 2 + j) % 2
                P.op("pe", lambda e, bb=bb, j=j, bank=bank: e.matmul(
                    b.sreg(bank, ncol), lhsT=A.kdm[:, bb, j * 128:(j + 1) * 128], rhs=A.v[:, 8, 0:ncol],
                    start=True, stop=True), reads=(("kdm",), A.k_v, ("v1",)), writes=(b.skey(bank),))
                if ml is None:
                    P.op("dve", lambda e, bb=bb, j=j, bank=bank: e.scalar_tensor_tensor(
                        out=A.Sst[:, bb, j, 0:ncol], in0=A.Sst[:, bb, j, 0:ncol], scalar=decay,
                        in1=b.sreg(bank, ncol), op0=ALU.mult, op1=ALU.add),
                        reads=(("Sst",), b.skey(bank)), writes=(("Sst",),))
                else:
                    sq = g * 4 + bb
                    P.op("dve", lambda e, bb=bb, j=j, bank=bank, sq=sq: e.scalar_tensor_tensor(
                        out=A.Sst[:, bb, j, 0:ncol], in0=A.Sst[:, bb, j, 0:ncol], scalar=ml.carry_col(sq),
                        in1=b.sreg(bank, ncol), op0=ALU.mult, op1=ALU.add),
                        reads=(("Sst",), b.skey(bank), ("I",)), writes=(("Sst",),))
        for bb in range(4):
            P.dma("sp", s_out[g * 4 + bb].rearrange("(j p) e -> p j e", p=128), A.Sst[:, bb, :, 0:256],
                  reads=(("Sst",),), writes=(("dram", "sout"),), semkey=("Sst_out",))
        if ml is not None:
            for bb in range(4):
                for j in range(2):
                    P.op("act", lambda e, bb=bb, j=j: e.copy(out=A.ncol[:, j, g * 4 + bb:g * 4 + bb + 1],
                                                            in_=A.Sst[:, bb, j, 256:257]),
                         reads=(("Sst",),), writes=(("ncol",),))


class _ML:
    def __init__(self, A, TP):
        self.A, self.TP = A, TP

    def carry_col(self, sq):
        t = self.TP + 8 * sq + 7
        return self.A.I[:, t:t + 1]


def _ml_proj_gen(b, A, h):
    cfg, P, c = b.cfg, b.P, b.c
    T, TP, RW, MW, MH, KC = cfg.T, cfg.TP, cfg.RW, cfg.MW, cfg.MH, cfg.KC
    W = b.w_in_ab
    base = 4 * RW
    sets = [(0, 1, 2)]
    if h == 0:
        b.wi = 0
        b.NW = 1
        P.dma("pool", b.wb[1][:, 0:KC, 0:2 * MH],
              W[:, base + 4 * MW:base + 4 * MW + 2 * MH].rearrange("(k p) n -> p k n", p=128),
              reads=(("coll",),), writes=(("wb", 1),))
    yield from _proj_fm_gen(b, W, base + 0 * MW + h * 256, _evac_to(b, A.qT, A.k_qT), sets)
    yield from _proj_fm_gen(b, W, base + 1 * MW + h * 256, _evac_to(b, A.kT, A.k_kT), sets)
    yield from _proj_tm_gen(b, W, base + 2 * MW + h * 256, 256, A.v, A.k_v)
    yield from _proj_fm_gen(b, W, base + 3 * MW + h * 256, _evac_to(b, A.gT, A.k_gT, AF.Sigmoid), sets)


def _ml_rec(b, A, h):
    cfg, P, c = b.cfg, b.P, b.c
    T, TP, RW, MW, MH, KC = cfg.T, cfg.TP, cfg.RW, cfg.MW, cfg.MH, cfg.KC
    W = b.w_in_ab
    ti = b.ti
    base = 4 * RW
    xn_res = tuple(("xn", k) for k in range(KC))
    IG, F, M, BE, I, RST = A.raw[:, 0, :], A.raw[:, 1, :], A.rot[:, 0, :], A.rot[:, 1, :], A.t1[:], A.t2[:]
    A.I = A.t1
    rIG, rF, rM, rBE, rI = ("raw0",), ("raw1",), ("rot0",), ("rot1",), ("I",)
    if h == 0:
        P.barrier()
        P.dma("sp", A.t2[:], b.tbl["rst"][0], writes=(("rst",),))
        A.wgate = b.wb[1]
    for gi, (dst, dres) in enumerate(((IG, rIG), (BE, rBE))):
        col = gi * MH + h
        P.op("dve", lambda e, col=col: e.tensor_copy(
            out=A.wrep, in_=A.wgate[:, 0:KC, col:col + 1].broadcast_to([128, KC, 128])),
            reads=(("wb", 1),), writes=(("wrep",),))
        fns = []
        for k in range(KC):
            for j in range(3):
                fns.append(lambda e, k=k, j=j: e.matmul(b.pb[4 + j][:, 0:384], lhsT=A.wrep[:, k, :],
                                                        rhs=b.xn[:, k, j * 384:(j + 1) * 384],
                                                        start=(k == 0), stop=(k == KC - 1)))
        P.group("pe", fns, reads=(("wrep",),) + xn_res, writes=tuple(("pb", 4 + j) for j in range(3)))
        for j in range(3):
            P.op("dve", lambda e, j=j, dst=dst, col=col: e.tensor_scalar(
                out=dst[:, j * 384:(j + 1) * 384], in0=b.pb[4 + j][:, 0:384], scalar1=c["bif"][:, col:col + 1],
                scalar2=None, op0=ALU.add), reads=(("pb", 4 + j), ("c", "bif")), writes=(dres,))
    P.op("act", lambda e: e.activation(out=I, in_=BE, func=AF.Abs), reads=(rBE,), writes=(rI,))
    P.op("act", lambda e: e.activation(out=I, in_=I, func=AF.Exp, scale=-1.0), reads=(rI,), writes=(rI,))
    P.op("act", lambda e: e.activation(out=I, in_=I, func=AF.Ln, bias=c["one"][:], scale=1.0),
         reads=(rI, ("c", "one")), writes=(rI,))
    P.op("dve", lambda e: e.scalar_tensor_tensor(out=I, in0=BE, scalar=0.0, in1=I, op0=ALU.min, op1=ALU.subtract),
         reads=(rBE, rI), writes=(rI,))
    LF = I
    if ti == 0:
        P.op("dve", lambda e: e.memset(A.S[:], 0.0), writes=(("S",),))
        P.op("dve", lambda e: e.memset(A.sm[:, 10:11], 0.0), writes=(("sm", 10),))
    else:
        P.dma("sp", A.S[:, :, 0:256], b.o_mc_p[h].rearrange("(j p) e -> p j e", p=128),
              reads=(("dram", "mcp", h),), writes=(("S",),))
        P.dma("sp", A.S[:, :, 256], b.o_mn_p[h].rearrange("(j p) -> p j", p=128),
              reads=(("dram", "mnp", h),), writes=(("S",),), allow_slow_non_contiguous=True)
        P.dma("sp", A.sm[:, 10:11], b.o_mm_p[0:1, h:h + 1].broadcast_to([128, 1]) if False else
              b.o_mm_p[0, h:h + 1].partition_broadcast(128),
              reads=(("dram", "mmp", h),), writes=(("sm", 10),))
    P.op("act", lambda e: e.copy(out=A.Sbf[:], in_=A.S[:]), reads=(("S",),), writes=(("Sbf",),))
    P.dma("sp", A.mrow[:], b.s_mm[ti, h].partition_broadcast(128), writes=(("mrow",),))
    P.dma("sp", A.nrow[:], b.s_mn[ti, :, h, :], writes=(("nrow",),))
    for j in range(2):
        P.op("pe", lambda e, j=j: e.transpose(out=b.pb[5][:, 0:16], in_=A.nrow[:, j * 128:(j + 1) * 128],
                                              identity=c["ident_f"][0:16, 0:16]),
             reads=(("nrow",), ("c", "ident_f")), writes=(("pb", 5),))
        P.op("act", lambda e, j=j: e.copy(out=A.ncol[:, j, :], in_=b.pb[5][:, 0:16]), reads=(("pb", 5),),
             writes=(("ncol",),))
    P.op("dve", lambda e: e.tensor_tensor_scan(out=F, data0=RST, data1=LF, initial=0.0, op0=ALU.mult, op1=ALU.add),
         reads=(("rst",), rI), writes=(rF,))
    if cfg.xch:
        P.op("dve", lambda e: e.tensor_tensor_scan(out=M[:, 0:TP], data0=LF[:, 0:TP], data1=IG[:, 0:TP],
                                                   initial=0.0, op0=ALU.add, op1=ALU.max),
             reads=(rI, rIG), writes=(rM,))
        P.op("dve", lambda e: e.tensor_tensor(out=A.sm[:, 12:13], in0=F[:, TP - 1:TP], in1=M[:, TP - 1:TP], op=ALU.subtract),
             reads=(rF, rM), writes=(("sm", 12),))
        P.op("dve", lambda e: e.tensor_tensor(out=BE[:, 0:TP], in0=IG[:, 0:TP], in1=F[:, 0:TP], op=ALU.subtract),
             reads=(rIG, rF), writes=(rBE,))
        P.op("act", lambda e: e.activation(out=BE[:, 0:TP], in_=BE[:, 0:TP], func=AF.Exp, bias=A.sm[:, 12:13], scale=1.0),
             reads=(rBE, ("sm", 12)), writes=(rBE,))
        for cidx in range(8):
            tok = slice(cidx * 128, (cidx + 1) * 128)
            P.op("pe", lambda e: e.matmul(b.pb[3][:, 128:129], lhsT=BE[0:1, tok], rhs=c["one"][0:1, 0:1], start=True, stop=True),
                 reads=(rBE, ("c", "one")), writes=(("pb", 3),))
            P.op("act", lambda e: e.copy(out=A.cols[:, 0:1], in_=b.pb[3][:, 128:129]), reads=(("pb", 3),), writes=(("cols",),))
            _transpose_pair(b, lambda j: A.kT[:, j, tok], (A.k_kT,), 1)
            P.op("dve", lambda e: e.tensor_scalar(out=A.kd[:], in0=b.pt[0][:, 256:512], scalar1=A.cols[:, 0:1],
                                                  scalar2=1.0 / 16.0, op0=ALU.mult, op1=ALU.mult),
                 reads=(("pt", 0), ("cols",)), writes=(("kd",),))
            for j in range(2):
                bank = j
                P.op("pe", lambda e, j=j, bank=bank: e.matmul(b.sreg(bank, 257), lhsT=A.kd[:, j * 128:(j + 1) * 128],
                                                              rhs=A.v[:, cidx, 0:257], start=True, stop=True),
                     reads=(("kd",), A.k_v, ("v1",)), writes=(b.skey(bank),))
                P.op("dve", lambda e, j=j, bank=bank: e.tensor_tensor(out=A.S[:, j, 0:257], in0=A.S[:, j, 0:257],
                                                                      in1=b.sreg(bank, 257), op=ALU.add),
                     reads=(("S",), b.skey(bank)), writes=(("S",),))
        P.op("act", lambda e: e.copy(out=A.S[:, 0, 258:259], in_=M[:, TP - 1:TP]), reads=(rM,), writes=(("S",),))
        _xchg(b, A.S[:].rearrange("p j e -> p (j e)"), b.snd[:, :], b.rcv[:, :], ("S",))
        P.op("act", lambda e: e.copy(out=A.sm[:, 10:11], in_=A.S[:, 0, 258:259]), reads=(("S",),), writes=(("sm", 10),))
        P.op("act", lambda e: e.copy(out=A.Sbf[:], in_=A.S[:]), reads=(("S",),), writes=(("Sbf",),))
    P.op("dve", lambda e: e.tensor_tensor_scan(out=M[:, 0:TP], data0=LF[:, 0:TP], data1=IG[:, 0:TP],
                                               initial=A.sm[:, 10:11], op0=ALU.add, op1=ALU.max),
         reads=(rI, rIG, ("sm", 10)), writes=(rM,))
    for sq in range(16):
        sl = slice(TP + 8 * sq, TP + 8 * sq + 8)
        P.op("dve", lambda e, sl=sl, sq=sq: e.tensor_tensor_scan(out=M[:, sl], data0=LF[:, sl], data1=IG[:, sl],
                                                                initial=A.mrow[:, sq:sq + 1], op0=ALU.add, op1=ALU.max),
             reads=(rI, rIG, ("mrow",)), writes=(rM,))
    P.op("dve", lambda e: e.tensor_tensor(out=IG, in0=IG, in1=F, op=ALU.subtract), reads=(rIG, rF), writes=(rIG,))
    P.op("dve", lambda e: e.tensor_tensor(out=F, in0=F, in1=M, op=ALU.subtract), reads=(rF, rM), writes=(rF,))
    P.op("act", lambda e: e.copy(out=A.sm[:, 11:12], in_=M[:, TP - 1:TP]), reads=(rM,), writes=(("sm", 11),))
    P.op("act", lambda e: e.copy(out=A.mend[:], in_=M[:, TP:T].rearrange("p (b t) -> p b t", t=8)[:, :, 7]),
         reads=(rM,), writes=(("mend",),))
    P.op("act", lambda e: e.activation(out=M, in_=M, func=AF.Exp, scale=-1.0), reads=(rM,), writes=(rM,))
    P.op("dve", lambda e: e.tensor_tensor(
        out=BE[:, 0:TP].rearrange("p (c t) -> p c t", t=128), in0=IG[:, 0:TP].rearrange("p (c t) -> p c t", t=128),
        in1=F[:, 0:TP].rearrange("p (c t) -> p c t", t=128)[:, :, 127:128].broadcast_to([128, 8, 128]), op=ALU.add),
        reads=(rIG, rF), writes=(rBE,))
    P.op("dve", lambda e: e.tensor_tensor(
        out=BE[:, TP:T].rearrange("p (c t) -> p c t", t=8), in0=IG[:, TP:T].rearrange("p (c t) -> p c t", t=8),
        in1=F[:, TP:T].rearrange("p (c t) -> p c t", t=8)[:, :, 7:8].broadcast_to([128, 16, 8]), op=ALU.add),
        reads=(rIG, rF), writes=(rBE,))
    P.op("act", lambda e: e.activation(out=BE, in_=BE, func=AF.Exp), reads=(rBE,), writes=(rBE,))
    P.op("dve", lambda e: e.tensor_scalar(out=I[:, 0:128], in0=F[:, 0:128], scalar1=A.sm[:, 10:11], scalar2=None,
                                          op0=ALU.add), reads=(rF, ("sm", 10)), writes=(rI,))
    for cc in range(1, 8):
        P.op("dve", lambda e, cc=cc: e.tensor_scalar(out=I[:, cc * 128:(cc + 1) * 128], in0=F[:, cc * 128:(cc + 1) * 128],
                                                     scalar1=F[:, cc * 128 - 1:cc * 128], scalar2=None, op0=ALU.subtract),
             reads=(rF,), writes=(rI,))
    P.op("dve", lambda e: e.tensor_tensor(
        out=I[:, TP:T].rearrange("p (c t) -> p c t", t=8), in0=F[:, TP:T].rearrange("p (c t) -> p c t", t=8),
        in1=A.mrow[:].unsqueeze(2).broadcast_to([128, 16, 8]), op=ALU.add), reads=(rF, ("mrow",)), writes=(rI,))
    P.op("act", lambda e: e.activation(out=I, in_=I, func=AF.Exp), reads=(rI,), writes=(rI,))
    ml = _ML(A, TP)
    pending = None
    for cidx in range(9):
        smp = 1 if cidx == 8 else 0
        tok = slice(cidx * 128, (cidx + 1) * 128)
        otot, kot = A.otot_db[cidx % 2], ("otot", cidx % 2)
        fns = []
        for i, src in enumerate((IG, BE, M)):
            fns.append(lambda e, i=i, src=src: e.matmul(b.pb[3][:, 128 + i:129 + i], lhsT=src[0:1, tok], rhs=c["one"][0:1, 0:1],
                                                        start=True, stop=True))
        P.group("pe", fns, reads=(rIG, rBE, rM, ("c", "one")), writes=(("pb", 3),))
        P.op("act", lambda e: e.copy(out=A.cols[:, 0:3], in_=b.pb[3][:, 128:131]), reads=(("pb", 3),), writes=(("cols",),))
        P.group("pe", [lambda e, j=j: e.matmul(b.pb[3][:, 0:128], lhsT=A.kT[:, j, tok], rhs=A.qT[:, j, tok],
                                               start=(j == 0), stop=(j == 1)) for j in range(2)],
                reads=(A.k_kT, A.k_qT), writes=(("pb", 3),))
        P.op("dve", lambda e: e.tensor_tensor(out=A.D[:], in0=F[:, tok], in1=c["masks"][:, smp, :], op=ALU.add),
             reads=(rF, ("c", "masks")), writes=(("D",),))
        P.op("act", lambda e: e.activation(out=A.D[:], in_=A.D[:], func=AF.Exp, bias=A.cols[:, 0:1], scale=1.0),
             reads=(("D",), ("cols",)), writes=(("D",),))
        P.op("dve", lambda e: e.scalar_tensor_tensor(out=A.attm[:], in0=b.pb[3][:, 0:128], scalar=1.0 / 16.0, in1=A.D[:],
                                                     op0=ALU.mult, op1=ALU.mult),
             reads=(("pb", 3), ("D",)), writes=(("attm",),))
        P.op("pe", lambda e: e.matmul(b.pb[4][:, 0:257], lhsT=A.attm[:], rhs=A.v[:, cidx, 0:257], start=True, stop=True),
             reads=(("attm",), A.k_v, ("v1",)), writes=(("pb", 4),))
        P.op("act", lambda e: e.copy(out=A.osb[:, 0:257], in_=b.pb[4][:, 0:257]), reads=(("pb", 4),), writes=(("osb",),))
        for j in range(2):
            P.op("dve", lambda e, j=j: e.tensor_tensor(out=A.qi[:, j, :], in0=A.qT[:, j, tok], in1=I[:, tok], op=ALU.mult),
                 reads=(A.k_qT, rI), writes=(("qi",),))
        _transpose_pair(b, lambda j: A.kT[:, j, tok], (A.k_kT,), 1)
        P.op("dve", lambda e: e.tensor_scalar(out=A.kd[:], in0=b.pt[0][:, 256:512], scalar1=A.cols[:, 1:2],
                                              scalar2=1.0 / 16.0, op0=ALU.mult, op1=ALU.mult),
             reads=(("pt", 0), ("cols",)), writes=(("kd",),))
        if not smp:
            P.group("pe", [lambda e, j=j: e.matmul(b.pb[5][:, 0:257], lhsT=A.qi[:, j, :], rhs=A.Sbf[:, j, 0:257],
                                                   start=(j == 0), stop=(j == 1)) for j in range(2)],
                    reads=(("qi",), ("Sbf",)), writes=(("pb", 5),))
            end = cidx * 128 + 127
            for j in range(2):
                bank = j
                P.op("pe", lambda e, j=j, bank=bank: e.matmul(b.sreg(bank, 257), lhsT=A.kd[:, j * 128:(j + 1) * 128],
                                                              rhs=A.v[:, cidx, 0:257], start=True, stop=True),
                     reads=(("kd",), A.k_v, ("v1",)), writes=(b.skey(bank),))
                P.op("dve", lambda e, j=j, bank=bank: e.scalar_tensor_tensor(
                    out=A.S[:, j, 0:257], in0=A.S[:, j, 0:257], scalar=I[:, end:end + 1], in1=b.sreg(bank, 257),
                    op0=ALU.mult, op1=ALU.add), reads=(("S",), b.skey(bank), rI), writes=(("S",),))
            P.op("act", lambda e: e.copy(out=A.Sbf[:], in_=A.S[:]), reads=(("S",),), writes=(("Sbf",),))
        else:
            _sample_states(b, A, tok, b.s_mc[ti, :, h], b.o_mc_s[ti, :, h], 257, None, ml)
        P.op("dve", lambda e: e.tensor_tensor(out=otot[:, 0:257], in0=b.pb[5][:, 0:257], in1=A.osb[:, 0:257], op=ALU.add),
             reads=(("pb", 5), ("osb",)), writes=(kot,))
        P.op("act", lambda e: e.copy(out=otot[:, 258:259], in_=A.cols[:, 2:3]), reads=(("cols",), kot), writes=(kot,))

        def back(tok=tok, otot=otot, kot=kot):
            P.op("act", lambda e: e.activation(out=A.sm[:, 4:5], in_=otot[:, 256:257], func=AF.Abs),
                 reads=(kot,), writes=(("sm", 4),))
            P.op("dve", lambda e: e.tensor_tensor(out=A.sm[:, 5:6], in0=A.sm[:, 4:5], in1=otot[:, 258:259], op=ALU.max),
                 reads=(("sm", 4), kot), writes=(("sm", 5),))
            P.op("dve", lambda e: e.reciprocal(out=A.sm[:, 8:9], in_=A.sm[:, 5:6]), reads=(("sm", 5),), writes=(("sm", 8),))
            _norm_gate_store(b, A, tok, A.sm[:, 8:9], lambda j: c["mlw"][:, h * 2 + j:h * 2 + j + 1], otot, kot)

        if pending is not None:
            pending()
        pending = back
    pending()
    for j in range(2):
        P.dma("sp", b.o_mc_p[h, j * 128:(j + 1) * 128, :], A.S[:, j, 0:256], reads=(("S",),),
              writes=(("dram", "mcp", h),), semkey=("Sout",))
    P.dma("sp", b.o_mn_p[h].rearrange("(j p) -> p j", p=128), A.S[:, :, 256], reads=(("S",),),
          writes=(("dram", "mnp", h),), semkey=("Sout",), allow_slow_non_contiguous=True)
    P.dma("sp", b.o_mm_p[0:1, h:h + 1], A.sm[0:1, 11:12], reads=(("sm", 11),), writes=(("dram", "mmp", h),),
          semkey=("Sout",))
    P.dma("sp", b.o_mm_s[ti, h:h + 1, :], A.mend[0:1, :], reads=(("mend",),), writes=(("dram", "mms"),),
          semkey=("Sout",))
    for j in range(2):
        P.op("pe", lambda e, j=j: e.transpose(out=b.pb[5][0:16, 0:128], in_=A.ncol[:, j, :], identity=c["ident_f"][:]),
             reads=(("ncol",), ("c", "ident_f")), writes=(("pb", 5),))
        P.op("act", lambda e, j=j: e.copy(out=A.nrow[:, j * 128:(j + 1) * 128], in_=b.pb[5][0:16, 0:128]),
             reads=(("pb", 5),), writes=(("nrow",),))
    P.dma("sp", b.o_mn_s[ti, :, h, :], A.nrow[:], reads=(("nrow",),), writes=(("dram", "mns"),), semkey=("Sout",))
    _store_mixed(b, A, cfg.RH + h)


def mixer_c_real(b):
    cfg, P, c = b.cfg, b.P, b.c
    P.barrier()
    KC, T, TP, D, HH = cfg.KC, cfg.T, cfg.TP, cfg.D, cfg.HH
    W = b.w_in_c
    ti = b.ti
    sets = [(0, 1, 2)]
    with ExitStack() as es:
        A = _AB()
        A.Q32_db = [b.sb(es, "Q32", [128, 2, T], BF16) for _ in range(2)]
        A.F32_db = [b.sb(es, "F32", [128, 2, T], F32) for _ in range(2)]
        A.LN = b.sb(es, "LN", [128, T], F32)
        A.G = b.sb(es, "G", [128, T], F32)
        A.E = b.sb(es, "E", [128, T], F32)
        A.RST = b.sb(es, "RST", [128, T], F32)
        A.qT_db = [b.sb(es, "qT", [128, 2, T], BF16)] * 2
        A.kT_db = [b.sb(es, "kT", [128, 2, T], BF16)] * 2
        A.gT_db = [b.sb(es, "gT", [128, 2, T], BF16) for _ in range(2)]
        A.v_db = [b.sb(es, "v", [128, 9, 256], BF16) for _ in range(2)]
        A.mst = b.sb(es, "mst", [128, 2, T], BF16)
        A.S = b.sb(es, "S", [128, 128], F32)
        A.Sp = b.sb(es, "Sp", [128, 128], F32)
        A.Sbf = b.sb(es, "Sbf", [128, 128], BF16)
        A.Sst = b.sb(es, "Sst", [128, 8, 128], F32)
        A.Sstb = b.sb(es, "Sstb", [128, 8, 128], BF16)
        A.qmx = b.sb(es, "qmx", [128, 8, 128], BF16)
        A.kdm = b.sb(es, "kdm", [128, 8, 128], BF16)
        A.attm = b.sb(es, "attm", [128, 128], BF16)
        A.osb = b.sb(es, "osb", [128, 128], F32)
        A.otot_db = [b.sb(es, "otot", [128, 128], F32) for _ in range(2)]
        A.junk = b.sb(es, "junk", [128, 128], F32)
        A.on = b.sb(es, "on", [128, 128], BF16)
        A.kd = b.sb(es, "kd", [128, 128], BF16)
        A.sm = b.sb(es, "sm", [128, 40], F32)
        A.ss = b.sb(es, "ss", [128, 4], F32)
        P.dma("sp", A.RST[:], b.tbl["rst"][1], writes=(("rst",),))
        views = [_View(A, 0), _View(A, 1)]

        def proj_gen(hp):
            V = views[hp % 2]
            yield from _proj_fm_gen(b, W, 0 * D + hp * 256, _evac_to(b, V.Q32, V.k_Q32, AF.Silu), sets)
            yield from _proj_fm_gen(b, W, 1 * D + hp * 256, _evac_to(b, V.F32, V.k_F32, AF.Sigmoid), sets)
            yield from _proj_tm_gen(b, W, 2 * D + hp * 256, 256, V.v, V.k_v)
            yield from _proj_fm_gen(b, W, 3 * D + hp * 256, _evac_to(b, V.gT, V.k_gT, AF.Silu), sets)

        def rec(hp):
            V = views[hp % 2]
            for s in range(2):
                _hg_head(b, V, hp * 2 + s, s)
            for j in range(2):
                P.dma("sp", b.mT[hp * 2 + j], A.mst[:, j, :], reads=(("mst",),), writes=(("dram", "m", hp * 2 + j),),
                      semkey=("mst",))

        _pipeline(b, HH // 2, proj_gen, rec)
        P.barrier()


def _hg_head(b, A, hh, s):
    cfg, P, c = b.cfg, b.P, b.c
    T, TP = cfg.T, cfg.TP
    ti = b.ti
    FG = A.F32[:, s, :]
    Q = A.Q32[:, s, :]
    rFG, rQ = A.k_F32, A.k_Q32
    P.op("dve", lambda e: e.tensor_scalar(out=FG, in0=FG, scalar1=c["oml"][:, hh:hh + 1], scalar2=c["lb"][:, hh:hh + 1],
                                          op0=ALU.mult, op1=ALU.add), reads=(rFG, ("c", "oml"), ("c", "lb")), writes=(rFG,))
    P.op("act", lambda e: e.activation(out=A.LN[:], in_=FG, func=AF.Ln), reads=(rFG,), writes=(("LN",),))
    P.op("dve", lambda e: e.tensor_tensor_scan(out=A.G[:], data0=A.RST[:], data1=A.LN[:], initial=0.0,
                                               op0=ALU.mult, op1=ALU.add), reads=(("rst",), ("LN",)), writes=(("G",),))
    P.op("dve", lambda e: e.tensor_scalar(out=FG, in0=FG, scalar1=-1.0, scalar2=1.0, op0=ALU.mult, op1=ALU.add),
         reads=(rFG,), writes=(rFG,))
    for cc in range(8):
        P.op("dve", lambda e, cc=cc: e.tensor_scalar(out=A.sm[:, cc:cc + 1], in0=A.G[:, cc * 128 + 63:cc * 128 + 64],
                                                     scalar1=-1.0, scalar2=None, op0=ALU.mult),
             reads=(("G",),), writes=(("sm", "a"),))
    for cc in range(8):
        tok = slice(cc * 128, (cc + 1) * 128)
        P.op("act", lambda e, cc=cc, tok=tok: e.activation(out=A.E[:, tok], in_=A.G[:, tok], func=AF.Exp,
                                                          bias=A.sm[:, cc:cc + 1], scale=1.0),
             reads=(("G",), ("sm", "a")), writes=(("E",),))
    P.op("act", lambda e: e.activation(out=A.E[:, TP:T], in_=A.G[:, TP:T], func=AF.Exp), reads=(("G",),), writes=(("E",),))
    P.op("dve", lambda e: e.tensor_tensor(out=A.qT[:, s, :], in0=Q, in1=A.E[:], op=ALU.mult), reads=(rQ, ("E",)),
         writes=(A.k_qT,))
    P.op("act", lambda e: e.copy(out=A.sm[:, 8:16], in_=A.E[:, 0:TP].rearrange("p (c t) -> p c t", t=128)[:, :, 127]),
         reads=(("E",),), writes=(("sm", "b"),))
    P.op("act", lambda e: e.copy(out=A.sm[:, 16:32], in_=A.E[:, TP:T].rearrange("p (c t) -> p c t", t=8)[:, :, 7]),
         reads=(("E",),), writes=(("sm", "b"),))
    P.op("act", lambda e: e.activation(out=A.sm[:, 32:40], in_=A.sm[:, 0:8], func=AF.Exp, scale=-1.0),
         reads=(("sm", "a"),), writes=(("sm", "c"),))
    P.op("dve", lambda e: e.reciprocal(out=A.E[:], in_=A.E[:]), reads=(("E",),), writes=(("E",),))
    P.op("dve", lambda e: e.tensor_tensor(out=A.kT[:, s, :], in0=FG, in1=A.E[:], op=ALU.mult), reads=(rFG, ("E",)),
         writes=(A.k_kT,))
    if ti == 0:
        P.op("dve", lambda e: e.memset(A.S[:], 0.0), writes=(("S",),))
    else:
        P.dma("sp", A.S[:], b.o_hg_p[hh], reads=(("dram", "hgp", hh),), writes=(("S",),))
    vs = slice(s * 128, (s + 1) * 128)
    if cfg.xch:
        for cidx in range(8):
            tok = slice(cidx * 128, (cidx + 1) * 128)
            P.op("pe", lambda e: e.transpose(out=b.pt[0][:, 256:384], in_=A.kT[:, s, tok], identity=c["ident_bf"][:]),
                 reads=(A.k_kT, ("c", "ident_bf")), writes=(("pt", 0),))
            P.op("act", lambda e: e.copy(out=A.kd[:], in_=b.pt[0][:, 256:384]), reads=(("pt", 0),), writes=(("kd",),))
            P.op("dve", lambda e: e.tensor_scalar(out=A.Sp[:], in0=A.S[:], scalar1=A.sm[:, 32 + cidx:33 + cidx], scalar2=None,
                                                  op0=ALU.mult), reads=(("S",), ("sm", "c")), writes=(("Sp",),))
            P.op("pe", lambda e: e.matmul(b.sreg(0, 128), lhsT=A.kd[:], rhs=A.v[:, cidx, vs], start=True, stop=True),
                 reads=(("kd",), A.k_v), writes=(b.skey(0),))
            P.op("dve", lambda e: e.tensor_tensor(out=A.Sp[:], in0=b.sreg(0, 128), in1=A.Sp[:], op=ALU.add),
                 reads=(b.skey(0), ("Sp",)), writes=(("Sp",),))
            P.op("dve", lambda e: e.tensor_scalar(out=A.S[:], in0=A.Sp[:], scalar1=A.sm[:, 8 + cidx:9 + cidx], scalar2=None,
                                                  op0=ALU.mult), reads=(("Sp",), ("sm", "b")), writes=(("S",),))
        _xchg(b, A.S[:], b.snd_c[:, :], b.rcv_c[:, :], ("S",))
    pending = None
    for cidx in range(9):
        smp = 1 if cidx == 8 else 0
        tok = slice(cidx * 128, (cidx + 1) * 128)
        otot, kot = A.otot_db[cidx % 2], ("otot", cidx % 2)
        P.op("pe", lambda e: e.matmul(b.pb[3][:, 0:128], lhsT=A.kT[:, s, tok], rhs=A.qT[:, s, tok], start=True, stop=True),
             reads=(A.k_kT, A.k_qT), writes=(("pb", 3),))
        P.op("dve", lambda e: e.tensor_tensor(out=A.attm[:], in0=b.pb[3][:, 0:128], in1=c["masks"][:, 2 + smp, :], op=ALU.mult),
             reads=(("pb", 3), ("c", "masks")), writes=(("attm",),))
        P.op("pe", lambda e: e.matmul(b.pb[4][:, 0:128], lhsT=A.attm[:], rhs=A.v[:, cidx, vs], start=True, stop=True),
             reads=(("attm",), A.k_v), writes=(("pb", 4),))
        P.op("act", lambda e: e.copy(out=A.osb[:], in_=b.pb[4][:, 0:128]), reads=(("pb", 4),), writes=(("osb",),))
        P.op("pe", lambda e: e.transpose(out=b.pt[0][:, 256:384], in_=A.kT[:, s, tok], identity=c["ident_bf"][:]),
             reads=(A.k_kT, ("c", "ident_bf")), writes=(("pt", 0),))
        P.op("act", lambda e: e.copy(out=A.kd[:], in_=b.pt[0][:, 256:384]), reads=(("pt", 0),), writes=(("kd",),))
        if not smp:
            P.op("dve", lambda e: e.tensor_scalar(out=A.Sp[:], in0=A.S[:], scalar1=A.sm[:, 32 + cidx:33 + cidx], scalar2=None,
                                                  op0=ALU.mult), reads=(("S",), ("sm", "c")), writes=(("Sp",),))
            P.op("act", lambda e: e.copy(out=A.Sbf[:], in_=A.Sp[:]), reads=(("Sp",),), writes=(("Sbf",),))
            P.op("pe", lambda e: e.matmul(b.pb[5][:, 0:128], lhsT=A.qT[:, s, tok], rhs=A.Sbf[:], start=True, stop=True),
                 reads=(A.k_qT, ("Sbf",)), writes=(("pb", 5),))
            P.op("pe", lambda e: e.matmul(b.sreg(0, 128), lhsT=A.kd[:], rhs=A.v[:, cidx, vs], start=True, stop=True),
                 reads=(("kd",), A.k_v), writes=(b.skey(0),))
            P.op("dve", lambda e: e.tensor_tensor(out=A.Sp[:], in0=b.sreg(0, 128), in1=A.Sp[:], op=ALU.add),
                 reads=(b.skey(0), ("Sp",)), writes=(("Sp",),))
            P.op("dve", lambda e: e.tensor_scalar(out=A.S[:], in0=A.Sp[:], scalar1=A.sm[:, 8 + cidx:9 + cidx], scalar2=None,
                                                  op0=ALU.mult), reads=(("Sp",), ("sm", "b")), writes=(("S",),))
        else:
            for g in range(2):
                P.dma("sp", A.Sst[:], b.s_hg[ti, g * 8:(g + 1) * 8, hh].rearrange("b p e -> p b e"), writes=(("Sst",),))
                P.op("act", lambda e: e.copy(out=A.Sstb[:], in_=A.Sst[:]), reads=(("Sst",),), writes=(("Sstb",),))
                P.op("dve", lambda e, g=g: e.tensor_tensor(
                    out=A.qmx[:], in0=c["qmask"][:, g * 8:(g + 1) * 8, :],
                    in1=A.qT[:, s, tok].unsqueeze(1).broadcast_to([128, 8, 128]), op=ALU.mult),
                    reads=(A.k_qT, ("c", "qmask")), writes=(("qmx",),))
                fns = []
                for bb in range(8):
                    fns.append(lambda e, bb=bb, g=g: e.matmul(b.pb[5][:, 0:128], lhsT=A.qmx[:, bb, :], rhs=A.Sstb[:, bb, :],
                                                              start=(g == 0 and bb == 0), stop=(g == 1 and bb == 7)))
                P.group("pe", fns, reads=(("qmx",), ("Sstb",)), writes=(("pb", 5),))
                P.op("dve", lambda e, g=g: e.tensor_tensor(
                    out=A.kdm[:], in0=A.kd[:].unsqueeze(1).broadcast_to([128, 8, 128]),
                    in1=c["rowmask"][:, g * 8:(g + 1) * 8].unsqueeze(2).broadcast_to([128, 8, 128]), op=ALU.mult),
                    reads=(("kd",), ("c", "rowmask")), writes=(("kdm",),))
                for bb in range(8):
                    bank = bb % 2
                    sq = g * 8 + bb
                    P.op("pe", lambda e, bb=bb, bank=bank: e.matmul(b.sreg(bank, 128), lhsT=A.kdm[:, bb, :],
                                                                    rhs=A.v[:, 8, vs], start=True, stop=True),
                         reads=(("kdm",), A.k_v), writes=(b.skey(bank),))
                    P.op("dve", lambda e, bb=bb, bank=bank: e.tensor_tensor(out=A.Sst[:, bb, :], in0=b.sreg(bank, 128),
                                                                            in1=A.Sst[:, bb, :], op=ALU.add),
                         reads=(b.skey(bank), ("Sst",)), writes=(("Sst",),))
                    P.op("dve", lambda e, bb=bb, sq=sq: e.tensor_scalar(out=A.Sst[:, bb, :], in0=A.Sst[:, bb, :],
                                                                        scalar1=A.sm[:, 16 + sq:17 + sq], scalar2=None,
                                                                        op0=ALU.mult),
                         reads=(("Sst",), ("sm", "b")), writes=(("Sst",),))
                P.dma("sp", b.o_hg_s[ti, g * 8:(g + 1) * 8, hh].rearrange("b p e -> p b e"), A.Sst[:],
                      reads=(("Sst",),), writes=(("dram", "hgs"),), semkey=("Sst_out",))
        P.op("dve", lambda e: e.tensor_tensor(out=otot[:], in0=b.pb[5][:, 0:128], in1=A.osb[:], op=ALU.add),
             reads=(("pb", 5), ("osb",)), writes=(kot,))

        def back(tok=tok, otot=otot, kot=kot):
            P.op("act", lambda e: e.activation(out=A.junk[:], in_=otot[:], func=AF.Square, accum_out=A.ss[:, 0:1]),
                 reads=(kot,), writes=(("junk",), ("ss", 0)))
            P.op("act", lambda e: e.activation(out=A.ss[:, 1:2], in_=A.ss[:, 0:1], func=AF.Sqrt, scale=1.0 / 128.0,
                                               bias=c["eps"][:]), reads=(("ss", 0), ("c", "eps")), writes=(("ss", 1),))
            P.op("dve", lambda e: e.reciprocal(out=A.ss[:, 2:3], in_=A.ss[:, 1:2]), reads=(("ss", 1),), writes=(("ss", 2),))
            P.op("dve", lambda e: e.tensor_scalar(out=A.on[:], in0=otot[:], scalar1=A.ss[:, 2:3], scalar2=None, op0=ALU.mult),
                 reads=(kot, ("ss", 2)), writes=(("on",),))
            P.op("pe", lambda e: e.transpose(out=b.pt[0][:, 0:128], in_=A.on[:], identity=c["ident_bf"][:]),
                 reads=(("on",), ("c", "ident_bf")), writes=(("pt", 0),))
            P.op("dve", lambda e: e.scalar_tensor_tensor(out=A.mst[:, s, tok], in0=b.pt[0][:, 0:128], scalar=c["hgw"][:, 0:1],
                                                         in1=A.gT[:, s, tok], op0=ALU.mult, op1=ALU.mult),
                 reads=(("pt", 0), A.k_gT, ("c", "hgw")), writes=(("mst",),))

        if pending is not None:
            pending()
        pending = back
    pending()
    P.dma("sp", b.o_hg_p[hh], A.S[:], reads=(("S",),), writes=(("dram", "hgp", hh),), semkey=("Sout",))
```

```python
import numpy as np
import ml_dtypes
from contextlib import ExitStack
import concourse.bass as bass
import concourse.mybir as mybir
from concourse.bass_utils import run_bass_kernel_spmd

F32 = mybir.dt.float32
BF16 = mybir.dt.bfloat16
AF = mybir.ActivationFunctionType
ALU = mybir.AluOpType
AX = mybir.AxisListType
EPS = 1e-6


class Cfg:
    def __init__(self, D=4096, RH=8, MH=8, DFF=11008, NT=1, ncores=8, batch=4, seq=2048, dec_batch=128,
                 past_len=16384, xch=True):
        self.xch = xch
        assert (NT == 1) if xch else True
        self.groups = [[2 * i, 2 * i + 1] for i in range(ncores // 2)]
        self.D = D
        self.KC = D // 128
        self.RH, self.MH = RH, MH
        self.RW = D // 2
        self.MW = D - self.RW
        assert self.RW // RH == 256 and self.MW // MH == 256
        self.HH = D // 128
        self.DFF = DFF
        self.FC = DFF // 128
        assert DFF % 128 == 0
        self.NT = NT
        self.ncores = ncores
        self.T = 1152
        self.TP = 1024
        self.NS = 16
        self.batch, self.seq, self.dec_batch, self.past_len = batch, seq, dec_batch, past_len
        self.ABC = 4 * self.RW + 4 * self.MW + 2 * MH
        self.CC = 4 * D
        nq = 4 if self.FC >= 8 else 1
        base = self.FC // nq
        rem = self.FC % nq
        self.fq = []
        s = 0
        for i in range(nq):
            n = base + (1 if i < rem else 0)
            self.fq.append((s, n))
            s += n


class Ev:
    __slots__ = ("sem", "val", "key", "dkey")

    def __init__(self, sem, val, key, dkey=None):
        self.sem, self.val, self.key, self.dkey = sem, val, key, dkey


class Prog:
    EPOCH = 30000

    def __init__(self, nc):
        self.nc = nc
        self.eng = {"pe": nc.tensor, "act": nc.scalar, "dve": nc.vector, "pool": nc.gpsimd, "sp": nc.sync}
        self.sem = {}
        self.cnt = {}
        self.nsem = 0
        for e in self.eng:
            self._new_epoch(e)
        self.known = {e: {} for e in self.eng}
        self.lastw = {}
        self.readers = {}
        self.dsem = {}
        self.dcnt = {}
        self.n_ins = 0
        self.n_wait = 0
        self.filler = None
        self.in_fill = False

    def _new_epoch(self, e):
        self.nsem += 1
        self.sem[e] = self.nc.alloc_semaphore(name="s_%s_%d" % (e, self.nsem))
        self.cnt[e] = 0

    def _wait(self, e, ev):
        if e == "pe" and ev.key == id(self.sem["pe"]):
            return
        k = self.known[e]
        val = ev.val
        if ev.dkey is not None:
            val = max(val, self.dcnt[ev.dkey])
        if k.get(ev.key, 0) >= val:
            return
        self.eng[e].wait_ge(ev.sem, val)
        self.n_wait += 1
        k[ev.key] = val

    def _deps(self, e, reads, writes):
        for r in reads:
            ev = self.lastw.get(r)
            if ev is not None:
                self._wait(e, ev)
        for w in writes:
            ev = self.lastw.get(w)
            if ev is not None:
                self._wait(e, ev)
            for ev in self.readers.get(w, ()):
                self._wait(e, ev)

    def _commit(self, ev, reads, writes):
        for r in reads:
            self.readers.setdefault(r, []).append(ev)
        for w in writes:
            self.lastw[w] = ev
            self.readers[w] = []

    def fill(self, n=1):
        if self.filler is None or self.in_fill:
            return
        self.in_fill = True
        try:
            for _ in range(n):
                try:
                    next(self.filler)
                except StopIteration:
                    self.filler = None
                    break
        finally:
            self.in_fill = False

    def op(self, e, fn, reads=(), writes=()):
        if e == "pe":
            self.fill()
        self._deps(e, reads, writes)
        ins = fn(self.eng[e])
        if self.cnt[e] >= self.EPOCH:
            self._new_epoch(e)
        self.cnt[e] += 1
        s = self.sem[e]
        ins.then_inc(s, 1)
        ev = Ev(s, self.cnt[e], id(s))
        self._commit(ev, reads, writes)
        self.n_ins += 1
        return ev

    def group(self, e, fns, reads=(), writes=()):
        if e == "pe":
            self.fill()
        self._deps(e, reads, writes)
        ins = None
        for fn in fns:
            ins = fn(self.eng[e])
            self.n_ins += 1
        if self.cnt[e] >= self.EPOCH:
            self._new_epoch(e)
        self.cnt[e] += 1
        s = self.sem[e]
        ins.then_inc(s, 1)
        ev = Ev(s, self.cnt[e], id(s))
        self._commit(ev, reads, writes)
        return ev

    def dma(self, q, out, in_, reads=(), writes=(), semkey=None, **kw):
        if semkey is None:
            semkey = writes[0] if writes else reads[0]
        if semkey not in self.dsem:
            self.nsem += 1
            self.dsem[semkey] = self.nc.alloc_semaphore(name="d_%d" % self.nsem)
            self.dcnt[semkey] = 0
        self._deps(q, reads, writes)
        ins = self.eng[q].dma_start(out=out, in_=in_, **kw)
        s = self.dsem[semkey]
        self.dcnt[semkey] += 16
        ins.then_inc(s, 16)
        ev = Ev(s, self.dcnt[semkey], id(s), semkey)
        self._commit(ev, reads, writes)
        self.n_ins += 1
        return ev

    def coll(self, fn, reads=(), writes=()):
        if not hasattr(self, "csem"):
            self.nsem += 1
            self.csem = self.nc.alloc_semaphore(name="c_%d" % self.nsem)
            self.ccnt = 0
        self._deps("pool", reads, writes)
        ins = fn(self.eng["pool"])
        self.ccnt += 1
        ins.then_inc(self.csem, 1)
        ev = Ev(self.csem, self.ccnt, id(self.csem))
        self._commit(ev, reads, writes)
        self.n_ins += 1
        return ev

    def barrier(self):
        best = {}
        for ev in list(self.lastw.values()) + [ev for lst in self.readers.values() for ev in lst]:
            if ev.key not in best or best[ev.key].val < ev.val:
                best[ev.key] = ev
        for e in self.eng:
            for ev in best.values():
                self._wait(e, ev)

    def finish(self):
        evs = set()
        for ev in self.lastw.values():
            evs.add((ev.key, ev.val, ev))
        for lst in self.readers.values():
            for ev in lst:
                evs.add((ev.key, ev.val, ev))
        best = {}
        for key, val, ev in evs:
            if key not in best or best[key].val < val:
                best[key] = ev
        for ev in best.values():
            self._wait("sp", ev)


def _tables(cfg, core, pos0=0):
    T, TP = cfg.T, cfg.TP
    f32 = np.float32
    tb = {}
    tb["ident_bf"] = np.eye(128, dtype=f32).astype(ml_dtypes.bfloat16)
    tb["ident_f"] = np.eye(128, dtype=f32)
    tb["ones_bf"] = np.ones((128, 128), f32).astype(ml_dtypes.bfloat16)
    tb["ones_f"] = np.ones((128, 128), f32)
    half = 128
    inv_freq = (10000.0 ** (-np.arange(half, dtype=f32) / f32(half))).astype(f32)
    rot = np.zeros((cfg.NT, 2, 128, T), f32)
    for ti in range(cfg.NT):
        pos = np.concatenate([np.arange(TP, dtype=f32) + f32(ti * TP + pos0),
                              np.tile(f32(cfg.past_len) + np.arange(8, dtype=f32), cfg.NS)]).astype(f32)
        ang = (pos[None, :] * inv_freq[:, None]).astype(f32)
        rot[ti, 0] = np.cos(ang)
        rot[ti, 1] = np.sin(ang)
    tb["rot"] = rot
    idx = np.arange(128)
    grp = idx // 8
    causal_p = (idx[:, None] <= idx[None, :])
    causal_s = causal_p & (grp[:, None] == grp[None, :])
    lg = np.log1p(-np.exp2(-5.0 - np.arange(cfg.RH, dtype=np.float64)))
    dec = np.zeros((cfg.RH, 2, 128, 128), f32)
    qdec = np.zeros((128, cfg.RH, 2), f32)
    kdec = np.zeros((128, cfg.RH, 2), f32)
    diff = (idx[None, :] - idx[:, None]).astype(np.float64)
    for h in range(cfg.RH):
        dec[h, 0] = np.where(causal_p, np.exp(lg[h] * diff), 0.0) / 16.0
        dec[h, 1] = np.where(causal_s, np.exp(lg[h] * diff), 0.0) / 16.0
        qdec[:, h, 0] = np.exp(lg[h] * (idx + 1.0))
        qdec[:, h, 1] = np.exp(lg[h] * ((idx % 8) + 1.0))
        kdec[:, h, 0] = np.exp(lg[h] * (127.0 - idx)) / 16.0
        kdec[:, h, 1] = np.exp(lg[h] * (7.0 - (idx % 8))) / 16.0
    tb["ret_dec"] = dec
    tb["ret_qdec"] = qdec
    tb["ret_kdec"] = kdec
    cfg.ret_cd = [(float(np.exp(lg[h] * 128.0)), float(np.exp(lg[h] * 8.0))) for h in range(cfg.RH)]
    mk = np.zeros((4, 128, 128), f32)
    mk[0] = np.where(causal_p, 0.0, -30000.0)
    mk[1] = np.where(causal_s, 0.0, -30000.0)
    mk[2] = causal_p.astype(f32)
    mk[3] = causal_s.astype(f32)
    tb["masks"] = mk
    qm = np.zeros((128, 16, 128), f32)
    for b in range(16):
        qm[:, b, b * 8:(b + 1) * 8] = 1.0
    tb["qmask"] = qm.astype(ml_dtypes.bfloat16)
    rm = np.zeros((128, 16), f32)
    rm[idx, grp] = 1.0
    tb["rowmask"] = rm
    rst = np.ones((2, 128, T), f32)
    rst[0, :, TP::8] = 0.0
    rst[1, :, 0:TP:128] = 0.0
    rst[1, :, TP::8] = 0.0
    tb["rst"] = rst
    return tb


class Builder:
    def __init__(self, cfg):
        self.cfg = cfg
        nc = bass.Bass("TRN2", target_bir_lowering=False)
        self.nc = nc
        self.P = Prog(nc)
        self.din = {}
        self.dout = {}

    def inp(self, name, shape, dt=F32):
        t = self.nc.dram_tensor(name, list(shape), dt, kind="ExternalInput").ap()
        self.din[name] = t
        return t

    def outp(self, name, shape, dt=F32):
        t = self.nc.dram_tensor(name, list(shape), dt, kind="ExternalOutput").ap()
        self.dout[name] = t
        return t

    def scratch(self, name, shape, dt=F32):
        return self.nc.dram_tensor(name, list(shape), dt, kind="Internal").ap()

    def sb(self, es, name, shape, dt):
        self._uid = getattr(self, "_uid", 0) + 1
        return es.enter_context(self.nc.sbuf_tensor("%s_%d" % (name, self._uid), list(shape), dt))

    def build(self):
        cfg, nc, P = self.cfg, self.nc, self.P
        D, KC, T, NT = cfg.D, cfg.KC, cfg.T, cfg.NT
        RH, MH, HH = cfg.RH, cfg.MH, cfg.HH
        self.xT = self.inp("xT", [NT, KC, 128, T])
        self.s_ret = self.inp("s_ret", [NT, 16, RH, 256, 256])
        self.s_mc = self.inp("s_mc", [NT, 16, MH, 256, 256])
        self.s_mn = self.inp("s_mn", [NT, 16, MH, 256])
        self.s_mm = self.inp("s_mm", [NT, MH, 16])
        self.s_hg = self.inp("s_hg", [NT, 16, HH, 128, 128])
        self.w_in_ab = self.inp("w_in_ab", [D, cfg.ABC])
        self.w_out_ab = self.inp("w_out_ab", [D, D])
        self.w_in_c = self.inp("w_in_c", [D, cfg.CC])
        self.w_out_c = self.inp("w_out_c", [D, D])
        self.w_gate = self.inp("w_gate", [2, D, cfg.DFF])
        self.w_up = self.inp("w_up", [2, D, cfg.DFF])
        self.w_down = self.inp("w_down", [2, cfg.DFF, D])
        self.nw = self.inp("nw", [128, 5, KC])
        self.mlw = self.inp("mlw", [128, MH * 2])
        self.hgw = self.inp("hgw", [128, 1])
        self.bif = self.inp("bif", [128, 2 * MH])
        self.lbt = self.inp("lbt", [128, 2, HH])
        self.cmask = self.inp("cmask", [128, 1])
        self.snd = self.scratch("snd", [128, 520])
        self.rcv = self.scratch("rcv", [256, 520])
        self.snd_c = self.scratch("snd_c", [128, 128])
        self.rcv_c = self.scratch("rcv_c", [256, 128])
        tb = _tables(cfg, 0)
        self.tbl = {}
        for k, v in tb.items():
            dt = BF16 if v.dtype == ml_dtypes.bfloat16 else F32
            self.tbl[k] = self.inp("tb_" + k, v.shape, dt)
        self.yT = self.outp("yT", [NT, KC, 128, T])
        self.o_ret_p = self.outp("o_ret_p", [RH, 256, 256])
        self.o_mc_p = self.outp("o_mc_p", [MH, 256, 256])
        self.o_mn_p = self.outp("o_mn_p", [MH, 256])
        self.o_mm_p = self.outp("o_mm_p", [1, MH])
        self.o_hg_p = self.outp("o_hg_p", [HH, 128, 128])
        self.o_ret_s = self.outp("o_ret_s", [NT, 16, RH, 256, 256])
        self.o_mc_s = self.outp("o_mc_s", [NT, 16, MH, 256, 256])
        self.o_mn_s = self.outp("o_mn_s", [NT, 16, MH, 256])
        self.o_mm_s = self.outp("o_mm_s", [NT, MH, 16])
        self.o_hg_s = self.outp("o_hg_s", [NT, 16, HH, 128, 128])
        self.rT = self.scratch("rT", [KC, 128, T])
        self.mT = self.scratch("mT", [KC, 128, T], BF16)

        with ExitStack() as es:
            self.es = es
            self.pb = [es.enter_context(nc.psum_tensor("pb%d" % i, [128, 512], F32)) for i in range(7)]
            self.pt = [es.enter_context(nc.psum_tensor("pt%d" % i, [128, 1024], BF16)) for i in range(1)]
            c = {}
            for k in ("ident_bf", "ones_bf"):
                c[k] = self.sb(es, "c_" + k, [128, 128], BF16)
            for k in ("ident_f", "ones_f"):
                c[k] = self.sb(es, "c_" + k, [128, 128], F32)
            c["masks"] = self.sb(es, "c_masks", [128, 4, 128], F32)
            c["qmask"] = self.sb(es, "c_qmask", [128, 16, 128], BF16)
            c["rowmask"] = self.sb(es, "c_rowmask", [128, 16], F32)
            c["ret_qdec"] = self.sb(es, "c_qdec", [128, RH, 2], F32)
            c["ret_kdec"] = self.sb(es, "c_kdec", [128, RH, 2], F32)
            c["nw"] = self.sb(es, "c_nw", [128, 5, KC], F32)
            c["mlw"] = self.sb(es, "c_mlw", [128, MH * 2], F32)
            c["hgw"] = self.sb(es, "c_hgw", [128, 1], F32)
            c["bif"] = self.sb(es, "c_bif", [128, 2 * MH], F32)
            c["lbt"] = self.sb(es, "c_lbt", [128, 2, HH], F32)
            c["lb"] = self.sb(es, "c_lb", [128, HH], F32)
            c["oml"] = self.sb(es, "c_oml", [128, HH], F32)
            c["eps"] = self.sb(es, "c_eps", [128, 1], F32)
            c["one"] = self.sb(es, "c_one", [128, 1], F32)
            c["cmask"] = self.sb(es, "c_cmask", [128, 1], F32)
            self.c = c
            init_evs = []
            ld = lambda k, src: init_evs.append(P.dma("sp", c[k][:], src, reads=(), writes=(("c", k),), semkey="init"))
            ld("ident_bf", self.tbl["ident_bf"])
            ld("ones_bf", self.tbl["ones_bf"])
            ld("ident_f", self.tbl["ident_f"])
            ld("ones_f", self.tbl["ones_f"])
            ld("masks", self.tbl["masks"].rearrange("m p t -> p m t"))
            ld("qmask", self.tbl["qmask"])
            ld("rowmask", self.tbl["rowmask"])
            ld("ret_qdec", self.tbl["ret_qdec"])
            ld("ret_kdec", self.tbl["ret_kdec"])
            ld("nw", self.nw)
            ld("mlw", self.mlw)
            ld("hgw", self.hgw)
            ld("bif", self.bif)
            ld("lbt", self.lbt)
            ld("cmask", self.cmask)
            P.op("dve", lambda e: e.memset(c["eps"][:], EPS), writes=(("c", "eps"),))
            P.op("dve", lambda e: e.memset(c["one"][:], 1.0), writes=(("c", "one"),))
            self.lower_bounds()
            self.xn = self.sb(es, "xn", [128, KC, T], BF16)
            self.NW = 2
            self.wb = [self.sb(es, "wb%d" % i, [128, KC, 256], BF16) for i in range(self.NW)]
            self.wi = 0
            self.rci = 0
            for ti in range(NT):
                self.tile(ti)
            P.finish()
        return nc

    def lower_bounds(self):
        P, c = self.P, self.c
        HH = self.cfg.HH
        P.op("dve", lambda e: e.tensor_tensor(out=c["lb"][:], in0=c["lbt"][:, 1, :], in1=c["lbt"][:, 0, :],
                                              op=ALU.subtract), reads=(("c", "lbt"),), writes=(("c", "lb"),))
        P.op("act", lambda e: e.activation(out=c["lb"][:], in_=c["lb"][:], func=AF.Sigmoid),
             reads=(("c", "lb"),), writes=(("c", "lb"),))
        P.op("dve", lambda e: e.tensor_scalar(out=c["oml"][:], in0=c["lb"][:], scalar1=-1.0, scalar2=1.0,
                                              op0=ALU.mult, op1=ALU.add),
             reads=(("c", "lb"),), writes=(("c", "oml"),))

    def sreg(self, j, n):
        return self.pb[6][:, 0:n] if j == 0 else self.pb[3][:, 132:132 + n]

    def skey(self, j):
        return ("pb", 6) if j == 0 else ("pb", 3)

    def wslot(self):
        i = self.wi
        self.wi = (self.wi + 1) % self.NW
        return i

    def load_w(self, W, r0, nk, c0, ncols):
        i = self.wslot()
        src = W[r0:r0 + nk * 128, c0:c0 + ncols].rearrange("(k p) n -> p k n", p=128)
        self.P.dma("pool", self.wb[i][:, 0:nk, 0:ncols], src, reads=(("coll",),), writes=(("wb", i),))
        return i

    def dense_fm_gen(self, wi, nk, cb, act, act_res, banks, consumer, slice_k=4):
        P = self.P
        wb = self.wb[wi]
        k0 = 0
        while k0 < nk:
            k1 = min(nk, k0 + slice_k)
            fns = []
            for k in range(k0, k1):
                for j in range(3):
                    def fn(e, k=k, j=j):
                        return e.matmul(self.pb[banks[j]][:, 0:384], lhsT=wb[:, k, cb * 128:(cb + 1) * 128],
                                        rhs=act(k)[:, j * 384:(j + 1) * 384], start=(k == 0), stop=(k == nk - 1))
                    fns.append(fn)
            P.group("pe", fns, reads=(("wb", wi),) + tuple(act_res), writes=tuple(("pb", b) for b in banks))
            k0 = k1
            if k0 < nk:
                yield
        consumer([self.pb[b][:, 0:384] for b in banks], [("pb", b) for b in banks])
        yield

    def dense_fm(self, wi, nk, cb, act, act_res, banks, consumer):
        save = self.P.in_fill
        self.P.in_fill = True
        try:
            for _ in self.dense_fm_gen(wi, nk, cb, act, act_res, banks, consumer, slice_k=nk):
                pass
        finally:
            self.P.in_fill = save

    def norm(self, src_dram, widx, out_fp32_dram=None):
        cfg, P, c = self.cfg, self.P, self.c
        KC, T = cfg.KC, cfg.T
        banks = (3, 4, 5)
        sqb = self.sq
        for k in range(KC):
            r = self.rci
            self.rci ^= 1
            P.dma("sp", self.rc[r][:], src_dram[k], reads=(("dram", "r", k),), writes=(("rc", r),))
            P.op("act", lambda e, r=r, k=k: e.activation(out=sqb[k % 2][:], in_=self.rc[r][:], func=AF.Square),
                 reads=(("rc", r),), writes=(("sq", k % 2),))
            fns = []
            for j in range(3):
                fns.append(lambda e, j=j, k=k: e.matmul(self.pb[banks[j]][:, 0:384], lhsT=c["ones_bf"][:],
                                                        rhs=sqb[k % 2][:, j * 384:(j + 1) * 384],
                                                        start=(k == 0), stop=(k == KC - 1)))
            P.group("pe", fns, reads=(("sq", k % 2), ("c", "ones_bf")), writes=tuple(("pb", b) for b in banks))
        for j in range(3):
            sl = slice(j * 384, (j + 1) * 384)
            P.op("act", lambda e, j=j, sl=sl: e.activation(out=self.rstd[:, sl], in_=self.pb[banks[j]][:, 0:384],
                                                          func=AF.Sqrt, scale=1.0 / cfg.D, bias=c["eps"][:]),
                 reads=(("pb", banks[j]), ("c", "eps")), writes=(("rstd",),))
        P.op("dve", lambda e: e.reciprocal(out=self.rstd[:], in_=self.rstd[:]), reads=(("rstd",),),
             writes=(("rstd",),))
        for k in range(KC):
            r = self.rci
            self.rci ^= 1
            P.dma("sp", self.rc[r][:], src_dram[k], reads=(("dram", "r", k),), writes=(("rc", r),))
            if out_fp32_dram is None:
                P.op("dve", lambda e, r=r, k=k: e.scalar_tensor_tensor(
                    out=self.xn[:, k, :], in0=self.rc[r][:], scalar=c["nw"][:, widx, k:k + 1], in1=self.rstd[:],
                    op0=ALU.mult, op1=ALU.mult), reads=(("rc", r), ("rstd",), ("c", "nw")), writes=(("xn", k),))
            else:
                P.op("dve", lambda e, r=r, k=k: e.scalar_tensor_tensor(
                    out=self.rc[r][:], in0=self.rc[r][:], scalar=c["nw"][:, widx, k:k + 1], in1=self.rstd[:],
                    op0=ALU.mult, op1=ALU.mult), reads=(("rc", r), ("rstd",), ("c", "nw")), writes=(("rc", r),))
                P.dma("sp", out_fp32_dram[k], self.rc[r][:], reads=(("rc", r),), writes=(("dram", "y", k),),
                      semkey=("rc", r))

    def add_residual(self, k, src_dram, ps, ps_res):
        P = self.P
        r = self.rci
        self.rci ^= 1
        P.dma("sp", self.rc[r][:], src_dram[k], reads=(("dram", "r", k),), writes=(("rc", r),))
        for j in range(3):
            sl = slice(j * 384, (j + 1) * 384)
            P.op("dve", lambda e, j=j, sl=sl, r=r: e.tensor_tensor(out=self.rc[r][:, sl], in0=ps[j],
                                                                   in1=self.rc[r][:, sl], op=ALU.add),
                 reads=(ps_res[j], ("rc", r)), writes=(("rc", r),))
        P.dma("sp", self.rT[k], self.rc[r][:], reads=(("rc", r),), writes=(("dram", "r", k),), semkey=("rc", r))

    def out_proj(self, W, src_dram):
        cfg, P = self.cfg, self.P
        KC = cfg.KC
        for k in range(KC):
            P.dma("sp", self.xn[:, k, :], self.mT[k], reads=(("dram", "m", k),), writes=(("xn", k),),
                  semkey=("xnload", k % 4))
        xn_res = tuple(("xn", k) for k in range(KC))
        for cb2 in range(KC // 2):
            wi = self.load_w(W, 0, KC, cb2 * 256, 256)
            for s in range(2):
                kk = cb2 * 2 + s
                self.dense_fm(wi, KC, s, lambda k: self.xn[:, k, :], xn_res, (0, 1, 2),
                              lambda ps, pr, kk=kk: self.add_residual(kk, src_dram, ps, pr))

    def ffn(self, layer):
        cfg, P = self.cfg, self.P
        KC, T = cfg.KC, cfg.T
        xn_res = tuple(("xn", k) for k in range(KC))
        Wg, Wu, Wd = self.w_gate[layer], self.w_up[layer], self.w_down[layer]
        with ExitStack() as es:
            nfmax = max(n for _, n in cfg.fq)
            hT = self.sb(es, "hT", [128, nfmax, T], BF16)
            sil = [self.sb(es, "sil%d" % i, [128, 384], F32) for i in range(2)]
            cnt = [0]
            for (f0, nf) in cfg.fq:
                fb = 0
                while fb < nf:
                    nb = min(2, nf - fb)
                    wg = self.load_w(Wg, 0, KC, (f0 + fb) * 128, nb * 128)
                    wu = self.load_w(Wu, 0, KC, (f0 + fb) * 128, nb * 128)
                    for s in range(nb):
                        fl = fb + s
                        store = {}

                        def cons_g(ps, pr, store=store):
                            store["g"] = (ps, pr)

                        self.dense_fm(wg, KC, s, lambda k: self.xn[:, k, :], xn_res, (0, 1, 2), cons_g)

                        def cons_u(ps, pr, store=store, fl=fl):
                            gps, gpr = store["g"]
                            for j in range(3):
                                sl = slice(j * 384, (j + 1) * 384)
                                si = cnt[0] % 2
                                cnt[0] += 1
                                P.op("act", lambda e, j=j, si=si: e.activation(out=sil[si][:], in_=gps[j], func=AF.Silu),
                                     reads=(gpr[j],), writes=(("sil", si),))
                                P.op("dve", lambda e, j=j, si=si, sl=sl: e.tensor_tensor(
                                    out=hT[:, fl, sl], in0=ps[j], in1=sil[si][:], op=ALU.mult),
                                    reads=(pr[j], ("sil", si)), writes=(("hT", fl),))

                        self.dense_fm(wu, KC, s, lambda k: self.xn[:, k, :], xn_res, (3, 4, 5), cons_u)
                    fb += nb
                h_res = tuple(("hT", f) for f in range(nf))
                for cb2 in range(KC // 2):
                    wd = self.load_w(Wd, f0 * 128, nf, cb2 * 256, 256)
                    for s in range(2):
                        kk = cb2 * 2 + s
                        self.dense_fm(wd, nf, s, lambda k: hT[:, k, :], h_res, (0, 1, 2),
                                      lambda ps, pr, kk=kk: self.add_residual(kk, self.rT, ps, pr))
            P.barrier()

    def nbufs(self):
        es = ExitStack()
        T = self.cfg.T
        self.rstd = self.sb(es, "rstd", [128, T], F32)
        self.rc = [self.sb(es, "rc%d" % i, [128, T], F32) for i in range(2)]
        self.sq = [self.sb(es, "sq%d" % i, [128, T], BF16) for i in range(2)]
        return es

    def tile(self, ti):
        cfg, P = self.cfg, self.P
        self.ti = ti
        with self.nbufs():
            self.norm(self.xT[ti], 0)
            P.barrier()
        from_x = self.xT[ti]
        self.mixer_ab()
        with self.nbufs():
            self.out_proj(self.w_out_ab, from_x)
            self.norm(self.rT, 1)
            self.ffn(0)
            self.norm(self.rT, 2)
            P.barrier()
        self.mixer_c()
        with self.nbufs():
            self.out_proj(self.w_out_c, self.rT)
            self.norm(self.rT, 3)
            self.ffn(1)
            self.norm(self.rT, 4, out_fp32_dram=self.yT[ti])
            P.barrier()

    def mixer_ab(self):
        mixer_ab(self)

    def mixer_c(self):
        mixer_c(self)


def _zero_mT(b):
    P, cfg = b.P, b.cfg
    with ExitStack() as es:
        z = b.sb(es, "zst", [128, cfg.T], BF16)
        P.op("dve", lambda e: e.memset(z[:], 0.0), writes=(("zst",),))
        for k in range(cfg.KC):
            P.dma("sp", b.mT[k], z[:], reads=(("zst",),), writes=(("dram", "m", k),), semkey=("zst",))
        P.barrier()


STUB_AB = False
STUB_C = False


def mixer_ab(b):
    if STUB_AB:
        return _zero_mT(b)
    return mixer_ab_real(b)


def mixer_c(b):
    if STUB_C:
        return _zero_mT(b)
    return mixer_c_real(b)


def core_maps(cfg):
    pm, sm = [], []
    if cfg.xch:
        for c in range(cfg.ncores):
            pm.append((c // 2, c % 2))
            sm.append([c * 16])
        return pm, sm
    if cfg.ncores == 8:
        for c in range(8):
            pm.append(c if c < 4 else None)
            sm.append([None] * cfg.NT if c < 4 else [((c - 4) * cfg.NT + ti) * 16 for ti in range(cfg.NT)])
    else:
        pm.append(0)
        sm.append([ti * 16 for ti in range(cfg.NT)])
    return pm, sm


def make_in_maps(cfg, inp):
    T, TP, KC, NT = cfg.T, cfg.TP, cfg.KC, cfg.NT
    f32 = np.float32
    pm, sm = core_maps(cfg)
    tb = _tables(cfg, 0)
    shared = {}
    shared["w_in_ab"] = np.ascontiguousarray(inp["w_in_ab"][0])
    shared["w_out_ab"] = np.ascontiguousarray(inp["w_out_ab"][0])
    shared["w_in_c"] = np.ascontiguousarray(inp["w_in_c"][0])
    shared["w_out_c"] = np.ascontiguousarray(inp["w_out_c"][0])
    shared["w_gate"] = np.ascontiguousarray(inp["w_gate"])
    shared["w_up"] = np.ascontiguousarray(inp["w_up"])
    shared["w_down"] = np.ascontiguousarray(inp["w_down"])
    nws = np.stack([inp["norm_mix_w"][0], inp["norm_ffn_w"][0], inp["norm_mix_w"][1], inp["norm_ffn_w"][1],
                    inp["norm_final_w"]], 0)
    shared["nw"] = np.ascontiguousarray(nws.reshape(5, KC, 128).transpose(2, 0, 1))
    shared["mlw"] = np.ascontiguousarray(inp["ml_norm_w"][0].reshape(cfg.MH * 2, 128).T)
    shared["hgw"] = np.ascontiguousarray(inp["hg_norm_w"][0].reshape(128, 1))
    shared["bif"] = np.ascontiguousarray(np.broadcast_to(inp["b_if_ab"][0][None, :], (128, 2 * cfg.MH)))
    shared["lbt"] = np.ascontiguousarray(inp["lb_logits"].reshape(2, cfg.HH, 128).transpose(2, 0, 1))
    for k, v in tb.items():
        shared["tb_" + k] = v
    maps = []
    for c in range(cfg.ncores):
        m = dict(shared)
        if cfg.xch:
            m["tb_rot"] = _tables(cfg, c, pos0=(c % 2) * TP)["rot"]
            m["cmask"] = np.full((128, 1), float(c % 2), f32)
        else:
            m["cmask"] = np.zeros((128, 1), f32)
        x = np.zeros((NT, T, cfg.D), f32)
        s_ret = np.zeros((NT, 16, cfg.RH, 256, 256), f32)
        s_mc = np.zeros((NT, 16, cfg.MH, 256, 256), f32)
        s_mn = np.zeros((NT, 16, cfg.MH, 256), f32)
        s_mm = np.zeros((NT, cfg.MH, 16), f32)
        s_hg = np.zeros((NT, 16, cfg.HH, 128, 128), f32)
        for ti in range(NT):
            if cfg.xch:
                sq_, hf_ = pm[c]
                x[ti, :TP] = inp["x_prompt"][sq_, hf_ * TP:(hf_ + 1) * TP]
            elif pm[c] is not None:
                x[ti, :TP] = inp["x_prompt"][pm[c], ti * TP:(ti + 1) * TP]
            if sm[c][ti] is not None:
                b0 = sm[c][ti]
                x[ti, TP:] = inp["x_sample"][b0:b0 + 16].reshape(128, cfg.D)
                s_ret[ti] = inp["state_ret"][0, b0:b0 + 16]
                s_mc[ti] = inp["state_mlstm_C"][0, b0:b0 + 16]
                s_mn[ti] = inp["state_mlstm_n"][0, b0:b0 + 16]
                s_mm[ti] = inp["state_mlstm_m"][0, b0:b0 + 16].T
                s_hg[ti] = inp["state_hgrn"][0, b0:b0 + 16]
        m["xT"] = np.ascontiguousarray(x.transpose(0, 2, 1).reshape(NT, KC, 128, T))
        m["s_ret"], m["s_mc"], m["s_mn"], m["s_mm"], m["s_hg"] = s_ret, s_mc, s_mn, s_mm, s_hg
        maps.append(m)
    return maps


def assemble(cfg, res):
    T, TP, KC, NT, D = cfg.T, cfg.TP, cfg.KC, cfg.NT, cfg.D
    f32 = np.float32
    pm, sm = core_maps(cfg)
    B, DB = cfg.batch, cfg.dec_batch
    y_p = np.zeros((B, cfg.seq, D), f32)
    y_s = np.zeros((DB, 8, D), f32)
    ret_p = np.zeros((1, B, cfg.RH, 256, 256), f32)
    mc_p = np.zeros((1, B, cfg.MH, 256, 256), f32)
    mn_p = np.zeros((1, B, cfg.MH, 256), f32)
    mm_p = np.zeros((1, B, cfg.MH), f32)
    hg_p = np.zeros((1, B, cfg.HH, 128, 128), f32)
    ret_s = np.zeros((1, DB, cfg.RH, 256, 256), f32)
    mc_s = np.zeros((1, DB, cfg.MH, 256, 256), f32)
    mn_s = np.zeros((1, DB, cfg.MH, 256), f32)
    mm_s = np.zeros((1, DB, cfg.MH), f32)
    hg_s = np.zeros((1, DB, cfg.HH, 128, 128), f32)
    for c in range(cfg.ncores):
        r = res[c]
        y = np.asarray(r["yT"]).reshape(NT, D, T).transpose(0, 2, 1)
        for ti in range(NT):
            if cfg.xch:
                sq_, hf_ = pm[c]
                y_p[sq_, hf_ * TP:(hf_ + 1) * TP] = y[ti, :TP]
            elif pm[c] is not None:
                y_p[pm[c], ti * TP:(ti + 1) * TP] = y[ti, :TP]
            if sm[c][ti] is not None:
                b0 = sm[c][ti]
                y_s[b0:b0 + 16] = y[ti, TP:].reshape(16, 8, D)
                ret_s[0, b0:b0 + 16] = r["o_ret_s"][ti]
                mc_s[0, b0:b0 + 16] = r["o_mc_s"][ti]
                mn_s[0, b0:b0 + 16] = r["o_mn_s"][ti]
                mm_s[0, b0:b0 + 16] = np.asarray(r["o_mm_s"][ti]).T
                hg_s[0, b0:b0 + 16] = r["o_hg_s"][ti]
        psel = None
        if cfg.xch:
            if pm[c][1] == 1:
                psel = pm[c][0]
        elif pm[c] is not None:
            psel = pm[c]
        if psel is not None:
            ret_p[0, psel] = r["o_ret_p"]
            mc_p[0, psel] = r["o_mc_p"]
            mn_p[0, psel] = r["o_mn_p"]
            mm_p[0, psel] = np.asarray(r["o_mm_p"]).reshape(cfg.MH)
            hg_p[0, psel] = r["o_hg_p"]
    return (y_p, y_s, ret_p, mc_p, mn_p, mm_p, hg_p, ret_s, mc_s, mn_s, mm_s, hg_s)


def run_cfg(cfg, inp):
    b = Builder(cfg)
    nc = b.build()
    maps = make_in_maps(cfg, inp)
    res = run_bass_kernel_spmd(nc, maps, core_ids=list(range(cfg.ncores)))
    return assemble(cfg, res.results)


def kernel(**inputs):
    cfg = Cfg()
    inp = {k: np.asarray(v) for k, v in inputs.items()}
    return run_cfg(cfg, inp)


class _AB:
    pass


def _xchg(b, src_ap, snd, rcv, res):
    P, c, cfg = b.P, b.c, b.cfg
    key = ("dram", id(snd))
    P.dma("sp", snd, src_ap, reads=(res,), writes=(key + ("s",),), semkey=("xs",))
    P.coll(lambda g: g.collective_compute("AllGather", ALU.bypass, replica_groups=cfg.groups, ins=[snd], outs=[rcv]),
           reads=(key + ("s",),) + tuple(("wb", i) for i in range(len(b.wb))), writes=(key + ("r",), ("coll",)))
    P.dma("sp", src_ap, rcv[0:128, :], reads=(key + ("r",),), writes=(res,), semkey=("xr",))
    P.op("dve", lambda e: e.tensor_scalar(out=src_ap, in0=src_ap, scalar1=c["cmask"][:, 0:1], scalar2=None, op0=ALU.mult),
         reads=(res, ("c", "cmask")), writes=(res,))


def _proj_fm_gen(b, W, col0, evac, bankset):
    cfg = b.cfg
    KC = cfg.KC
    xn_res = tuple(("xn", k) for k in range(KC))
    wi = b.load_w(W, 0, KC, col0, 256)
    for s in range(2):
        banks = bankset[s % len(bankset)]

        def cons(ps, pr, s=s):
            for j in range(3):
                evac(s, j, ps[j], pr[j])

        yield from b.dense_fm_gen(wi, KC, s, lambda k: b.xn[:, k, :], xn_res, banks, cons)


def _proj_tm_gen(b, W, col0, ncols, dst, dst_res, dcol0=0):
    cfg, P = b.cfg, b.P
    KC = cfg.KC
    xn_res = tuple(("xn", k) for k in range(KC))
    wi = b.load_w(W, 0, KC, col0, ncols)
    for tb in range(9):
        bank = tb % 2
        fns = []
        for k in range(KC):
            fns.append(lambda e, k=k, tb=tb, bank=bank: e.matmul(
                b.pb[bank][:, 0:ncols], lhsT=b.xn[:, k, tb * 128:(tb + 1) * 128], rhs=b.wb[wi][:, k, 0:ncols],
                start=(k == 0), stop=(k == KC - 1)))
        P.group("pe", fns, reads=(("wb", wi),) + xn_res, writes=(("pb", bank),))
        P.op("act", lambda e, tb=tb, bank=bank: e.copy(out=dst[:, tb, dcol0:dcol0 + ncols], in_=b.pb[bank][:, 0:ncols]),
             reads=(("pb", bank),), writes=(dst_res,))
        yield


def _evac_to(b, dst, dres, func=None):
    P = b.P

    def ev(s, j, ps, pr):
        if func is None:
            P.op("act", lambda e: e.copy(out=dst[:, s, j * 384:(j + 1) * 384], in_=ps), reads=(pr,), writes=(dres,))
        else:
            P.op("act", lambda e: e.activation(out=dst[:, s, j * 384:(j + 1) * 384], in_=ps, func=func),
                 reads=(pr,), writes=(dres,))
    return ev


def _pipeline(b, n, proj_gen, rec):
    P = b.P
    _drain(b, proj_gen(0))
    for i in range(n):
        nxt = proj_gen(i + 1) if i + 1 < n else None
        P.filler = nxt
        rec(i)
        P.filler = None
        _drain(b, nxt)


def _drain(b, gen):
    if gen is None:
        return
    P = b.P
    save = P.in_fill
    P.in_fill = True
    try:
        for _ in gen:
            pass
    finally:
        P.in_fill = save


class _View:
    DB = ("qT", "kT", "gT", "v", "Q32", "F32")

    def __init__(self, base, par):
        self.__dict__["_base"] = base
        self.__dict__["_par"] = par

    def __getattr__(self, n):
        base, par = self.__dict__["_base"], self.__dict__["_par"]
        if n in _View.DB:
            return getattr(base, n + "_db")[par]
        if n.startswith("k_") and n[2:] in _View.DB:
            return (n[2:], par)
        return getattr(base, n)

    def __setattr__(self, n, v):
        setattr(self.__dict__["_base"], n, v)


def _transpose_pair(b, src_fn, src_res, pt_i, n=2):
    P, c = b.P, b.c
    fns = []
    for j in range(n):
        fns.append(lambda e, j=j: e.transpose(out=b.pt[0][:, pt_i * 256 + j * 128:pt_i * 256 + (j + 1) * 128], in_=src_fn(j),
                                              identity=c["ident_bf"][:]))
    P.group("pe", fns, reads=tuple(src_res) + (("c", "ident_bf"),), writes=(("pt", 0),))


def mixer_ab_real(b):
    cfg, P, c = b.cfg, b.P, b.c
    P.barrier()
    KC, T, TP = cfg.KC, cfg.T, cfg.TP
    RW, MW, RH, MH = cfg.RW, cfg.MW, cfg.RH, cfg.MH
    W = b.w_in_ab
    ti = b.ti
    with ExitStack() as es:
        A = _AB()
        A.raw = b.sb(es, "raw", [128, 2, T], F32)
        A.qT_db = [b.sb(es, "qT", [128, 2, T], BF16) for _ in range(2)]
        A.kT_db = [b.sb(es, "kT", [128, 2, T], BF16) for _ in range(2)]
        A.gT_db = [b.sb(es, "gT", [128, 2, T], BF16) for _ in range(2)]
        A.v_db = [b.sb(es, "v", [128, 9, 260], BF16) for _ in range(2)]
        A.rot = b.sb(es, "rot", [128, 2, T], F32)
        A.t1 = b.sb(es, "t1", [128, T], F32)
        A.t2 = b.sb(es, "t2", [128, T], F32)
        A.mst = b.sb(es, "mst", [128, 2, T], BF16)
        A.S = b.sb(es, "S", [128, 2, 260], F32)
        A.Sbf = b.sb(es, "Sbf", [128, 2, 260], BF16)
        A.Sst = b.sb(es, "Sst", [128, 2, 2, 260], F32)
        A.Sout = b.sb(es, "Sout", [128, 2, 2, 260], F32)
        A.Sstb = b.sb(es, "Sstb", [128, 2, 2, 260], BF16)
        A.qmx = b.sb(es, "qmx", [128, 2, 2, 128], BF16)
        A.kdm = b.sb(es, "kdm", [128, 2, 256], BF16)
        A.dec = b.sb(es, "dec", [128, 2, 128], F32)
        A.attm = b.sb(es, "attm", [128, 128], BF16)
        A.osb = b.sb(es, "osb", [128, 260], F32)
        A.otot_db = [b.sb(es, "otot", [128, 260], F32) for _ in range(2)]
        A.junk = A.osb
        A.on = b.sb(es, "on", [128, 256], BF16)
        A.kd = b.sb(es, "kd", [128, 256], BF16)
        A.sm = b.sb(es, "sm", [128, 16], F32)
        A.cols = b.sb(es, "cols", [128, 4], F32)
        A.D = b.sb(es, "Dm", [128, 128], F32)
        A.nrow = b.sb(es, "nrow", [16, 256], F32)
        A.ncol = b.sb(es, "ncol", [128, 2, 16], F32)
        A.mrow = b.sb(es, "mrow", [128, 16], F32)
        A.qi = b.sb(es, "qi", [128, 2, 128], BF16)
        A.wrep = b.wb[1][:, 0:KC, 128:256]
        A.mend = b.sb(es, "mend", [128, 16], F32)
        for par in range(2):
            P.op("dve", lambda e, par=par: e.memset(A.v_db[par][:, :, 256:257], 1.0), writes=(("v1",),))
        P.op("dve", lambda e: e.memset(A.Sst[:], 0.0), writes=(("Sst",),))
        P.op("dve", lambda e: e.memset(A.S[:], 0.0), writes=(("S",),))
        P.dma("sp", A.rot[:], b.tbl["rot"][ti].rearrange("a p t -> p a t"), writes=(("rot",),))
        views = [_View(A, 0), _View(A, 1)]

        def proj_gen(i):
            if i < RH:
                return _ret_proj_gen(b, views[i % 2], i)
            return _ml_proj_gen(b, views[i % 2], i - RH)

        def rec(i):
            if i < RH:
                _ret_rec(b, views[i % 2], i)
            else:
                _ml_rec(b, views[i % 2], i - RH)

        _pipeline(b, RH + MH, proj_gen, rec)
        b.NW = 2
        b.wi = 0
        P.barrier()


def _store_mixed(b, A, hidx):
    P = b.P
    for j in range(2):
        P.dma("sp", b.mT[hidx * 2 + j], A.mst[:, j, :], reads=(("mst",),), writes=(("dram", "m", hidx * 2 + j),),
              semkey=("mst",))


def _ret_proj_gen(b, A, h):
    cfg, P, c = b.cfg, b.P, b.c
    T, TP, RW = cfg.T, cfg.TP, cfg.RW
    W = b.w_in_ab
    ti = b.ti
    sets = [(0, 1, 2)]

    def evac_raw(s, j, ps, pr):
        P.op("act", lambda e: e.copy(out=A.raw[:, s, j * 384:(j + 1) * 384], in_=ps), reads=(pr,), writes=(("raw",),))

    def rotary(dst, dres):
        x1, x2 = A.raw[:, 0, :], A.raw[:, 1, :]
        cs, sn = A.rot[:, 0, :], A.rot[:, 1, :]
        rr = (("raw",), ("rot",))
        P.op("dve", lambda e: e.tensor_tensor(out=A.t1[:], in0=x1, in1=cs, op=ALU.mult), reads=rr, writes=(("t1",),))
        P.op("dve", lambda e: e.tensor_tensor(out=A.t2[:], in0=x2, in1=sn, op=ALU.mult), reads=rr, writes=(("t2",),))
        P.op("dve", lambda e: e.tensor_tensor(out=dst[:, 0, :], in0=A.t1[:], in1=A.t2[:], op=ALU.subtract),
             reads=(("t1",), ("t2",)), writes=(dres,))
        P.op("dve", lambda e: e.tensor_tensor(out=A.t1[:], in0=x1, in1=sn, op=ALU.mult), reads=rr, writes=(("t1",),))
        P.op("dve", lambda e: e.tensor_tensor(out=A.t2[:], in0=x2, in1=cs, op=ALU.mult), reads=rr, writes=(("t2",),))
        P.op("dve", lambda e: e.tensor_tensor(out=dst[:, 1, :], in0=A.t1[:], in1=A.t2[:], op=ALU.add),
             reads=(("t1",), ("t2",)), writes=(dres,))

    yield from _proj_fm_gen(b, W, 0 * RW + h * 256, evac_raw, sets)
    rotary(A.qT, A.k_qT)
    yield from _proj_fm_gen(b, W, 1 * RW + h * 256, evac_raw, sets)
    rotary(A.kT, A.k_kT)
    yield from _proj_tm_gen(b, W, 2 * RW + h * 256, 256, A.v, A.k_v)

    def evac_g(s, j, ps, pr):
        P.op("act", lambda e: e.activation(out=A.gT[:, s, j * 384:(j + 1) * 384], in_=ps, func=AF.Silu),
             reads=(pr,), writes=(A.k_gT,))

    yield from _proj_fm_gen(b, W, 3 * RW + h * 256, evac_g, sets)


def _ret_rec(b, A, h):
    cfg, P, c = b.cfg, b.P, b.c
    T, TP, RW = cfg.T, cfg.TP, cfg.RW
    ti = b.ti
    P.dma("sp", A.dec[:], b.tbl["ret_dec"][h].rearrange("a p t -> p a t"), writes=(("dec",),))
    cdp, cds = cfg.ret_cd[h]
    if ti == 0:
        P.op("dve", lambda e: e.memset(A.S[:], 0.0), writes=(("S",),))
    else:
        P.dma("sp", A.S[:, :, 0:256], b.o_ret_p[h].rearrange("(j p) e -> p j e", p=128),
              reads=(("dram", "retp", h),), writes=(("S",),))
    if cfg.xch:
        for cidx in range(8):
            tok = slice(cidx * 128, (cidx + 1) * 128)
            _transpose_pair(b, lambda j: A.kT[:, j, tok], (A.k_kT,), 1)
            P.op("dve", lambda e: e.tensor_scalar(out=A.kd[:], in0=b.pt[0][:, 256:512], scalar1=c["ret_kdec"][:, h, 0:1],
                                                  scalar2=None, op0=ALU.mult),
                 reads=(("pt", 0), ("c", "ret_kdec")), writes=(("kd",),))
            for j in range(2):
                bank = j
                P.op("pe", lambda e, j=j, bank=bank: e.matmul(b.sreg(bank, 256), lhsT=A.kd[:, j * 128:(j + 1) * 128],
                                                              rhs=A.v[:, cidx, 0:256], start=True, stop=True),
                     reads=(("kd",), A.k_v), writes=(b.skey(bank),))
                P.op("dve", lambda e, j=j, bank=bank: e.scalar_tensor_tensor(
                    out=A.S[:, j, 0:256], in0=A.S[:, j, 0:256], scalar=cdp, in1=b.sreg(bank, 256),
                    op0=ALU.mult, op1=ALU.add), reads=(("S",), b.skey(bank)), writes=(("S",),))
        _xchg(b, A.S[:].rearrange("p j e -> p (j e)"), b.snd[:, :], b.rcv[:, :], ("S",))
    P.op("act", lambda e: e.copy(out=A.Sbf[:], in_=A.S[:]), reads=(("S",),), writes=(("Sbf",),))
    pending = None
    for pos, cidx in enumerate([8, 0, 1, 2, 3, 4, 5, 6, 7]):
        smp = 1 if cidx == 8 else 0
        tok = slice(cidx * 128, (cidx + 1) * 128)
        otot, kot = A.otot_db[pos % 2], ("otot", pos % 2)
        P.group("pe", [lambda e, j=j: e.matmul(b.pb[3][:, 0:128], lhsT=A.kT[:, j, tok], rhs=A.qT[:, j, tok],
                                               start=(j == 0), stop=(j == 1)) for j in range(2)],
                reads=(A.k_kT, A.k_qT), writes=(("pb", 3),))
        P.op("dve", lambda e: e.tensor_tensor(out=A.attm[:], in0=b.pb[3][:, 0:128], in1=A.dec[:, smp, :], op=ALU.mult),
             reads=(("pb", 3), ("dec",)), writes=(("attm",),))
        P.op("pe", lambda e: e.matmul(b.pb[4][:, 0:256], lhsT=A.attm[:], rhs=A.v[:, cidx, 0:256], start=True, stop=True),
             reads=(("attm",), A.k_v), writes=(("pb", 4),))
        P.op("act", lambda e: e.copy(out=A.osb[:, 0:256], in_=b.pb[4][:, 0:256]), reads=(("pb", 4),), writes=(("osb",),))
        _transpose_pair(b, lambda j: A.kT[:, j, tok], (A.k_kT,), 1)
        P.op("dve", lambda e: e.tensor_scalar(out=A.kd[:], in0=b.pt[0][:, 256:512], scalar1=c["ret_kdec"][:, h, smp:smp + 1],
                                              scalar2=None, op0=ALU.mult),
             reads=(("pt", 0), ("c", "ret_kdec")), writes=(("kd",),))
        if not smp:
            P.group("pe", [lambda e, j=j: e.matmul(b.pb[5][:, 0:256], lhsT=A.qT[:, j, tok], rhs=A.Sbf[:, j, 0:256],
                                                   start=(j == 0), stop=(j == 1)) for j in range(2)],
                    reads=(A.k_qT, ("Sbf",)), writes=(("pb", 5),))
            for j in range(2):
                bank = j
                P.op("pe", lambda e, j=j, bank=bank: e.matmul(b.sreg(bank, 256), lhsT=A.kd[:, j * 128:(j + 1) * 128],
                                                              rhs=A.v[:, cidx, 0:256], start=True, stop=True),
                     reads=(("kd",), A.k_v), writes=(b.skey(bank),))
                P.op("dve", lambda e, j=j, bank=bank: e.scalar_tensor_tensor(
                    out=A.S[:, j, 0:256], in0=A.S[:, j, 0:256], scalar=cdp, in1=b.sreg(bank, 256),
                    op0=ALU.mult, op1=ALU.add), reads=(("S",), b.skey(bank)), writes=(("S",),))
            P.op("act", lambda e: e.copy(out=A.Sbf[:], in_=A.S[:]), reads=(("S",),), writes=(("Sbf",),))
        else:
            _sample_states(b, A, tok, b.s_ret[ti, :, h], b.o_ret_s[ti, :, h], 256, cds, None)
        P.op("dve", lambda e: e.scalar_tensor_tensor(out=otot[:, 0:256], in0=b.pb[5][:, 0:256],
                                                     scalar=c["ret_qdec"][:, h, smp:smp + 1], in1=A.osb[:, 0:256],
                                                     op0=ALU.mult, op1=ALU.add),
             reads=(("pb", 5), ("osb",), ("c", "ret_qdec")), writes=(kot,))
        if pending is not None:
            pending()
        pending = (lambda tok=tok, otot=otot, kot=kot: _norm_gate_store(b, A, tok, None, None, otot, kot))
    pending()
    if True:
        for j in range(2):
            P.dma("sp", b.o_ret_p[h, j * 128:(j + 1) * 128, :], A.S[:, j, 0:256], reads=(("S",),),
                  writes=(("dram", "retp", h),), semkey=("Sout",))
    _store_mixed(b, A, h)


def _norm_gate_store(b, A, tok, rec_col, wcol_fn, otot, kot):
    P, c = b.P, b.c
    P.op("act", lambda e: e.activation(out=A.junk[:, 0:256], in_=otot[:, 0:256], func=AF.Square,
                                       accum_out=A.sm[:, 0:1]),
         reads=(kot,), writes=(("osb",), ("sm", 0)))
    if rec_col is None:
        P.op("act", lambda e: e.activation(out=A.sm[:, 1:2], in_=A.sm[:, 0:1], func=AF.Sqrt, scale=1.0 / 256.0,
                                           bias=c["eps"][:]), reads=(("sm", 0), ("c", "eps")), writes=(("sm", 1),))
        P.op("dve", lambda e: e.reciprocal(out=A.sm[:, 2:3], in_=A.sm[:, 1:2]), reads=(("sm", 1),), writes=(("sm", 2),))
    else:
        P.op("dve", lambda e: e.tensor_tensor(out=A.sm[:, 3:4], in0=rec_col, in1=rec_col, op=ALU.mult),
             reads=(("sm", 8),), writes=(("sm", 3),))
        P.op("dve", lambda e: e.tensor_tensor(out=A.sm[:, 3:4], in0=A.sm[:, 3:4], in1=A.sm[:, 0:1], op=ALU.mult),
             reads=(("sm", 3), ("sm", 0)), writes=(("sm", 3),))
        P.op("act", lambda e: e.activation(out=A.sm[:, 1:2], in_=A.sm[:, 3:4], func=AF.Sqrt, scale=1.0 / 256.0,
                                           bias=c["eps"][:]), reads=(("sm", 3), ("c", "eps")), writes=(("sm", 1),))
        P.op("dve", lambda e: e.reciprocal(out=A.sm[:, 2:3], in_=A.sm[:, 1:2]), reads=(("sm", 1),), writes=(("sm", 2),))
        P.op("dve", lambda e: e.tensor_tensor(out=A.sm[:, 2:3], in0=A.sm[:, 2:3], in1=rec_col, op=ALU.mult),
             reads=(("sm", 2), ("sm", 8)), writes=(("sm", 2),))
    P.op("dve", lambda e: e.tensor_scalar(out=A.on[:], in0=otot[:, 0:256], scalar1=A.sm[:, 2:3], scalar2=None,
                                          op0=ALU.mult), reads=(kot, ("sm", 2)), writes=(("on",),))
    _transpose_pair(b, lambda j: A.on[:, j * 128:(j + 1) * 128], (("on",),), 0)
    for j in range(2):
        if wcol_fn is None:
            P.op("dve", lambda e, j=j: e.tensor_tensor(out=A.mst[:, j, tok], in0=b.pt[0][:, j * 128:(j + 1) * 128],
                                                       in1=A.gT[:, j, tok], op=ALU.mult),
                 reads=(("pt", 0), A.k_gT), writes=(("mst",),))
        else:
            P.op("dve", lambda e, j=j: e.scalar_tensor_tensor(out=A.mst[:, j, tok], in0=b.pt[0][:, j * 128:(j + 1) * 128],
                                                              scalar=wcol_fn(j), in1=A.gT[:, j, tok],
                                                              op0=ALU.mult, op1=ALU.mult),
                 reads=(("pt", 0), A.k_gT, ("c", "mlw")), writes=(("mst",),))


def _sample_states(b, A, tok, s_in, s_out, ncol, decay, ml):
    P, c = b.P, b.c
    GS = 2
    NG = 16 // GS
    for g in range(NG):
        P.fill(3)
        for bb in range(GS):
            P.dma("sp", A.Sst[:, bb, :, 0:256], s_in[g * GS + bb].rearrange("(j p) e -> p j e", p=128),
                  writes=(("Sst",),))
        if ml is not None:
            for bb in range(GS):
                for j in range(2):
                    P.op("act", lambda e, bb=bb, j=j: e.copy(out=A.Sst[:, bb, j, 256:257],
                                                            in_=A.ncol[:, j, g * GS + bb:g * GS + bb + 1]),
                         reads=(("ncol",), ("Sst",)), writes=(("Sst",),))
        P.op("act", lambda e: e.copy(out=A.Sstb[:], in_=A.Sst[:]), reads=(("Sst",),), writes=(("Sstb",),))
        for j in range(2):
            if ml is None:
                src = A.qT[:, j, tok]
            else:
                src = A.qi[:, j, :]
            P.op("dve", lambda e, j=j, src=src: e.tensor_tensor(
                out=A.qmx[:, j, :, :], in0=c["qmask"][:, g * GS:(g + 1) * GS, :],
                in1=src.unsqueeze(1).broadcast_to([128, GS, 128]), op=ALU.mult),
                reads=(A.k_qT, ("qi",), ("c", "qmask")), writes=(("qmx",),))
        fns = []
        for bb in range(GS):
            for j in range(2):
                first = (g == 0 and bb == 0 and j == 0)
                last = (g == NG - 1 and bb == GS - 1 and j == 1)
                fns.append(lambda e, bb=bb, j=j, first=first, last=last: e.matmul(
                    b.pb[5][:, 0:ncol], lhsT=A.qmx[:, j, bb, :], rhs=A.Sstb[:, bb, j, 0:ncol], start=first, stop=last))
        P.group("pe", fns, reads=(("qmx",), ("Sstb",)), writes=(("pb", 5),))
        P.op("dve", lambda e: e.tensor_tensor(
            out=A.kdm[:], in0=A.kd[:].unsqueeze(1).broadcast_to([128, GS, 256]),
            in1=c["rowmask"][:, g * GS:(g + 1) * GS].unsqueeze(2).broadcast_to([128, GS, 256]), op=ALU.mult),
            reads=(("kd",), ("c", "rowmask")), writes=(("kdm",),))
        for bb in range(GS):
            for j in range(2):
                bank = (bb * 2 + j) % 2
                P.op("pe", lambda e, bb=bb, j=j, bank=bank: e.matmul(
                    b.sreg(bank, ncol), lhsT=A.kdm[:, bb, j * 128:(j + 1) * 128], rhs=A.v[:, 8, 0:ncol],
                    start=True, stop=True), reads=(("kdm",), A.k_v, ("v1",)), writes=(b.skey(bank),))
                if ml is None:
                    P.op("dve", lambda e, bb=bb, j=j, bank=bank: e.scalar_tensor_tensor(
                        out=A.Sout[:, bb, j, 0:ncol], in0=A.Sst[:, bb, j, 0:ncol], scalar=decay,
                        in1=b.sreg(bank, ncol), op0=ALU.mult, op1=ALU.add),
                        reads=(("Sst",), b.skey(bank)), writes=(("Sout",),))
                else:
                    sq = g * GS + bb
                    P.op("dve", lambda e, bb=bb, j=j, bank=bank, sq=sq: e.scalar_tensor_tensor(
                        out=A.Sout[:, bb, j, 0:ncol], in0=A.Sst[:, bb, j, 0:ncol], scalar=ml.carry_col(sq),
                        in1=b.sreg(bank, ncol), op0=ALU.mult, op1=ALU.add),
                        reads=(("Sst",), b.skey(bank), ("I",)), writes=(("Sout",),))
        for bb in range(GS):
            P.dma("sp", s_out[g * GS + bb].rearrange("(j p) e -> p j e", p=128), A.Sout[:, bb, :, 0:256],
                  reads=(("Sout",),), writes=(("dram", "sout"),), semkey=("Sst_out",))
        if ml is not None:
            for bb in range(GS):
                for j in range(2):
                    P.op("act", lambda e, bb=bb, j=j: e.copy(out=A.ncol[:, j, g * GS + bb:g * GS + bb + 1],
                                                            in_=A.Sout[:, bb, j, 256:257]),
                         reads=(("Sout",),), writes=(("ncol",),))


class _ML:
    def __init__(self, A, TP):
        self.A, self.TP = A, TP

    def carry_col(self, sq):
        t = self.TP + 8 * sq + 7
        return self.A.I[:, t:t + 1]


def _ml_proj_gen(b, A, h):
    cfg, P, c = b.cfg, b.P, b.c
    T, TP, RW, MW, MH, KC = cfg.T, cfg.TP, cfg.RW, cfg.MW, cfg.MH, cfg.KC
    W = b.w_in_ab
    base = 4 * RW
    sets = [(0, 1, 2)]
    if h == 0:
        b.wi = 0
        b.NW = 1
        P.dma("pool", b.wb[1][:, 0:KC, 0:2 * MH],
              W[:, base + 4 * MW:base + 4 * MW + 2 * MH].rearrange("(k p) n -> p k n", p=128),
              reads=(("coll",),), writes=(("wb", 1),))
    yield from _proj_fm_gen(b, W, base + 0 * MW + h * 256, _evac_to(b, A.qT, A.k_qT), sets)
    yield from _proj_fm_gen(b, W, base + 1 * MW + h * 256, _evac_to(b, A.kT, A.k_kT), sets)
    yield from _proj_tm_gen(b, W, base + 2 * MW + h * 256, 256, A.v, A.k_v)
    yield from _proj_fm_gen(b, W, base + 3 * MW + h * 256, _evac_to(b, A.gT, A.k_gT, AF.Sigmoid), sets)


def _ml_rec(b, A, h):
    cfg, P, c = b.cfg, b.P, b.c
    T, TP, RW, MW, MH, KC = cfg.T, cfg.TP, cfg.RW, cfg.MW, cfg.MH, cfg.KC
    W = b.w_in_ab
    ti = b.ti
    base = 4 * RW
    xn_res = tuple(("xn", k) for k in range(KC))
    IG, F, M, BE, I, RST = A.raw[:, 0, :], A.raw[:, 1, :], A.rot[:, 0, :], A.rot[:, 1, :], A.t1[:], A.t2[:]
    A.I = A.t1
    rIG, rF, rM, rBE, rI = ("raw0",), ("raw1",), ("rot0",), ("rot1",), ("I",)
    if h == 0:
        P.barrier()
        P.dma("sp", A.t2[:], b.tbl["rst"][0], writes=(("rst",),))
        A.wgate = b.wb[1]
    for gi, (dst, dres) in enumerate(((IG, rIG), (BE, rBE))):
        col = gi * MH + h
        P.op("dve", lambda e, col=col: e.tensor_copy(
            out=A.wrep, in_=A.wgate[:, 0:KC, col:col + 1].broadcast_to([128, KC, 128])),
            reads=(("wb", 1),), writes=(("wrep",),))
        fns = []
        for k in range(KC):
            for j in range(3):
                fns.append(lambda e, k=k, j=j: e.matmul(b.pb[4 + j][:, 0:384], lhsT=A.wrep[:, k, :],
                                                        rhs=b.xn[:, k, j * 384:(j + 1) * 384],
                                                        start=(k == 0), stop=(k == KC - 1)))
        P.group("pe", fns, reads=(("wrep",),) + xn_res, writes=tuple(("pb", 4 + j) for j in range(3)))
        for j in range(3):
            P.op("dve", lambda e, j=j, dst=dst, col=col: e.tensor_scalar(
                out=dst[:, j * 384:(j + 1) * 384], in0=b.pb[4 + j][:, 0:384], scalar1=c["bif"][:, col:col + 1],
                scalar2=None, op0=ALU.add), reads=(("pb", 4 + j), ("c", "bif")), writes=(dres,))
    P.op("act", lambda e: e.activation(out=I, in_=BE, func=AF.Abs), reads=(rBE,), writes=(rI,))
    P.op("act", lambda e: e.activation(out=I, in_=I, func=AF.Exp, scale=-1.0), reads=(rI,), writes=(rI,))
    P.op("act", lambda e: e.activation(out=I, in_=I, func=AF.Ln, bias=c["one"][:], scale=1.0),
         reads=(rI, ("c", "one")), writes=(rI,))
    P.op("dve", lambda e: e.scalar_tensor_tensor(out=I, in0=BE, scalar=0.0, in1=I, op0=ALU.min, op1=ALU.subtract),
         reads=(rBE, rI), writes=(rI,))
    LF = I
    if ti == 0:
        P.op("dve", lambda e: e.memset(A.S[:], 0.0), writes=(("S",),))
        P.op("dve", lambda e: e.memset(A.sm[:, 10:11], 0.0), writes=(("sm", 10),))
    else:
        P.dma("sp", A.S[:, :, 0:256], b.o_mc_p[h].rearrange("(j p) e -> p j e", p=128),
              reads=(("dram", "mcp", h),), writes=(("S",),))
        P.dma("sp", A.S[:, :, 256], b.o_mn_p[h].rearrange("(j p) -> p j", p=128),
              reads=(("dram", "mnp", h),), writes=(("S",),), allow_slow_non_contiguous=True)
        P.dma("sp", A.sm[:, 10:11], b.o_mm_p[0:1, h:h + 1].broadcast_to([128, 1]) if False else
              b.o_mm_p[0, h:h + 1].partition_broadcast(128),
              reads=(("dram", "mmp", h),), writes=(("sm", 10),))
    P.op("act", lambda e: e.copy(out=A.Sbf[:], in_=A.S[:]), reads=(("S",),), writes=(("Sbf",),))
    P.dma("sp", A.mrow[:], b.s_mm[ti, h].partition_broadcast(128), writes=(("mrow",),))
    P.dma("sp", A.nrow[:], b.s_mn[ti, :, h, :], writes=(("nrow",),))
    for j in range(2):
        P.op("pe", lambda e, j=j: e.transpose(out=b.pb[5][:, 0:16], in_=A.nrow[:, j * 128:(j + 1) * 128],
                                              identity=c["ident_f"][0:16, 0:16]),
             reads=(("nrow",), ("c", "ident_f")), writes=(("pb", 5),))
        P.op("act", lambda e, j=j: e.copy(out=A.ncol[:, j, :], in_=b.pb[5][:, 0:16]), reads=(("pb", 5),),
             writes=(("ncol",),))
    P.op("dve", lambda e: e.tensor_tensor_scan(out=F, data0=RST, data1=LF, initial=0.0, op0=ALU.mult, op1=ALU.add),
         reads=(("rst",), rI), writes=(rF,))
    if cfg.xch:
        P.op("dve", lambda e: e.tensor_tensor_scan(out=M[:, 0:TP], data0=LF[:, 0:TP], data1=IG[:, 0:TP],
                                                   initial=0.0, op0=ALU.add, op1=ALU.max),
             reads=(rI, rIG), writes=(rM,))
        P.op("dve", lambda e: e.tensor_tensor(out=A.sm[:, 12:13], in0=F[:, TP - 1:TP], in1=M[:, TP - 1:TP], op=ALU.subtract),
             reads=(rF, rM), writes=(("sm", 12),))
        P.op("dve", lambda e: e.tensor_tensor(out=BE[:, 0:TP], in0=IG[:, 0:TP], in1=F[:, 0:TP], op=ALU.subtract),
             reads=(rIG, rF), writes=(rBE,))
        P.op("act", lambda e: e.activation(out=BE[:, 0:TP], in_=BE[:, 0:TP], func=AF.Exp, bias=A.sm[:, 12:13], scale=1.0),
             reads=(rBE, ("sm", 12)), writes=(rBE,))
        for cidx in range(8):
            tok = slice(cidx * 128, (cidx + 1) * 128)
            P.op("pe", lambda e: e.matmul(b.pb[3][:, 128:129], lhsT=BE[0:1, tok], rhs=c["one"][0:1, 0:1], start=True, stop=True),
                 reads=(rBE, ("c", "one")), writes=(("pb", 3),))
            P.op("act", lambda e: e.copy(out=A.cols[:, 0:1], in_=b.pb[3][:, 128:129]), reads=(("pb", 3),), writes=(("cols",),))
            _transpose_pair(b, lambda j: A.kT[:, j, tok], (A.k_kT,), 1)
            P.op("dve", lambda e: e.tensor_scalar(out=A.kd[:], in0=b.pt[0][:, 256:512], scalar1=A.cols[:, 0:1],
                                                  scalar2=1.0 / 16.0, op0=ALU.mult, op1=ALU.mult),
                 reads=(("pt", 0), ("cols",)), writes=(("kd",),))
            for j in range(2):
                bank = j
                P.op("pe", lambda e, j=j, bank=bank: e.matmul(b.sreg(bank, 257), lhsT=A.kd[:, j * 128:(j + 1) * 128],
                                                              rhs=A.v[:, cidx, 0:257], start=True, stop=True),
                     reads=(("kd",), A.k_v, ("v1",)), writes=(b.skey(bank),))
                P.op("dve", lambda e, j=j, bank=bank: e.tensor_tensor(out=A.S[:, j, 0:257], in0=A.S[:, j, 0:257],
                                                                      in1=b.sreg(bank, 257), op=ALU.add),
                     reads=(("S",), b.skey(bank)), writes=(("S",),))
        P.op("act", lambda e: e.copy(out=A.S[:, 0, 258:259], in_=M[:, TP - 1:TP]), reads=(rM,), writes=(("S",),))
        _xchg(b, A.S[:].rearrange("p j e -> p (j e)"), b.snd[:, :], b.rcv[:, :], ("S",))
        P.op("act", lambda e: e.copy(out=A.sm[:, 10:11], in_=A.S[:, 0, 258:259]), reads=(("S",),), writes=(("sm", 10),))
        P.op("act", lambda e: e.copy(out=A.Sbf[:], in_=A.S[:]), reads=(("S",),), writes=(("Sbf",),))
    P.op("dve", lambda e: e.tensor_tensor_scan(out=M[:, 0:TP], data0=LF[:, 0:TP], data1=IG[:, 0:TP],
                                               initial=A.sm[:, 10:11], op0=ALU.add, op1=ALU.max),
         reads=(rI, rIG, ("sm", 10)), writes=(rM,))
    for sq in range(16):
        sl = slice(TP + 8 * sq, TP + 8 * sq + 8)
        P.op("dve", lambda e, sl=sl, sq=sq: e.tensor_tensor_scan(out=M[:, sl], data0=LF[:, sl], data1=IG[:, sl],
                                                                initial=A.mrow[:, sq:sq + 1], op0=ALU.add, op1=ALU.max),
             reads=(rI, rIG, ("mrow",)), writes=(rM,))
    P.op("dve", lambda e: e.tensor_tensor(out=IG, in0=IG, in1=F, op=ALU.subtract), reads=(rIG, rF), writes=(rIG,))
    P.op("dve", lambda e: e.tensor_tensor(out=F, in0=F, in1=M, op=ALU.subtract), reads=(rF, rM), writes=(rF,))
    P.op("act", lambda e: e.copy(out=A.sm[:, 11:12], in_=M[:, TP - 1:TP]), reads=(rM,), writes=(("sm", 11),))
    P.op("act", lambda e: e.copy(out=A.mend[:], in_=M[:, TP:T].rearrange("p (b t) -> p b t", t=8)[:, :, 7]),
         reads=(rM,), writes=(("mend",),))
    P.op("act", lambda e: e.activation(out=M, in_=M, func=AF.Exp, scale=-1.0), reads=(rM,), writes=(rM,))
    P.op("dve", lambda e: e.tensor_tensor(
        out=BE[:, 0:TP].rearrange("p (c t) -> p c t", t=128), in0=IG[:, 0:TP].rearrange("p (c t) -> p c t", t=128),
        in1=F[:, 0:TP].rearrange("p (c t) -> p c t", t=128)[:, :, 127:128].broadcast_to([128, 8, 128]), op=ALU.add),
        reads=(rIG, rF), writes=(rBE,))
    P.op("dve", lambda e: e.tensor_tensor(
        out=BE[:, TP:T].rearrange("p (c t) -> p c t", t=8), in0=IG[:, TP:T].rearrange("p (c t) -> p c t", t=8),
        in1=F[:, TP:T].rearrange("p (c t) -> p c t", t=8)[:, :, 7:8].broadcast_to([128, 16, 8]), op=ALU.add),
        reads=(rIG, rF), writes=(rBE,))
    P.op("act", lambda e: e.activation(out=BE, in_=BE, func=AF.Exp), reads=(rBE,), writes=(rBE,))
    P.op("dve", lambda e: e.tensor_scalar(out=I[:, 0:128], in0=F[:, 0:128], scalar1=A.sm[:, 10:11], scalar2=None,
                                          op0=ALU.add), reads=(rF, ("sm", 10)), writes=(rI,))
    for cc in range(1, 8):
        P.op("dve", lambda e, cc=cc: e.tensor_scalar(out=I[:, cc * 128:(cc + 1) * 128], in0=F[:, cc * 128:(cc + 1) * 128],
                                                     scalar1=F[:, cc * 128 - 1:cc * 128], scalar2=None, op0=ALU.subtract),
             reads=(rF,), writes=(rI,))
    P.op("dve", lambda e: e.tensor_tensor(
        out=I[:, TP:T].rearrange("p (c t) -> p c t", t=8), in0=F[:, TP:T].rearrange("p (c t) -> p c t", t=8),
        in1=A.mrow[:].unsqueeze(2).broadcast_to([128, 16, 8]), op=ALU.add), reads=(rF, ("mrow",)), writes=(rI,))
    P.op("act", lambda e: e.activation(out=I, in_=I, func=AF.Exp), reads=(rI,), writes=(rI,))
    ml = _ML(A, TP)
    pending = None
    for pos, cidx in enumerate([8, 0, 1, 2, 3, 4, 5, 6, 7]):
        smp = 1 if cidx == 8 else 0
        tok = slice(cidx * 128, (cidx + 1) * 128)
        otot, kot = A.otot_db[pos % 2], ("otot", pos % 2)
        fns = []
        for i, src in enumerate((IG, BE, M)):
            fns.append(lambda e, i=i, src=src: e.matmul(b.pb[3][:, 128 + i:129 + i], lhsT=src[0:1, tok], rhs=c["one"][0:1, 0:1],
                                                        start=True, stop=True))
        P.group("pe", fns, reads=(rIG, rBE, rM, ("c", "one")), writes=(("pb", 3),))
        P.op("act", lambda e: e.copy(out=A.cols[:, 0:3], in_=b.pb[3][:, 128:131]), reads=(("pb", 3),), writes=(("cols",),))
        P.group("pe", [lambda e, j=j: e.matmul(b.pb[3][:, 0:128], lhsT=A.kT[:, j, tok], rhs=A.qT[:, j, tok],
                                               start=(j == 0), stop=(j == 1)) for j in range(2)],
                reads=(A.k_kT, A.k_qT), writes=(("pb", 3),))
        P.op("dve", lambda e: e.tensor_tensor(out=A.D[:], in0=F[:, tok], in1=c["masks"][:, smp, :], op=ALU.add),
             reads=(rF, ("c", "masks")), writes=(("D",),))
        P.op("act", lambda e: e.activation(out=A.D[:], in_=A.D[:], func=AF.Exp, bias=A.cols[:, 0:1], scale=1.0),
             reads=(("D",), ("cols",)), writes=(("D",),))
        P.op("dve", lambda e: e.scalar_tensor_tensor(out=A.attm[:], in0=b.pb[3][:, 0:128], scalar=1.0 / 16.0, in1=A.D[:],
                                                     op0=ALU.mult, op1=ALU.mult),
             reads=(("pb", 3), ("D",)), writes=(("attm",),))
        P.op("pe", lambda e: e.matmul(b.pb[4][:, 0:257], lhsT=A.attm[:], rhs=A.v[:, cidx, 0:257], start=True, stop=True),
             reads=(("attm",), A.k_v, ("v1",)), writes=(("pb", 4),))
        P.op("act", lambda e: e.copy(out=A.osb[:, 0:257], in_=b.pb[4][:, 0:257]), reads=(("pb", 4),), writes=(("osb",),))
        for j in range(2):
            P.op("dve", lambda e, j=j: e.tensor_tensor(out=A.qi[:, j, :], in0=A.qT[:, j, tok], in1=I[:, tok], op=ALU.mult),
                 reads=(A.k_qT, rI), writes=(("qi",),))
        _transpose_pair(b, lambda j: A.kT[:, j, tok], (A.k_kT,), 1)
        P.op("dve", lambda e: e.tensor_scalar(out=A.kd[:], in0=b.pt[0][:, 256:512], scalar1=A.cols[:, 1:2],
                                              scalar2=1.0 / 16.0, op0=ALU.mult, op1=ALU.mult),
             reads=(("pt", 0), ("cols",)), writes=(("kd",),))
        if not smp:
            P.group("pe", [lambda e, j=j: e.matmul(b.pb[5][:, 0:257], lhsT=A.qi[:, j, :], rhs=A.Sbf[:, j, 0:257],
                                                   start=(j == 0), stop=(j == 1)) for j in range(2)],
                    reads=(("qi",), ("Sbf",)), writes=(("pb", 5),))
            end = cidx * 128 + 127
            for j in range(2):
                bank = j
                P.op("pe", lambda e, j=j, bank=bank: e.matmul(b.sreg(bank, 257), lhsT=A.kd[:, j * 128:(j + 1) * 128],
                                                              rhs=A.v[:, cidx, 0:257], start=True, stop=True),
                     reads=(("kd",), A.k_v, ("v1",)), writes=(b.skey(bank),))
                P.op("dve", lambda e, j=j, bank=bank: e.scalar_tensor_tensor(
                    out=A.S[:, j, 0:257], in0=A.S[:, j, 0:257], scalar=I[:, end:end + 1], in1=b.sreg(bank, 257),
                    op0=ALU.mult, op1=ALU.add), reads=(("S",), b.skey(bank), rI), writes=(("S",),))
            P.op("act", lambda e: e.copy(out=A.Sbf[:], in_=A.S[:]), reads=(("S",),), writes=(("Sbf",),))
        else:
            _sample_states(b, A, tok, b.s_mc[ti, :, h], b.o_mc_s[ti, :, h], 257, None, ml)
        P.op("dve", lambda e: e.tensor_tensor(out=otot[:, 0:257], in0=b.pb[5][:, 0:257], in1=A.osb[:, 0:257], op=ALU.add),
             reads=(("pb", 5), ("osb",)), writes=(kot,))
        P.op("act", lambda e: e.copy(out=otot[:, 258:259], in_=A.cols[:, 2:3]), reads=(("cols",), kot), writes=(kot,))

        def back(tok=tok, otot=otot, kot=kot):
            P.op("act", lambda e: e.activation(out=A.sm[:, 4:5], in_=otot[:, 256:257], func=AF.Abs),
                 reads=(kot,), writes=(("sm", 4),))
            P.op("dve", lambda e: e.tensor_tensor(out=A.sm[:, 5:6], in0=A.sm[:, 4:5], in1=otot[:, 258:259], op=ALU.max),
                 reads=(("sm", 4), kot), writes=(("sm", 5),))
            P.op("dve", lambda e: e.reciprocal(out=A.sm[:, 8:9], in_=A.sm[:, 5:6]), reads=(("sm", 5),), writes=(("sm", 8),))
            _norm_gate_store(b, A, tok, A.sm[:, 8:9], lambda j: c["mlw"][:, h * 2 + j:h * 2 + j + 1], otot, kot)

        if pending is not None:
            pending()
        pending = back
    pending()
    for j in range(2):
        P.dma("sp", b.o_mc_p[h, j * 128:(j + 1) * 128, :], A.S[:, j, 0:256], reads=(("S",),),
              writes=(("dram", "mcp", h),), semkey=("Sout",))
    P.dma("sp", b.o_mn_p[h].rearrange("(j p) -> p j", p=128), A.S[:, :, 256], reads=(("S",),),
          writes=(("dram", "mnp", h),), semkey=("Sout",), allow_slow_non_contiguous=True)
    P.dma("sp", b.o_mm_p[0:1, h:h + 1], A.sm[0:1, 11:12], reads=(("sm", 11),), writes=(("dram", "mmp", h),),
          semkey=("Sout",))
    P.dma("sp", b.o_mm_s[ti, h:h + 1, :], A.mend[0:1, :], reads=(("mend",),), writes=(("dram", "mms"),),
          semkey=("Sout",))
    for j in range(2):
        P.op("pe", lambda e, j=j: e.transpose(out=b.pb[5][0:16, 0:128], in_=A.ncol[:, j, :], identity=c["ident_f"][:]),
             reads=(("ncol",), ("c", "ident_f")), writes=(("pb", 5),))
        P.op("act", lambda e, j=j: e.copy(out=A.nrow[:, j * 128:(j + 1) * 128], in_=b.pb[5][0:16, 0:128]),
             reads=(("pb", 5),), writes=(("nrow",),))
    P.dma("sp", b.o_mn_s[ti, :, h, :], A.nrow[:], reads=(("nrow",),), writes=(("dram", "mns"),), semkey=("Sout",))
    _store_mixed(b, A, cfg.RH + h)


def mixer_c_real(b):
    cfg, P, c = b.cfg, b.P, b.c
    P.barrier()
    KC, T, TP, D, HH = cfg.KC, cfg.T, cfg.TP, cfg.D, cfg.HH
    W = b.w_in_c
    ti = b.ti
    sets = [(0, 1, 2)]
    with ExitStack() as es:
        A = _AB()
        A.Q32_db = [b.sb(es, "Q32", [128, 2, T], BF16) for _ in range(2)]
        A.F32_db = [b.sb(es, "F32", [128, 2, T], F32) for _ in range(2)]
        A.LN = b.sb(es, "LN", [128, T], F32)
        A.G = b.sb(es, "G", [128, T], F32)
        A.E = b.sb(es, "E", [128, T], F32)
        A.RST = b.sb(es, "RST", [128, T], F32)
        A.qT_db = [b.sb(es, "qT", [128, 2, T], BF16)] * 2
        A.kT_db = [b.sb(es, "kT", [128, 2, T], BF16)] * 2
        A.gT_db = [b.sb(es, "gT", [128, 2, T], BF16) for _ in range(2)]
        A.v_db = [b.sb(es, "v", [128, 9, 256], BF16) for _ in range(2)]
        A.mst = b.sb(es, "mst", [128, 2, T], BF16)
        A.S = b.sb(es, "S", [128, 128], F32)
        A.Sp = b.sb(es, "Sp", [128, 128], F32)
        A.Sbf = b.sb(es, "Sbf", [128, 128], BF16)
        A.Sst = b.sb(es, "Sst", [128, 4, 128], F32)
        A.Sout = b.sb(es, "Sout", [128, 4, 128], F32)
        A.Sstb = b.sb(es, "Sstb", [128, 4, 128], BF16)
        A.qmx = b.sb(es, "qmx", [128, 4, 128], BF16)
        A.kdm = b.sb(es, "kdm", [128, 4, 128], BF16)
        A.attm = b.sb(es, "attm", [128, 128], BF16)
        A.osb = b.sb(es, "osb", [128, 128], F32)
        A.otot_db = [b.sb(es, "otot", [128, 128], F32) for _ in range(2)]
        A.junk = b.sb(es, "junk", [128, 128], F32)
        A.on = b.sb(es, "on", [128, 128], BF16)
        A.kd = b.sb(es, "kd", [128, 128], BF16)
        A.sm = b.sb(es, "sm", [128, 40], F32)
        A.ss = b.sb(es, "ss", [128, 4], F32)
        P.dma("sp", A.RST[:], b.tbl["rst"][1], writes=(("rst",),))
        views = [_View(A, 0), _View(A, 1)]

        def proj_gen(hp):
            V = views[hp % 2]
            yield from _proj_fm_gen(b, W, 0 * D + hp * 256, _evac_to(b, V.Q32, V.k_Q32, AF.Silu), sets)
            yield from _proj_fm_gen(b, W, 1 * D + hp * 256, _evac_to(b, V.F32, V.k_F32, AF.Sigmoid), sets)
            yield from _proj_tm_gen(b, W, 2 * D + hp * 256, 256, V.v, V.k_v)
            yield from _proj_fm_gen(b, W, 3 * D + hp * 256, _evac_to(b, V.gT, V.k_gT, AF.Silu), sets)

        def rec(hp):
            V = views[hp % 2]
            for s in range(2):
                _hg_head(b, V, hp * 2 + s, s)
            for j in range(2):
                P.dma("sp", b.mT[hp * 2 + j], A.mst[:, j, :], reads=(("mst",),), writes=(("dram", "m", hp * 2 + j),),
                      semkey=("mst",))

        _pipeline(b, HH // 2, proj_gen, rec)
        P.barrier()


def _hg_head(b, A, hh, s):
    cfg, P, c = b.cfg, b.P, b.c
    T, TP = cfg.T, cfg.TP
    ti = b.ti
    FG = A.F32[:, s, :]
    Q = A.Q32[:, s, :]
    rFG, rQ = A.k_F32, A.k_Q32
    P.op("dve", lambda e: e.tensor_scalar(out=FG, in0=FG, scalar1=c["oml"][:, hh:hh + 1], scalar2=c["lb"][:, hh:hh + 1],
                                          op0=ALU.mult, op1=ALU.add), reads=(rFG, ("c", "oml"), ("c", "lb")), writes=(rFG,))
    P.op("act", lambda e: e.activation(out=A.LN[:], in_=FG, func=AF.Ln), reads=(rFG,), writes=(("LN",),))
    P.op("dve", lambda e: e.tensor_tensor_scan(out=A.G[:], data0=A.RST[:], data1=A.LN[:], initial=0.0,
                                               op0=ALU.mult, op1=ALU.add), reads=(("rst",), ("LN",)), writes=(("G",),))
    P.op("dve", lambda e: e.tensor_scalar(out=FG, in0=FG, scalar1=-1.0, scalar2=1.0, op0=ALU.mult, op1=ALU.add),
         reads=(rFG,), writes=(rFG,))
    for cc in range(8):
        P.op("dve", lambda e, cc=cc: e.tensor_scalar(out=A.sm[:, cc:cc + 1], in0=A.G[:, cc * 128 + 63:cc * 128 + 64],
                                                     scalar1=-1.0, scalar2=None, op0=ALU.mult),
             reads=(("G",),), writes=(("sm", "a"),))
    for cc in range(8):
        tok = slice(cc * 128, (cc + 1) * 128)
        P.op("act", lambda e, cc=cc, tok=tok: e.activation(out=A.E[:, tok], in_=A.G[:, tok], func=AF.Exp,
                                                          bias=A.sm[:, cc:cc + 1], scale=1.0),
             reads=(("G",), ("sm", "a")), writes=(("E",),))
    P.op("act", lambda e: e.activation(out=A.E[:, TP:T], in_=A.G[:, TP:T], func=AF.Exp), reads=(("G",),), writes=(("E",),))
    P.op("dve", lambda e: e.tensor_tensor(out=A.qT[:, s, :], in0=Q, in1=A.E[:], op=ALU.mult), reads=(rQ, ("E",)),
         writes=(A.k_qT,))
    P.op("act", lambda e: e.copy(out=A.sm[:, 8:16], in_=A.E[:, 0:TP].rearrange("p (c t) -> p c t", t=128)[:, :, 127]),
         reads=(("E",),), writes=(("sm", "b"),))
    P.op("act", lambda e: e.copy(out=A.sm[:, 16:32], in_=A.E[:, TP:T].rearrange("p (c t) -> p c t", t=8)[:, :, 7]),
         reads=(("E",),), writes=(("sm", "b"),))
    P.op("act", lambda e: e.activation(out=A.sm[:, 32:40], in_=A.sm[:, 0:8], func=AF.Exp, scale=-1.0),
         reads=(("sm", "a"),), writes=(("sm", "c"),))
    P.op("dve", lambda e: e.reciprocal(out=A.E[:], in_=A.E[:]), reads=(("E",),), writes=(("E",),))
    P.op("dve", lambda e: e.tensor_tensor(out=A.kT[:, s, :], in0=FG, in1=A.E[:], op=ALU.mult), reads=(rFG, ("E",)),
         writes=(A.k_kT,))
    if ti == 0:
        P.op("dve", lambda e: e.memset(A.S[:], 0.0), writes=(("S",),))
    else:
        P.dma("sp", A.S[:], b.o_hg_p[hh], reads=(("dram", "hgp", hh),), writes=(("S",),))
    vs = slice(s * 128, (s + 1) * 128)
    if cfg.xch:
        for cidx in range(8):
            tok = slice(cidx * 128, (cidx + 1) * 128)
            P.op("pe", lambda e: e.transpose(out=b.pt[0][:, 256:384], in_=A.kT[:, s, tok], identity=c["ident_bf"][:]),
                 reads=(A.k_kT, ("c", "ident_bf")), writes=(("pt", 0),))
            P.op("act", lambda e: e.copy(out=A.kd[:], in_=b.pt[0][:, 256:384]), reads=(("pt", 0),), writes=(("kd",),))
            P.op("dve", lambda e: e.tensor_scalar(out=A.Sp[:], in0=A.S[:], scalar1=A.sm[:, 32 + cidx:33 + cidx], scalar2=None,
                                                  op0=ALU.mult), reads=(("S",), ("sm", "c")), writes=(("Sp",),))
            P.op("pe", lambda e: e.matmul(b.sreg(0, 128), lhsT=A.kd[:], rhs=A.v[:, cidx, vs], start=True, stop=True),
                 reads=(("kd",), A.k_v), writes=(b.skey(0),))
            P.op("dve", lambda e: e.tensor_tensor(out=A.Sp[:], in0=b.sreg(0, 128), in1=A.Sp[:], op=ALU.add),
                 reads=(b.skey(0), ("Sp",)), writes=(("Sp",),))
            P.op("dve", lambda e: e.tensor_scalar(out=A.S[:], in0=A.Sp[:], scalar1=A.sm[:, 8 + cidx:9 + cidx], scalar2=None,
                                                  op0=ALU.mult), reads=(("Sp",), ("sm", "b")), writes=(("S",),))
        _xchg(b, A.S[:], b.snd_c[:, :], b.rcv_c[:, :], ("S",))
    pending = None
    for pos, cidx in enumerate([8, 0, 1, 2, 3, 4, 5, 6, 7]):
        smp = 1 if cidx == 8 else 0
        tok = slice(cidx * 128, (cidx + 1) * 128)
        otot, kot = A.otot_db[pos % 2], ("otot", pos % 2)
        P.op("pe", lambda e: e.matmul(b.pb[3][:, 0:128], lhsT=A.kT[:, s, tok], rhs=A.qT[:, s, tok], start=True, stop=True),
             reads=(A.k_kT, A.k_qT), writes=(("pb", 3),))
        P.op("dve", lambda e: e.tensor_tensor(out=A.attm[:], in0=b.pb[3][:, 0:128], in1=c["masks"][:, 2 + smp, :], op=ALU.mult),
             reads=(("pb", 3), ("c", "masks")), writes=(("attm",),))
        P.op("pe", lambda e: e.matmul(b.pb[4][:, 0:128], lhsT=A.attm[:], rhs=A.v[:, cidx, vs], start=True, stop=True),
             reads=(("attm",), A.k_v), writes=(("pb", 4),))
        P.op("act", lambda e: e.copy(out=A.osb[:], in_=b.pb[4][:, 0:128]), reads=(("pb", 4),), writes=(("osb",),))
        P.op("pe", lambda e: e.transpose(out=b.pt[0][:, 256:384], in_=A.kT[:, s, tok], identity=c["ident_bf"][:]),
             reads=(A.k_kT, ("c", "ident_bf")), writes=(("pt", 0),))
        P.op("act", lambda e: e.copy(out=A.kd[:], in_=b.pt[0][:, 256:384]), reads=(("pt", 0),), writes=(("kd",),))
        if not smp:
            P.op("dve", lambda e: e.tensor_scalar(out=A.Sp[:], in0=A.S[:], scalar1=A.sm[:, 32 + cidx:33 + cidx], scalar2=None,
                                                  op0=ALU.mult), reads=(("S",), ("sm", "c")), writes=(("Sp",),))
            P.op("act", lambda e: e.copy(out=A.Sbf[:], in_=A.Sp[:]), reads=(("Sp",),), writes=(("Sbf",),))
            P.op("pe", lambda e: e.matmul(b.pb[5][:, 0:128], lhsT=A.qT[:, s, tok], rhs=A.Sbf[:], start=True, stop=True),
                 reads=(A.k_qT, ("Sbf",)), writes=(("pb", 5),))
            P.op("pe", lambda e: e.matmul(b.sreg(0, 128), lhsT=A.kd[:], rhs=A.v[:, cidx, vs], start=True, stop=True),
                 reads=(("kd",), A.k_v), writes=(b.skey(0),))
            P.op("dve", lambda e: e.tensor_tensor(out=A.Sp[:], in0=b.sreg(0, 128), in1=A.Sp[:], op=ALU.add),
                 reads=(b.skey(0), ("Sp",)), writes=(("Sp",),))
            P.op("dve", lambda e: e.tensor_scalar(out=A.S[:], in0=A.Sp[:], scalar1=A.sm[:, 8 + cidx:9 + cidx], scalar2=None,
                                                  op0=ALU.mult), reads=(("Sp",), ("sm", "b")), writes=(("S",),))
        else:
            GS, NG = 4, 4
            for g in range(NG):
                P.fill(2)
                P.dma("sp", A.Sst[:], b.s_hg[ti, g * GS:(g + 1) * GS, hh].rearrange("b p e -> p b e"), writes=(("Sst",),))
                P.op("act", lambda e: e.copy(out=A.Sstb[:], in_=A.Sst[:]), reads=(("Sst",),), writes=(("Sstb",),))
                P.op("dve", lambda e, g=g: e.tensor_tensor(
                    out=A.qmx[:], in0=c["qmask"][:, g * GS:(g + 1) * GS, :],
                    in1=A.qT[:, s, tok].unsqueeze(1).broadcast_to([128, GS, 128]), op=ALU.mult),
                    reads=(A.k_qT, ("c", "qmask")), writes=(("qmx",),))
                fns = []
                for bb in range(GS):
                    fns.append(lambda e, bb=bb, g=g: e.matmul(b.pb[5][:, 0:128], lhsT=A.qmx[:, bb, :], rhs=A.Sstb[:, bb, :],
                                                              start=(g == 0 and bb == 0), stop=(g == NG - 1 and bb == GS - 1)))
                P.group("pe", fns, reads=(("qmx",), ("Sstb",)), writes=(("pb", 5),))
                P.op("dve", lambda e, g=g: e.tensor_tensor(
                    out=A.kdm[:], in0=A.kd[:].unsqueeze(1).broadcast_to([128, GS, 128]),
                    in1=c["rowmask"][:, g * GS:(g + 1) * GS].unsqueeze(2).broadcast_to([128, GS, 128]), op=ALU.mult),
                    reads=(("kd",), ("c", "rowmask")), writes=(("kdm",),))
                for bb in range(GS):
                    bank = bb % 2
                    sq = g * GS + bb
                    P.op("pe", lambda e, bb=bb, bank=bank: e.matmul(b.sreg(bank, 128), lhsT=A.kdm[:, bb, :],
                                                                    rhs=A.v[:, 8, vs], start=True, stop=True),
                         reads=(("kdm",), A.k_v), writes=(b.skey(bank),))
                    P.op("dve", lambda e, bb=bb, bank=bank: e.tensor_tensor(out=A.Sout[:, bb, :], in0=b.sreg(bank, 128),
                                                                            in1=A.Sst[:, bb, :], op=ALU.add),
                         reads=(b.skey(bank), ("Sst",)), writes=(("Sout",),))
                    P.op("dve", lambda e, bb=bb, sq=sq: e.tensor_scalar(out=A.Sout[:, bb, :], in0=A.Sout[:, bb, :],
                                                                        scalar1=A.sm[:, 16 + sq:17 + sq], scalar2=None,
                                                                        op0=ALU.mult),
                         reads=(("Sout",), ("sm", "b")), writes=(("Sout",),))
                P.dma("sp", b.o_hg_s[ti, g * GS:(g + 1) * GS, hh].rearrange("b p e -> p b e"), A.Sout[:],
                      reads=(("Sout",),), writes=(("dram", "hgs"),), semkey=("Sst_out",))
        P.op("dve", lambda e: e.tensor_tensor(out=otot[:], in0=b.pb[5][:, 0:128], in1=A.osb[:], op=ALU.add),
             reads=(("pb", 5), ("osb",)), writes=(kot,))

        def back(tok=tok, otot=otot, kot=kot):
            P.op("act", lambda e: e.activation(out=A.junk[:], in_=otot[:], func=AF.Square, accum_out=A.ss[:, 0:1]),
                 reads=(kot,), writes=(("junk",), ("ss", 0)))
            P.op("act", lambda e: e.activation(out=A.ss[:, 1:2], in_=A.ss[:, 0:1], func=AF.Sqrt, scale=1.0 / 128.0,
                                               bias=c["eps"][:]), reads=(("ss", 0), ("c", "eps")), writes=(("ss", 1),))
            P.op("dve", lambda e: e.reciprocal(out=A.ss[:, 2:3], in_=A.ss[:, 1:2]), reads=(("ss", 1),), writes=(("ss", 2),))
            P.op("dve", lambda e: e.tensor_scalar(out=A.on[:], in0=otot[:], scalar1=A.ss[:, 2:3], scalar2=None, op0=ALU.mult),
                 reads=(kot, ("ss", 2)), writes=(("on",),))
            P.op("pe", lambda e: e.transpose(out=b.pt[0][:, 0:128], in_=A.on[:], identity=c["ident_bf"][:]),
                 reads=(("on",), ("c", "ident_bf")), writes=(("pt", 0),))
            P.op("dve", lambda e: e.scalar_tensor_tensor(out=A.mst[:, s, tok], in0=b.pt[0][:, 0:128], scalar=c["hgw"][:, 0:1],
                                                         in1=A.gT[:, s, tok], op0=ALU.mult, op1=ALU.mult),
                 reads=(("pt", 0), A.k_gT, ("c", "hgw")), writes=(("mst",),))

        if pending is not None:
            pending()
        pending = back
    pending()
    P.dma("sp", b.o_hg_p[hh], A.S[:], reads=(("S",),), writes=(("dram", "hgp", hh),), semkey=("Sout",))
```

```python
import numpy as np
import ml_dtypes
from contextlib import ExitStack
import concourse.bass as bass
import concourse.mybir as mybir
from concourse.bass_utils import run_bass_kernel_spmd

F32 = mybir.dt.float32
BF16 = mybir.dt.bfloat16
AF = mybir.ActivationFunctionType
ALU = mybir.AluOpType
AX = mybir.AxisListType
EPS = 1e-6


class Cfg:
    def __init__(self, D=4096, RH=8, MH=8, DFF=11008, NT=1, ncores=8, batch=4, seq=2048, dec_batch=128,
                 past_len=16384, xch=True):
        self.xch = xch
        assert (NT == 1) if xch else True
        self.groups = [[2 * i, 2 * i + 1] for i in range(ncores // 2)]
        self.D = D
        self.KC = D // 128
        self.RH, self.MH = RH, MH
        self.RW = D // 2
        self.MW = D - self.RW
        assert self.RW // RH == 256 and self.MW // MH == 256
        self.HH = D // 128
        self.DFF = DFF
        self.FC = DFF // 128
        assert DFF % 128 == 0
        self.NT = NT
        self.ncores = ncores
        self.T = 1152
        self.TP = 1024
        self.NS = 16
        self.batch, self.seq, self.dec_batch, self.past_len = batch, seq, dec_batch, past_len
        self.ABC = 4 * self.RW + 4 * self.MW + 2 * MH
        self.CC = 4 * D
        nq = 4 if self.FC >= 8 else 1
        base = self.FC // nq
        rem = self.FC % nq
        self.fq = []
        s = 0
        for i in range(nq):
            n = base + (1 if i < rem else 0)
            self.fq.append((s, n))
            s += n


class Ev:
    __slots__ = ("sem", "val", "key", "dkey")

    def __init__(self, sem, val, key, dkey=None):
        self.sem, self.val, self.key, self.dkey = sem, val, key, dkey


class Prog:
    EPOCH = 30000

    def __init__(self, nc):
        self.nc = nc
        self.eng = {"pe": nc.tensor, "act": nc.scalar, "dve": nc.vector, "pool": nc.gpsimd, "sp": nc.sync}
        self.sem = {}
        self.cnt = {}
        self.nsem = 0
        for e in self.eng:
            self._new_epoch(e)
        self.known = {e: {} for e in self.eng}
        self.lastw = {}
        self.readers = {}
        self.dsem = {}
        self.dcnt = {}
        self.n_ins = 0
        self.n_wait = 0
        self.filler = None
        self.in_fill = False

    def _new_epoch(self, e):
        self.nsem += 1
        self.sem[e] = self.nc.alloc_semaphore(name="s_%s_%d" % (e, self.nsem))
        self.cnt[e] = 0

    def _wait(self, e, ev):
        if e == "pe" and ev.key == id(self.sem["pe"]):
            return
        k = self.known[e]
        val = ev.val
        if ev.dkey is not None:
            val = max(val, self.dcnt[ev.dkey])
        if k.get(ev.key, 0) >= val:
            return
        self.eng[e].wait_ge(ev.sem, val)
        self.n_wait += 1
        k[ev.key] = val

    def _deps(self, e, reads, writes):
        for r in reads:
            ev = self.lastw.get(r)
            if ev is not None:
                self._wait(e, ev)
        for w in writes:
            ev = self.lastw.get(w)
            if ev is not None:
                self._wait(e, ev)
            for ev in self.readers.get(w, ()):
                self._wait(e, ev)

    def _commit(self, ev, reads, writes):
        for r in reads:
            self.readers.setdefault(r, []).append(ev)
        for w in writes:
            self.lastw[w] = ev
            self.readers[w] = []

    def fill(self, n=1):
        if self.filler is None or self.in_fill:
            return
        self.in_fill = True
        try:
            for _ in range(n):
                try:
                    next(self.filler)
                except StopIteration:
                    self.filler = None
                    break
        finally:
            self.in_fill = False

    def op(self, e, fn, reads=(), writes=()):
        if e == "pe":
            self.fill()
        self._deps(e, reads, writes)
        ins = fn(self.eng[e])
        if self.cnt[e] >= self.EPOCH:
            self._new_epoch(e)
        self.cnt[e] += 1
        s = self.sem[e]
        ins.then_inc(s, 1)
        ev = Ev(s, self.cnt[e], id(s))
        self._commit(ev, reads, writes)
        self.n_ins += 1
        return ev

    def group(self, e, fns, reads=(), writes=()):
        if e == "pe":
            self.fill()
        self._deps(e, reads, writes)
        ins = None
        for fn in fns:
            ins = fn(self.eng[e])
            self.n_ins += 1
        if self.cnt[e] >= self.EPOCH:
            self._new_epoch(e)
        self.cnt[e] += 1
        s = self.sem[e]
        ins.then_inc(s, 1)
        ev = Ev(s, self.cnt[e], id(s))
        self._commit(ev, reads, writes)
        return ev

    def dma(self, q, out, in_, reads=(), writes=(), semkey=None, **kw):
        if semkey is None:
            semkey = writes[0] if writes else reads[0]
        if semkey not in self.dsem:
            self.nsem += 1
            self.dsem[semkey] = self.nc.alloc_semaphore(name="d_%d" % self.nsem)
            self.dcnt[semkey] = 0
        self._deps(q, reads, writes)
        ins = self.eng[q].dma_start(out=out, in_=in_, **kw)
        s = self.dsem[semkey]
        self.dcnt[semkey] += 16
        ins.then_inc(s, 16)
        ev = Ev(s, self.dcnt[semkey], id(s), semkey)
        self._commit(ev, reads, writes)
        self.n_ins += 1
        return ev

    def coll(self, fn, reads=(), writes=()):
        if not hasattr(self, "csem"):
            self.nsem += 1
            self.csem = self.nc.alloc_semaphore(name="c_%d" % self.nsem)
            self.ccnt = 0
        self._deps("pool", reads, writes)
        ins = fn(self.eng["pool"])
        self.ccnt += 1
        ins.then_inc(self.csem, 1)
        ev = Ev(self.csem, self.ccnt, id(self.csem))
        self._commit(ev, reads, writes)
        self.n_ins += 1
        return ev

    def barrier(self):
        best = {}
        for ev in list(self.lastw.values()) + [ev for lst in self.readers.values() for ev in lst]:
            if ev.key not in best or best[ev.key].val < ev.val:
                best[ev.key] = ev
        for e in self.eng:
            for ev in best.values():
                self._wait(e, ev)

    def finish(self):
        evs = set()
        for ev in self.lastw.values():
            evs.add((ev.key, ev.val, ev))
        for lst in self.readers.values():
            for ev in lst:
                evs.add((ev.key, ev.val, ev))
        best = {}
        for key, val, ev in evs:
            if key not in best or best[key].val < val:
                best[key] = ev
        for ev in best.values():
            self._wait("sp", ev)


def _tables(cfg, core, pos0=0):
    T, TP = cfg.T, cfg.TP
    f32 = np.float32
    tb = {}
    tb["ident_bf"] = np.eye(128, dtype=f32).astype(ml_dtypes.bfloat16)
    tb["ident_f"] = np.eye(128, dtype=f32)
    tb["ones_bf"] = np.ones((128, 128), f32).astype(ml_dtypes.bfloat16)
    tb["ones_f"] = np.ones((128, 128), f32)
    half = 128
    inv_freq = (10000.0 ** (-np.arange(half, dtype=f32) / f32(half))).astype(f32)
    rot = np.zeros((cfg.NT, 2, 128, T), f32)
    for ti in range(cfg.NT):
        pos = np.concatenate([np.arange(TP, dtype=f32) + f32(ti * TP + pos0),
                              np.tile(f32(cfg.past_len) + np.arange(8, dtype=f32), cfg.NS)]).astype(f32)
        ang = (pos[None, :] * inv_freq[:, None]).astype(f32)
        rot[ti, 0] = np.cos(ang)
        rot[ti, 1] = np.sin(ang)
    tb["rot"] = rot
    idx = np.arange(128)
    grp = idx // 8
    causal_p = (idx[:, None] <= idx[None, :])
    causal_s = causal_p & (grp[:, None] == grp[None, :])
    lg = np.log1p(-np.exp2(-5.0 - np.arange(cfg.RH, dtype=np.float64)))
    dec = np.zeros((cfg.RH, 2, 128, 128), f32)
    qdec = np.zeros((128, cfg.RH, 2), f32)
    kdec = np.zeros((128, cfg.RH, 2), f32)
    diff = (idx[None, :] - idx[:, None]).astype(np.float64)
    for h in range(cfg.RH):
        dec[h, 0] = np.where(causal_p, np.exp(lg[h] * diff), 0.0) / 16.0
        dec[h, 1] = np.where(causal_s, np.exp(lg[h] * diff), 0.0) / 16.0
        qdec[:, h, 0] = np.exp(lg[h] * (idx + 1.0))
        qdec[:, h, 1] = np.exp(lg[h] * ((idx % 8) + 1.0))
        kdec[:, h, 0] = np.exp(lg[h] * (127.0 - idx)) / 16.0
        kdec[:, h, 1] = np.exp(lg[h] * (7.0 - (idx % 8))) / 16.0
    tb["ret_dec"] = dec
    tb["ret_qdec"] = qdec
    tb["ret_kdec"] = kdec
    cfg.ret_cd = [(float(np.exp(lg[h] * 128.0)), float(np.exp(lg[h] * 8.0))) for h in range(cfg.RH)]
    mk = np.zeros((4, 128, 128), f32)
    mk[0] = np.where(causal_p, 0.0, -30000.0)
    mk[1] = np.where(causal_s, 0.0, -30000.0)
    mk[2] = causal_p.astype(f32)
    mk[3] = causal_s.astype(f32)
    tb["masks"] = mk
    qm = np.zeros((128, 16, 128), f32)
    for b in range(16):
        qm[:, b, b * 8:(b + 1) * 8] = 1.0
    tb["qmask"] = qm.astype(ml_dtypes.bfloat16)
    rm = np.zeros((128, 16), f32)
    rm[idx, grp] = 1.0
    tb["rowmask"] = rm
    rst = np.ones((2, 128, T), f32)
    rst[0, :, TP::8] = 0.0
    rst[1, :, 0:TP:128] = 0.0
    rst[1, :, TP::8] = 0.0
    tb["rst"] = rst
    return tb


class Builder:
    def __init__(self, cfg):
        self.cfg = cfg
        nc = bass.Bass("TRN2", target_bir_lowering=False)
        self.nc = nc
        self.P = Prog(nc)
        self.din = {}
        self.dout = {}

    def inp(self, name, shape, dt=F32):
        t = self.nc.dram_tensor(name, list(shape), dt, kind="ExternalInput").ap()
        self.din[name] = t
        return t

    def outp(self, name, shape, dt=F32):
        t = self.nc.dram_tensor(name, list(shape), dt, kind="ExternalOutput").ap()
        self.dout[name] = t
        return t

    def scratch(self, name, shape, dt=F32):
        return self.nc.dram_tensor(name, list(shape), dt, kind="Internal").ap()

    def sb(self, es, name, shape, dt):
        self._uid = getattr(self, "_uid", 0) + 1
        return es.enter_context(self.nc.sbuf_tensor("%s_%d" % (name, self._uid), list(shape), dt))

    def build(self):
        cfg, nc, P = self.cfg, self.nc, self.P
        D, KC, T, NT = cfg.D, cfg.KC, cfg.T, cfg.NT
        RH, MH, HH = cfg.RH, cfg.MH, cfg.HH
        self.xT = self.inp("xT", [NT, KC, 128, T])
        self.s_ret = self.inp("s_ret", [NT, 16, RH, 256, 256])
        self.s_mc = self.inp("s_mc", [NT, 16, MH, 256, 256])
        self.s_mn = self.inp("s_mn", [NT, 16, MH, 256])
        self.s_mm = self.inp("s_mm", [NT, MH, 16])
        self.s_hg = self.inp("s_hg", [NT, 16, HH, 128, 128])
        self.w_in_ab = self.inp("w_in_ab", [D, cfg.ABC])
        self.w_out_ab = self.inp("w_out_ab", [D, D])
        self.w_in_c = self.inp("w_in_c", [D, cfg.CC])
        self.w_out_c = self.inp("w_out_c", [D, D])
        self.w_gate = self.inp("w_gate", [2, D, cfg.DFF])
        self.w_up = self.inp("w_up", [2, D, cfg.DFF])
        self.w_down = self.inp("w_down", [2, cfg.DFF, D])
        self.nw = self.inp("nw", [128, 5, KC])
        self.mlw = self.inp("mlw", [128, MH * 2])
        self.hgw = self.inp("hgw", [128, 1])
        self.bif = self.inp("bif", [128, 2 * MH])
        self.lbt = self.inp("lbt", [128, 2, HH])
        self.cmask = self.inp("cmask", [128, 1])
        self.snd = self.scratch("snd", [128, 520])
        self.rcv = self.scratch("rcv", [256, 520])
        self.snd_c = self.scratch("snd_c", [128, 128])
        self.rcv_c = self.scratch("rcv_c", [256, 128])
        tb = _tables(cfg, 0)
        self.tbl = {}
        for k, v in tb.items():
            dt = BF16 if v.dtype == ml_dtypes.bfloat16 else F32
            self.tbl[k] = self.inp("tb_" + k, v.shape, dt)
        self.yT = self.outp("yT", [NT, KC, 128, T])
        self.o_ret_p = self.outp("o_ret_p", [RH, 256, 256])
        self.o_mc_p = self.outp("o_mc_p", [MH, 256, 256])
        self.o_mn_p = self.outp("o_mn_p", [MH, 256])
        self.o_mm_p = self.outp("o_mm_p", [1, MH])
        self.o_hg_p = self.outp("o_hg_p", [HH, 128, 128])
        self.o_ret_s = self.outp("o_ret_s", [NT, 16, RH, 256, 256])
        self.o_mc_s = self.outp("o_mc_s", [NT, 16, MH, 256, 256])
        self.o_mn_s = self.outp("o_mn_s", [NT, 16, MH, 256])
        self.o_mm_s = self.outp("o_mm_s", [NT, MH, 16])
        self.o_hg_s = self.outp("o_hg_s", [NT, 16, HH, 128, 128])
        self.rT = self.scratch("rT", [KC, 128, T])
        self.mT = self.scratch("mT", [KC, 128, T], BF16)

        with ExitStack() as es:
            self.es = es
            self.pb = [es.enter_context(nc.psum_tensor("pb%d" % i, [128, 512], F32)) for i in range(7)]
            self.pt = [es.enter_context(nc.psum_tensor("pt%d" % i, [128, 1024], BF16)) for i in range(1)]
            c = {}
            for k in ("ident_bf", "ones_bf"):
                c[k] = self.sb(es, "c_" + k, [128, 128], BF16)
            for k in ("ident_f", "ones_f"):
                c[k] = self.sb(es, "c_" + k, [128, 128], F32)
            c["masks"] = self.sb(es, "c_masks", [128, 4, 128], F32)
            c["qmask"] = self.sb(es, "c_qmask", [128, 16, 128], BF16)
            c["rowmask"] = self.sb(es, "c_rowmask", [128, 16], F32)
            c["ret_qdec"] = self.sb(es, "c_qdec", [128, RH, 2], F32)
            c["ret_kdec"] = self.sb(es, "c_kdec", [128, RH, 2], F32)
            c["nw"] = self.sb(es, "c_nw", [128, 5, KC], F32)
            c["mlw"] = self.sb(es, "c_mlw", [128, MH * 2], F32)
            c["hgw"] = self.sb(es, "c_hgw", [128, 1], F32)
            c["bif"] = self.sb(es, "c_bif", [128, 2 * MH], F32)
            c["lbt"] = self.sb(es, "c_lbt", [128, 2, HH], F32)
            c["lb"] = self.sb(es, "c_lb", [128, HH], F32)
            c["oml"] = self.sb(es, "c_oml", [128, HH], F32)
            c["eps"] = self.sb(es, "c_eps", [128, 1], F32)
            c["one"] = self.sb(es, "c_one", [128, 1], F32)
            c["cmask"] = self.sb(es, "c_cmask", [128, 1], F32)
            self.c = c
            init_evs = []
            ld = lambda k, src: init_evs.append(P.dma("sp", c[k][:], src, reads=(), writes=(("c", k),), semkey="init"))
            ld("ident_bf", self.tbl["ident_bf"])
            ld("ones_bf", self.tbl["ones_bf"])
            ld("ident_f", self.tbl["ident_f"])
            ld("ones_f", self.tbl["ones_f"])
            ld("masks", self.tbl["masks"].rearrange("m p t -> p m t"))
            ld("qmask", self.tbl["qmask"])
            ld("rowmask", self.tbl["rowmask"])
            ld("ret_qdec", self.tbl["ret_qdec"])
            ld("ret_kdec", self.tbl["ret_kdec"])
            ld("nw", self.nw)
            ld("mlw", self.mlw)
            ld("hgw", self.hgw)
            ld("bif", self.bif)
            ld("lbt", self.lbt)
            ld("cmask", self.cmask)
            P.op("dve", lambda e: e.memset(c["eps"][:], EPS), writes=(("c", "eps"),))
            P.op("dve", lambda e: e.memset(c["one"][:], 1.0), writes=(("c", "one"),))
            self.lower_bounds()
            self.xn = self.sb(es, "xn", [128, KC, T], BF16)
            self.NW = 2
            self.wb = [self.sb(es, "wb%d" % i, [128, KC, 256], BF16) for i in range(self.NW)]
            self.wi = 0
            self.rci = 0
            for ti in range(NT):
                self.tile(ti)
            P.finish()
        return nc

    def lower_bounds(self):
        P, c = self.P, self.c
        HH = self.cfg.HH
        P.op("dve", lambda e: e.tensor_tensor(out=c["lb"][:], in0=c["lbt"][:, 1, :], in1=c["lbt"][:, 0, :],
                                              op=ALU.subtract), reads=(("c", "lbt"),), writes=(("c", "lb"),))
        P.op("act", lambda e: e.activation(out=c["lb"][:], in_=c["lb"][:], func=AF.Sigmoid),
             reads=(("c", "lb"),), writes=(("c", "lb"),))
        P.op("dve", lambda e: e.tensor_scalar(out=c["oml"][:], in0=c["lb"][:], scalar1=-1.0, scalar2=1.0,
                                              op0=ALU.mult, op1=ALU.add),
             reads=(("c", "lb"),), writes=(("c", "oml"),))

    def sreg(self, j, n):
        return self.pb[6][:, 0:n] if j == 0 else self.pb[3][:, 132:132 + n]

    def skey(self, j):
        return ("pb", 6) if j == 0 else ("pb", 3)

    def wslot(self):
        i = self.wi
        self.wi = (self.wi + 1) % self.NW
        return i

    def load_w(self, W, r0, nk, c0, ncols):
        i = self.wslot()
        src = W[r0:r0 + nk * 128, c0:c0 + ncols].rearrange("(k p) n -> p k n", p=128)
        self.P.dma("pool", self.wb[i][:, 0:nk, 0:ncols], src, reads=(("coll",),), writes=(("wb", i),))
        return i

    def dense_fm_gen(self, wi, nk, cb, act, act_res, banks, consumer, slice_k=4):
        P = self.P
        wb = self.wb[wi]
        k0 = 0
        while k0 < nk:
            k1 = min(nk, k0 + slice_k)
            fns = []
            for k in range(k0, k1):
                for j in range(3):
                    def fn(e, k=k, j=j):
                        return e.matmul(self.pb[banks[j]][:, 0:384], lhsT=wb[:, k, cb * 128:(cb + 1) * 128],
                                        rhs=act(k)[:, j * 384:(j + 1) * 384], start=(k == 0), stop=(k == nk - 1))
                    fns.append(fn)
            P.group("pe", fns, reads=(("wb", wi),) + tuple(act_res), writes=tuple(("pb", b) for b in banks))
            k0 = k1
            if k0 < nk:
                yield
        consumer([self.pb[b][:, 0:384] for b in banks], [("pb", b) for b in banks])
        yield

    def dense_fm(self, wi, nk, cb, act, act_res, banks, consumer):
        save = self.P.in_fill
        self.P.in_fill = True
        try:
            for _ in self.dense_fm_gen(wi, nk, cb, act, act_res, banks, consumer, slice_k=nk):
                pass
        finally:
            self.P.in_fill = save

    def norm(self, src_dram, widx, out_fp32_dram=None):
        cfg, P, c = self.cfg, self.P, self.c
        KC, T = cfg.KC, cfg.T
        banks = (3, 4, 5)
        sqb = self.sq
        for k in range(KC):
            r = self.rci
            self.rci ^= 1
            P.dma("sp", self.rc[r][:], src_dram[k], reads=(("dram", "r", k),), writes=(("rc", r),))
            P.op("act", lambda e, r=r, k=k: e.activation(out=sqb[k % 2][:], in_=self.rc[r][:], func=AF.Square),
                 reads=(("rc", r),), writes=(("sq", k % 2),))
            fns = []
            for j in range(3):
                fns.append(lambda e, j=j, k=k: e.matmul(self.pb[banks[j]][:, 0:384], lhsT=c["ones_bf"][:],
                                                        rhs=sqb[k % 2][:, j * 384:(j + 1) * 384],
                                                        start=(k == 0), stop=(k == KC - 1)))
            P.group("pe", fns, reads=(("sq", k % 2), ("c", "ones_bf")), writes=tuple(("pb", b) for b in banks))
        for j in range(3):
            sl = slice(j * 384, (j + 1) * 384)
            P.op("act", lambda e, j=j, sl=sl: e.activation(out=self.rstd[:, sl], in_=self.pb[banks[j]][:, 0:384],
                                                          func=AF.Sqrt, scale=1.0 / cfg.D, bias=c["eps"][:]),
                 reads=(("pb", banks[j]), ("c", "eps")), writes=(("rstd",),))
        P.op("dve", lambda e: e.reciprocal(out=self.rstd[:], in_=self.rstd[:]), reads=(("rstd",),),
             writes=(("rstd",),))
        for k in range(KC):
            r = self.rci
            self.rci ^= 1
            P.dma("sp", self.rc[r][:], src_dram[k], reads=(("dram", "r", k),), writes=(("rc", r),))
            if out_fp32_dram is None:
                P.op("dve", lambda e, r=r, k=k: e.scalar_tensor_tensor(
                    out=self.xn[:, k, :], in0=self.rc[r][:], scalar=c["nw"][:, widx, k:k + 1], in1=self.rstd[:],
                    op0=ALU.mult, op1=ALU.mult), reads=(("rc", r), ("rstd",), ("c", "nw")), writes=(("xn", k),))
            else:
                P.op("dve", lambda e, r=r, k=k: e.scalar_tensor_tensor(
                    out=self.rc[r][:], in0=self.rc[r][:], scalar=c["nw"][:, widx, k:k + 1], in1=self.rstd[:],
                    op0=ALU.mult, op1=ALU.mult), reads=(("rc", r), ("rstd",), ("c", "nw")), writes=(("rc", r),))
                P.dma("sp", out_fp32_dram[k], self.rc[r][:], reads=(("rc", r),), writes=(("dram", "y", k),),
                      semkey=("rc", r))

    def add_residual(self, k, src_dram, ps, ps_res):
        P = self.P
        r = self.rci
        self.rci ^= 1
        P.dma("sp", self.rc[r][:], src_dram[k], reads=(("dram", "r", k),), writes=(("rc", r),))
        for j in range(3):
            sl = slice(j * 384, (j + 1) * 384)
            P.op("dve", lambda e, j=j, sl=sl, r=r: e.tensor_tensor(out=self.rc[r][:, sl], in0=ps[j],
                                                                   in1=self.rc[r][:, sl], op=ALU.add),
                 reads=(ps_res[j], ("rc", r)), writes=(("rc", r),))
        P.dma("sp", self.rT[k], self.rc[r][:], reads=(("rc", r),), writes=(("dram", "r", k),), semkey=("rc", r))

    def out_proj(self, W, src_dram):
        cfg, P = self.cfg, self.P
        KC = cfg.KC
        for k in range(KC):
            P.dma("sp", self.xn[:, k, :], self.mT[k], reads=(("dram", "m", k),), writes=(("xn", k),),
                  semkey=("xnload", k % 4))
        xn_res = tuple(("xn", k) for k in range(KC))
        for cb2 in range(KC // 2):
            wi = self.load_w(W, 0, KC, cb2 * 256, 256)
            for s in range(2):
                kk = cb2 * 2 + s
                self.dense_fm(wi, KC, s, lambda k: self.xn[:, k, :], xn_res, (0, 1, 2) if kk % 2 == 0 else (3, 4, 5),
                              lambda ps, pr, kk=kk: self.add_residual(kk, src_dram, ps, pr))

    def ffn(self, layer):
        cfg, P = self.cfg, self.P
        KC, T = cfg.KC, cfg.T
        xn_res = tuple(("xn", k) for k in range(KC))
        Wg, Wu, Wd = self.w_gate[layer], self.w_up[layer], self.w_down[layer]
        with ExitStack() as es:
            nfmax = max(n for _, n in cfg.fq)
            hT = self.sb(es, "hT", [128, nfmax, T], BF16)
            sil = [self.sb(es, "sil%d" % i, [128, 384], F32) for i in range(2)]
            cnt = [0]
            for (f0, nf) in cfg.fq:
                fb = 0
                while fb < nf:
                    nb = min(2, nf - fb)
                    wg = self.load_w(Wg, 0, KC, (f0 + fb) * 128, nb * 128)
                    wu = self.load_w(Wu, 0, KC, (f0 + fb) * 128, nb * 128)
                    for s in range(nb):
                        fl = fb + s
                        store = {}

                        def cons_g(ps, pr, store=store):
                            store["g"] = (ps, pr)

                        self.dense_fm(wg, KC, s, lambda k: self.xn[:, k, :], xn_res, (0, 1, 2), cons_g)

                        def cons_u(ps, pr, store=store, fl=fl):
                            gps, gpr = store["g"]
                            for j in range(3):
                                sl = slice(j * 384, (j + 1) * 384)
                                si = cnt[0] % 2
                                cnt[0] += 1
                                P.op("act", lambda e, j=j, si=si: e.activation(out=sil[si][:], in_=gps[j], func=AF.Silu),
                                     reads=(gpr[j],), writes=(("sil", si),))
                                P.op("dve", lambda e, j=j, si=si, sl=sl: e.tensor_tensor(
                                    out=hT[:, fl, sl], in0=ps[j], in1=sil[si][:], op=ALU.mult),
                                    reads=(pr[j], ("sil", si)), writes=(("hT", fl),))

                        self.dense_fm(wu, KC, s, lambda k: self.xn[:, k, :], xn_res, (3, 4, 5), cons_u)
                    fb += nb
                h_res = tuple(("hT", f) for f in range(nf))
                for cb2 in range(KC // 2):
                    wd = self.load_w(Wd, f0 * 128, nf, cb2 * 256, 256)
                    for s in range(2):
                        kk = cb2 * 2 + s
                        self.dense_fm(wd, nf, s, lambda k: hT[:, k, :], h_res, (0, 1, 2) if kk % 2 == 0 else (3, 4, 5),
                                      lambda ps, pr, kk=kk: self.add_residual(kk, self.rT, ps, pr))
            P.barrier()

    def nbufs(self):
        es = ExitStack()
        T = self.cfg.T
        self.rstd = self.sb(es, "rstd", [128, T], F32)
        self.rc = [self.sb(es, "rc%d" % i, [128, T], F32) for i in range(2)]
        self.sq = [self.sb(es, "sq%d" % i, [128, T], BF16) for i in range(2)]
        w3 = self.sb(es, "wb_extra", [128, self.cfg.KC, 256], BF16)
        self.wb.append(w3)
        self.NW = 3
        self.wi = 0

        def _drop():
            self.P.barrier()
            self.wb.pop()
            self.NW = 2
            self.wi = 0
        es.callback(_drop)
        return es

    def tile(self, ti):
        cfg, P = self.cfg, self.P
        self.ti = ti
        with self.nbufs():
            self.norm(self.xT[ti], 0)
            P.barrier()
        from_x = self.xT[ti]
        self.mixer_ab()
        with self.nbufs():
            self.out_proj(self.w_out_ab, from_x)
            self.norm(self.rT, 1)
            self.ffn(0)
            self.norm(self.rT, 2)
            P.barrier()
        self.mixer_c()
        with self.nbufs():
            self.out_proj(self.w_out_c, self.rT)
            self.norm(self.rT, 3)
            self.ffn(1)
            self.norm(self.rT, 4, out_fp32_dram=self.yT[ti])
            P.barrier()

    def mixer_ab(self):
        mixer_ab(self)

    def mixer_c(self):
        mixer_c(self)


def _zero_mT(b):
    P, cfg = b.P, b.cfg
    with ExitStack() as es:
        z = b.sb(es, "zst", [128, cfg.T], BF16)
        P.op("dve", lambda e: e.memset(z[:], 0.0), writes=(("zst",),))
        for k in range(cfg.KC):
            P.dma("sp", b.mT[k], z[:], reads=(("zst",),), writes=(("dram", "m", k),), semkey=("zst",))
        P.barrier()


STUB_AB = False
STUB_C = False


def mixer_ab(b):
    if STUB_AB:
        return _zero_mT(b)
    return mixer_ab_real(b)


def mixer_c(b):
    if STUB_C:
        return _zero_mT(b)
    return mixer_c_real(b)


def core_maps(cfg):
    pm, sm = [], []
    if cfg.xch:
        for c in range(cfg.ncores):
            pm.append((c // 2, c % 2))
            sm.append([c * 16])
        return pm, sm
    if cfg.ncores == 8:
        for c in range(8):
            pm.append(c if c < 4 else None)
            sm.append([None] * cfg.NT if c < 4 else [((c - 4) * cfg.NT + ti) * 16 for ti in range(cfg.NT)])
    else:
        pm.append(0)
        sm.append([ti * 16 for ti in range(cfg.NT)])
    return pm, sm


def make_in_maps(cfg, inp):
    T, TP, KC, NT = cfg.T, cfg.TP, cfg.KC, cfg.NT
    f32 = np.float32
    pm, sm = core_maps(cfg)
    tb = _tables(cfg, 0)
    shared = {}
    shared["w_in_ab"] = np.ascontiguousarray(inp["w_in_ab"][0])
    shared["w_out_ab"] = np.ascontiguousarray(inp["w_out_ab"][0])
    shared["w_in_c"] = np.ascontiguousarray(inp["w_in_c"][0])
    shared["w_out_c"] = np.ascontiguousarray(inp["w_out_c"][0])
    shared["w_gate"] = np.ascontiguousarray(inp["w_gate"])
    shared["w_up"] = np.ascontiguousarray(inp["w_up"])
    shared["w_down"] = np.ascontiguousarray(inp["w_down"])
    nws = np.stack([inp["norm_mix_w"][0], inp["norm_ffn_w"][0], inp["norm_mix_w"][1], inp["norm_ffn_w"][1],
                    inp["norm_final_w"]], 0)
    shared["nw"] = np.ascontiguousarray(nws.reshape(5, KC, 128).transpose(2, 0, 1))
    shared["mlw"] = np.ascontiguousarray(inp["ml_norm_w"][0].reshape(cfg.MH * 2, 128).T)
    shared["hgw"] = np.ascontiguousarray(inp["hg_norm_w"][0].reshape(128, 1))
    shared["bif"] = np.ascontiguousarray(np.broadcast_to(inp["b_if_ab"][0][None, :], (128, 2 * cfg.MH)))
    shared["lbt"] = np.ascontiguousarray(inp["lb_logits"].reshape(2, cfg.HH, 128).transpose(2, 0, 1))
    for k, v in tb.items():
        shared["tb_" + k] = v
    maps = []
    for c in range(cfg.ncores):
        m = dict(shared)
        if cfg.xch:
            m["tb_rot"] = _tables(cfg, c, pos0=(c % 2) * TP)["rot"]
            m["cmask"] = np.full((128, 1), float(c % 2), f32)
        else:
            m["cmask"] = np.zeros((128, 1), f32)
        x = np.zeros((NT, T, cfg.D), f32)
        s_ret = np.zeros((NT, 16, cfg.RH, 256, 256), f32)
        s_mc = np.zeros((NT, 16, cfg.MH, 256, 256), f32)
        s_mn = np.zeros((NT, 16, cfg.MH, 256), f32)
        s_mm = np.zeros((NT, cfg.MH, 16), f32)
        s_hg = np.zeros((NT, 16, cfg.HH, 128, 128), f32)
        for ti in range(NT):
            if cfg.xch:
                sq_, hf_ = pm[c]
                x[ti, :TP] = inp["x_prompt"][sq_, hf_ * TP:(hf_ + 1) * TP]
            elif pm[c] is not None:
                x[ti, :TP] = inp["x_prompt"][pm[c], ti * TP:(ti + 1) * TP]
            if sm[c][ti] is not None:
                b0 = sm[c][ti]
                x[ti, TP:] = inp["x_sample"][b0:b0 + 16].reshape(128, cfg.D)
                s_ret[ti] = inp["state_ret"][0, b0:b0 + 16]
                s_mc[ti] = inp["state_mlstm_C"][0, b0:b0 + 16]
                s_mn[ti] = inp["state_mlstm_n"][0, b0:b0 + 16]
                s_mm[ti] = inp["state_mlstm_m"][0, b0:b0 + 16].T
                s_hg[ti] = inp["state_hgrn"][0, b0:b0 + 16]
        m["xT"] = np.ascontiguousarray(x.transpose(0, 2, 1).reshape(NT, KC, 128, T))
        m["s_ret"], m["s_mc"], m["s_mn"], m["s_mm"], m["s_hg"] = s_ret, s_mc, s_mn, s_mm, s_hg
        maps.append(m)
    return maps


def assemble(cfg, res):
    T, TP, KC, NT, D = cfg.T, cfg.TP, cfg.KC, cfg.NT, cfg.D
    f32 = np.float32
    pm, sm = core_maps(cfg)
    B, DB = cfg.batch, cfg.dec_batch
    y_p = np.zeros((B, cfg.seq, D), f32)
    y_s = np.zeros((DB, 8, D), f32)
    ret_p = np.zeros((1, B, cfg.RH, 256, 256), f32)
    mc_p = np.zeros((1, B, cfg.MH, 256, 256), f32)
    mn_p = np.zeros((1, B, cfg.MH, 256), f32)
    mm_p = np.zeros((1, B, cfg.MH), f32)
    hg_p = np.zeros((1, B, cfg.HH, 128, 128), f32)
    ret_s = np.zeros((1, DB, cfg.RH, 256, 256), f32)
    mc_s = np.zeros((1, DB, cfg.MH, 256, 256), f32)
    mn_s = np.zeros((1, DB, cfg.MH, 256), f32)
    mm_s = np.zeros((1, DB, cfg.MH), f32)
    hg_s = np.zeros((1, DB, cfg.HH, 128, 128), f32)
    for c in range(cfg.ncores):
        r = res[c]
        y = np.asarray(r["yT"]).reshape(NT, D, T).transpose(0, 2, 1)
        for ti in range(NT):
            if cfg.xch:
                sq_, hf_ = pm[c]
                y_p[sq_, hf_ * TP:(hf_ + 1) * TP] = y[ti, :TP]
            elif pm[c] is not None:
                y_p[pm[c], ti * TP:(ti + 1) * TP] = y[ti, :TP]
            if sm[c][ti] is not None:
                b0 = sm[c][ti]
                y_s[b0:b0 + 16] = y[ti, TP:].reshape(16, 8, D)
                ret_s[0, b0:b0 + 16] = r["o_ret_s"][ti]
                mc_s[0, b0:b0 + 16] = r["o_mc_s"][ti]
                mn_s[0, b0:b0 + 16] = r["o_mn_s"][ti]
                mm_s[0, b0:b0 + 16] = np.asarray(r["o_mm_s"][ti]).T
                hg_s[0, b0:b0 + 16] = r["o_hg_s"][ti]
        psel = None
        if cfg.xch:
            if pm[c][1] == 1:
                psel = pm[c][0]
        elif pm[c] is not None:
            psel = pm[c]
        if psel is not None:
            ret_p[0, psel] = r["o_ret_p"]
            mc_p[0, psel] = r["o_mc_p"]
            mn_p[0, psel] = r["o_mn_p"]
            mm_p[0, psel] = np.asarray(r["o_mm_p"]).reshape(cfg.MH)
            hg_p[0, psel] = r["o_hg_p"]
    return (y_p, y_s, ret_p, mc_p, mn_p, mm_p, hg_p, ret_s, mc_s, mn_s, mm_s, hg_s)


def run_cfg(cfg, inp):
    b = Builder(cfg)
    nc = b.build()
    maps = make_in_maps(cfg, inp)
    res = run_bass_kernel_spmd(nc, maps, core_ids=list(range(cfg.ncores)))
    return assemble(cfg, res.results)


def kernel(**inputs):
    cfg = Cfg()
    inp = {k: np.asarray(v) for k, v in inputs.items()}
    return run_cfg(cfg, inp)


class _AB:
    pass


def _xchg(b, src_ap, snd, rcv, res):
    P, c, cfg = b.P, b.c, b.cfg
    key = ("dram", id(snd))
    P.dma("sp", snd, src_ap, reads=(res,), writes=(key + ("s",),), semkey=("xs",))
    P.coll(lambda g: g.collective_compute("AllGather", ALU.bypass, replica_groups=cfg.groups, ins=[snd], outs=[rcv]),
           reads=(key + ("s",),) + tuple(("wb", i) for i in range(len(b.wb))), writes=(key + ("r",), ("coll",)))
    P.dma("sp", src_ap, rcv[0:128, :], reads=(key + ("r",),), writes=(res,), semkey=("xr",))
    P.op("dve", lambda e: e.tensor_scalar(out=src_ap, in0=src_ap, scalar1=c["cmask"][:, 0:1], scalar2=None, op0=ALU.mult),
         reads=(res, ("c", "cmask")), writes=(res,))


def _proj_fm_gen(b, W, col0, evac, bankset):
    cfg = b.cfg
    KC = cfg.KC
    xn_res = tuple(("xn", k) for k in range(KC))
    wi = b.load_w(W, 0, KC, col0, 256)
    for s in range(2):
        banks = bankset[s % len(bankset)]

        def cons(ps, pr, s=s):
            for j in range(3):
                evac(s, j, ps[j], pr[j])

        yield from b.dense_fm_gen(wi, KC, s, lambda k: b.xn[:, k, :], xn_res, banks, cons)


def _proj_tm_gen(b, W, col0, ncols, dst, dst_res, dcol0=0):
    cfg, P = b.cfg, b.P
    KC = cfg.KC
    xn_res = tuple(("xn", k) for k in range(KC))
    wi = b.load_w(W, 0, KC, col0, ncols)
    for tb in range(9):
        bank = tb % 2
        fns = []
        for k in range(KC):
            fns.append(lambda e, k=k, tb=tb, bank=bank: e.matmul(
                b.pb[bank][:, 0:ncols], lhsT=b.xn[:, k, tb * 128:(tb + 1) * 128], rhs=b.wb[wi][:, k, 0:ncols],
                start=(k == 0), stop=(k == KC - 1)))
        P.group("pe", fns, reads=(("wb", wi),) + xn_res, writes=(("pb", bank),))
        P.op("act", lambda e, tb=tb, bank=bank: e.copy(out=dst[:, tb, dcol0:dcol0 + ncols], in_=b.pb[bank][:, 0:ncols]),
             reads=(("pb", bank),), writes=(dst_res,))
        yield


def _evac_to(b, dst, dres, func=None):
    P = b.P

    def ev(s, j, ps, pr):
        if func is None:
            P.op("act", lambda e: e.copy(out=dst[:, s, j * 384:(j + 1) * 384], in_=ps), reads=(pr,), writes=(dres,))
        else:
            P.op("act", lambda e: e.activation(out=dst[:, s, j * 384:(j + 1) * 384], in_=ps, func=func),
                 reads=(pr,), writes=(dres,))
    return ev


def _pipeline(b, n, proj_gen, rec):
    P = b.P
    _drain(b, proj_gen(0))
    for i in range(n):
        nxt = proj_gen(i + 1) if i + 1 < n else None
        P.filler = nxt
        rec(i)
        P.filler = None
        _drain(b, nxt)


def _drain(b, gen):
    if gen is None:
        return
    P = b.P
    save = P.in_fill
    P.in_fill = True
    try:
        for _ in gen:
            pass
    finally:
        P.in_fill = save


class _View:
    DB = ("qT", "kT", "gT", "v", "Q32", "F32")

    def __init__(self, base, par):
        self.__dict__["_base"] = base
        self.__dict__["_par"] = par

    def __getattr__(self, n):
        base, par = self.__dict__["_base"], self.__dict__["_par"]
        if n in _View.DB:
            return getattr(base, n + "_db")[par]
        if n.startswith("k_") and n[2:] in _View.DB:
            return (n[2:], par)
        return getattr(base, n)

    def __setattr__(self, n, v):
        setattr(self.__dict__["_base"], n, v)


def _transpose_pair(b, src_fn, src_res, pt_i, n=2):
    P, c = b.P, b.c
    fns = []
    for j in range(n):
        fns.append(lambda e, j=j: e.transpose(out=b.pt[0][:, pt_i * 256 + j * 128:pt_i * 256 + (j + 1) * 128], in_=src_fn(j),
                                              identity=c["ident_bf"][:]))
    P.group("pe", fns, reads=tuple(src_res) + (("c", "ident_bf"),), writes=(("pt", 0),))


def mixer_ab_real(b):
    cfg, P, c = b.cfg, b.P, b.c
    P.barrier()
    KC, T, TP = cfg.KC, cfg.T, cfg.TP
    RW, MW, RH, MH = cfg.RW, cfg.MW, cfg.RH, cfg.MH
    W = b.w_in_ab
    ti = b.ti
    with ExitStack() as es:
        A = _AB()
        A.raw = b.sb(es, "raw", [128, 2, T], F32)
        A.qT_db = [b.sb(es, "qT", [128, 2, T], BF16) for _ in range(2)]
        A.kT_db = [b.sb(es, "kT", [128, 2, T], BF16) for _ in range(2)]
        A.gT_db = [b.sb(es, "gT", [128, 2, T], BF16) for _ in range(2)]
        A.v_db = [b.sb(es, "v", [128, 9, 260], BF16) for _ in range(2)]
        A.rot = b.sb(es, "rot", [128, 2, T], F32)
        A.t1 = b.sb(es, "t1", [128, T], F32)
        A.t2 = b.sb(es, "t2", [128, T], F32)
        A.mst = b.sb(es, "mst", [128, 2, T], BF16)
        A.S = b.sb(es, "S", [128, 2, 260], F32)
        A.Sbf = b.sb(es, "Sbf", [128, 2, 260], BF16)
        A.Sst = b.sb(es, "Sst", [128, 4, 2, 260], F32)
        A.Sstb = b.sb(es, "Sstb", [128, 4, 2, 260], BF16)
        A.qmx = b.sb(es, "qmx", [128, 2, 4, 128], BF16)
        A.kdm = b.sb(es, "kdm", [128, 4, 256], BF16)
        A.dec = b.sb(es, "dec", [128, 2, 128], F32)
        A.attm = b.sb(es, "attm", [128, 128], BF16)
        A.osb = b.sb(es, "osb", [128, 260], F32)
        A.otot_db = [b.sb(es, "otot", [128, 260], F32) for _ in range(2)]
        A.junk = A.osb
        A.on = b.sb(es, "on", [128, 256], BF16)
        A.kd = b.sb(es, "kd", [128, 256], BF16)
        A.sm = b.sb(es, "sm", [128, 16], F32)
        A.cols = b.sb(es, "cols", [128, 4], F32)
        A.D = b.sb(es, "Dm", [128, 128], F32)
        A.nrow = b.sb(es, "nrow", [16, 256], F32)
        A.ncol = b.sb(es, "ncol", [128, 2, 16], F32)
        A.mrow = b.sb(es, "mrow", [128, 16], F32)
        A.qi = b.sb(es, "qi", [128, 2, 128], BF16)
        A.wrep = b.wb[1][:, 0:KC, 128:256]
        A.mend = b.sb(es, "mend", [128, 16], F32)
        for par in range(2):
            P.op("dve", lambda e, par=par: e.memset(A.v_db[par][:, :, 256:257], 1.0), writes=(("v1",),))
        P.op("dve", lambda e: e.memset(A.Sst[:], 0.0), writes=(("Sst",),))
        P.op("dve", lambda e: e.memset(A.S[:], 0.0), writes=(("S",),))
        P.dma("sp", A.rot[:], b.tbl["rot"][ti].rearrange("a p t -> p a t"), writes=(("rot",),))
        views = [_View(A, 0), _View(A, 1)]

        def proj_gen(i):
            if i < RH:
                return _ret_proj_gen(b, views[i % 2], i)
            return _ml_proj_gen(b, views[i % 2], i - RH)

        def rec(i):
            if i < RH:
                _ret_rec(b, views[i % 2], i)
            else:
                _ml_rec(b, views[i % 2], i - RH)

        _pipeline(b, RH + MH, proj_gen, rec)
        b.NW = 2
        b.wi = 0
        P.barrier()


def _store_mixed(b, A, hidx):
    P = b.P
    for j in range(2):
        P.dma("sp", b.mT[hidx * 2 + j], A.mst[:, j, :], reads=(("mst",),), writes=(("dram", "m", hidx * 2 + j),),
              semkey=("mst",))


def _ret_proj_gen(b, A, h):
    cfg, P, c = b.cfg, b.P, b.c
    T, TP, RW = cfg.T, cfg.TP, cfg.RW
    W = b.w_in_ab
    ti = b.ti
    sets = [(0, 1, 2)]

    def evac_raw(s, j, ps, pr):
        P.op("act", lambda e: e.copy(out=A.raw[:, s, j * 384:(j + 1) * 384], in_=ps), reads=(pr,), writes=(("raw",),))

    def rotary(dst, dres):
        x1, x2 = A.raw[:, 0, :], A.raw[:, 1, :]
        cs, sn = A.rot[:, 0, :], A.rot[:, 1, :]
        rr = (("raw",), ("rot",))
        P.op("dve", lambda e: e.tensor_tensor(out=A.t1[:], in0=x1, in1=cs, op=ALU.mult), reads=rr, writes=(("t1",),))
        P.op("dve", lambda e: e.tensor_tensor(out=A.t2[:], in0=x2, in1=sn, op=ALU.mult), reads=rr, writes=(("t2",),))
        P.op("dve", lambda e: e.tensor_tensor(out=dst[:, 0, :], in0=A.t1[:], in1=A.t2[:], op=ALU.subtract),
             reads=(("t1",), ("t2",)), writes=(dres,))
        P.op("dve", lambda e: e.tensor_tensor(out=A.t1[:], in0=x1, in1=sn, op=ALU.mult), reads=rr, writes=(("t1",),))
        P.op("dve", lambda e: e.tensor_tensor(out=A.t2[:], in0=x2, in1=cs, op=ALU.mult), reads=rr, writes=(("t2",),))
        P.op("dve", lambda e: e.tensor_tensor(out=dst[:, 1, :], in0=A.t1[:], in1=A.t2[:], op=ALU.add),
             reads=(("t1",), ("t2",)), writes=(dres,))

    yield from _proj_fm_gen(b, W, 0 * RW + h * 256, evac_raw, sets)
    rotary(A.qT, A.k_qT)
    yield from _proj_fm_gen(b, W, 1 * RW + h * 256, evac_raw, sets)
    rotary(A.kT, A.k_kT)
    yield from _proj_tm_gen(b, W, 2 * RW + h * 256, 256, A.v, A.k_v)

    def evac_g(s, j, ps, pr):
        P.op("act", lambda e: e.activation(out=A.gT[:, s, j * 384:(j + 1) * 384], in_=ps, func=AF.Silu),
             reads=(pr,), writes=(A.k_gT,))

    yield from _proj_fm_gen(b, W, 3 * RW + h * 256, evac_g, sets)


def _ret_rec(b, A, h):
    cfg, P, c = b.cfg, b.P, b.c
    T, TP, RW = cfg.T, cfg.TP, cfg.RW
    ti = b.ti
    P.dma("sp", A.dec[:], b.tbl["ret_dec"][h].rearrange("a p t -> p a t"), writes=(("dec",),))
    cdp, cds = cfg.ret_cd[h]
    if ti == 0:
        P.op("dve", lambda e: e.memset(A.S[:], 0.0), writes=(("S",),))
    else:
        P.dma("sp", A.S[:, :, 0:256], b.o_ret_p[h].rearrange("(j p) e -> p j e", p=128),
              reads=(("dram", "retp", h),), writes=(("S",),))
    if cfg.xch:
        for cidx in range(8):
            tok = slice(cidx * 128, (cidx + 1) * 128)
            _transpose_pair(b, lambda j: A.kT[:, j, tok], (A.k_kT,), 1)
            P.op("dve", lambda e: e.tensor_scalar(out=A.kd[:], in0=b.pt[0][:, 256:512], scalar1=c["ret_kdec"][:, h, 0:1],
                                                  scalar2=None, op0=ALU.mult),
                 reads=(("pt", 0), ("c", "ret_kdec")), writes=(("kd",),))
            for j in range(2):
                bank = j
                P.op("pe", lambda e, j=j, bank=bank: e.matmul(b.sreg(bank, 256), lhsT=A.kd[:, j * 128:(j + 1) * 128],
                                                              rhs=A.v[:, cidx, 0:256], start=True, stop=True),
                     reads=(("kd",), A.k_v), writes=(b.skey(bank),))
                P.op("dve", lambda e, j=j, bank=bank: e.scalar_tensor_tensor(
                    out=A.S[:, j, 0:256], in0=A.S[:, j, 0:256], scalar=cdp, in1=b.sreg(bank, 256),
                    op0=ALU.mult, op1=ALU.add), reads=(("S",), b.skey(bank)), writes=(("S",),))
        _xchg(b, A.S[:].rearrange("p j e -> p (j e)"), b.snd[:, :], b.rcv[:, :], ("S",))
    P.op("act", lambda e: e.copy(out=A.Sbf[:], in_=A.S[:]), reads=(("S",),), writes=(("Sbf",),))
    pending = None
    for cidx in range(9):
        smp = 1 if cidx == 8 else 0
        tok = slice(cidx * 128, (cidx + 1) * 128)
        otot, kot = A.otot_db[cidx % 2], ("otot", cidx % 2)
        P.group("pe", [lambda e, j=j: e.matmul(b.pb[3][:, 0:128], lhsT=A.kT[:, j, tok], rhs=A.qT[:, j, tok],
                                               start=(j == 0), stop=(j == 1)) for j in range(2)],
                reads=(A.k_kT, A.k_qT), writes=(("pb", 3),))
        P.op("dve", lambda e: e.tensor_tensor(out=A.attm[:], in0=b.pb[3][:, 0:128], in1=A.dec[:, smp, :], op=ALU.mult),
             reads=(("pb", 3), ("dec",)), writes=(("attm",),))
        P.op("pe", lambda e: e.matmul(b.pb[4][:, 0:256], lhsT=A.attm[:], rhs=A.v[:, cidx, 0:256], start=True, stop=True),
             reads=(("attm",), A.k_v), writes=(("pb", 4),))
        P.op("act", lambda e: e.copy(out=A.osb[:, 0:256], in_=b.pb[4][:, 0:256]), reads=(("pb", 4),), writes=(("osb",),))
        _transpose_pair(b, lambda j: A.kT[:, j, tok], (A.k_kT,), 1)
        P.op("dve", lambda e: e.tensor_scalar(out=A.kd[:], in0=b.pt[0][:, 256:512], scalar1=c["ret_kdec"][:, h, smp:smp + 1],
                                              scalar2=None, op0=ALU.mult),
             reads=(("pt", 0), ("c", "ret_kdec")), writes=(("kd",),))
        if not smp:
            P.group("pe", [lambda e, j=j: e.matmul(b.pb[5][:, 0:256], lhsT=A.qT[:, j, tok], rhs=A.Sbf[:, j, 0:256],
                                                   start=(j == 0), stop=(j == 1)) for j in range(2)],
                    reads=(A.k_qT, ("Sbf",)), writes=(("pb", 5),))
            for j in range(2):
                bank = j
                P.op("pe", lambda e, j=j, bank=bank: e.matmul(b.sreg(bank, 256), lhsT=A.kd[:, j * 128:(j + 1) * 128],
                                                              rhs=A.v[:, cidx, 0:256], start=True, stop=True),
                     reads=(("kd",), A.k_v), writes=(b.skey(bank),))
                P.op("dve", lambda e, j=j, bank=bank: e.scalar_tensor_tensor(
                    out=A.S[:, j, 0:256], in0=A.S[:, j, 0:256], scalar=cdp, in1=b.sreg(bank, 256),
                    op0=ALU.mult, op1=ALU.add), reads=(("S",), b.skey(bank)), writes=(("S",),))
            P.op("act", lambda e: e.copy(out=A.Sbf[:], in_=A.S[:]), reads=(("S",),), writes=(("Sbf",),))
        else:
            _sample_states(b, A, tok, b.s_ret[ti, :, h], b.o_ret_s[ti, :, h], 256, cds, None)
        P.op("dve", lambda e: e.scalar_tensor_tensor(out=otot[:, 0:256], in0=b.pb[5][:, 0:256],
                                                     scalar=c["ret_qdec"][:, h, smp:smp + 1], in1=A.osb[:, 0:256],
                                                     op0=ALU.mult, op1=ALU.add),
             reads=(("pb", 5), ("osb",), ("c", "ret_qdec")), writes=(kot,))
        if pending is not None:
            pending()
        pending = (lambda tok=tok, otot=otot, kot=kot: _norm_gate_store(b, A, tok, None, None, otot, kot))
    pending()
    if True:
        for j in range(2):
            P.dma("sp", b.o_ret_p[h, j * 128:(j + 1) * 128, :], A.S[:, j, 0:256], reads=(("S",),),
                  writes=(("dram", "retp", h),), semkey=("Sout",))
    _store_mixed(b, A, h)


def _norm_gate_store(b, A, tok, rec_col, wcol_fn, otot, kot):
    P, c = b.P, b.c
    P.op("act", lambda e: e.activation(out=A.junk[:, 0:256], in_=otot[:, 0:256], func=AF.Square,
                                       accum_out=A.sm[:, 0:1]),
         reads=(kot,), writes=(("osb",), ("sm", 0)))
    if rec_col is None:
        P.op("act", lambda e: e.activation(out=A.sm[:, 1:2], in_=A.sm[:, 0:1], func=AF.Sqrt, scale=1.0 / 256.0,
                                           bias=c["eps"][:]), reads=(("sm", 0), ("c", "eps")), writes=(("sm", 1),))
        P.op("dve", lambda e: e.reciprocal(out=A.sm[:, 2:3], in_=A.sm[:, 1:2]), reads=(("sm", 1),), writes=(("sm", 2),))
    else:
        P.op("dve", lambda e: e.tensor_tensor(out=A.sm[:, 3:4], in0=rec_col, in1=rec_col, op=ALU.mult),
             reads=(("sm", 8),), writes=(("sm", 3),))
        P.op("dve", lambda e: e.tensor_tensor(out=A.sm[:, 3:4], in0=A.sm[:, 3:4], in1=A.sm[:, 0:1], op=ALU.mult),
             reads=(("sm", 3), ("sm", 0)), writes=(("sm", 3),))
        P.op("act", lambda e: e.activation(out=A.sm[:, 1:2], in_=A.sm[:, 3:4], func=AF.Sqrt, scale=1.0 / 256.0,
                                           bias=c["eps"][:]), reads=(("sm", 3), ("c", "eps")), writes=(("sm", 1),))
        P.op("dve", lambda e: e.reciprocal(out=A.sm[:, 2:3], in_=A.sm[:, 1:2]), reads=(("sm", 1),), writes=(("sm", 2),))
        P.op("dve", lambda e: e.tensor_tensor(out=A.sm[:, 2:3], in0=A.sm[:, 2:3], in1=rec_col, op=ALU.mult),
             reads=(("sm", 2), ("sm", 8)), writes=(("sm", 2),))
    P.op("dve", lambda e: e.tensor_scalar(out=A.on[:], in0=otot[:, 0:256], scalar1=A.sm[:, 2:3], scalar2=None,
                                          op0=ALU.mult), reads=(kot, ("sm", 2)), writes=(("on",),))
    _transpose_pair(b, lambda j: A.on[:, j * 128:(j + 1) * 128], (("on",),), 0)
    for j in range(2):
        if wcol_fn is None:
            P.op("dve", lambda e, j=j: e.tensor_tensor(out=A.mst[:, j, tok], in0=b.pt[0][:, j * 128:(j + 1) * 128],
                                                       in1=A.gT[:, j, tok], op=ALU.mult),
                 reads=(("pt", 0), A.k_gT), writes=(("mst",),))
        else:
            P.op("dve", lambda e, j=j: e.scalar_tensor_tensor(out=A.mst[:, j, tok], in0=b.pt[0][:, j * 128:(j + 1) * 128],
                                                              scalar=wcol_fn(j), in1=A.gT[:, j, tok],
                                                              op0=ALU.mult, op1=ALU.mult),
                 reads=(("pt", 0), A.k_gT, ("c", "mlw")), writes=(("mst",),))


def _sample_states(b, A, tok, s_in, s_out, ncol, decay, ml):
    P, c = b.P, b.c
    qsrc = A.qT if ml is None else A.qi
    for g in range(4):
        for bb in range(4):
            P.dma("sp", A.Sst[:, bb, :, 0:256], s_in[g * 4 + bb].rearrange("(j p) e -> p j e", p=128),
                  writes=(("Sst",),))
        if ml is not None:
            for bb in range(4):
                for j in range(2):
                    P.op("act", lambda e, bb=bb, j=j: e.copy(out=A.Sst[:, bb, j, 256:257],
                                                            in_=A.ncol[:, j, g * 4 + bb:g * 4 + bb + 1]),
                         reads=(("ncol",), ("Sst",)), writes=(("Sst",),))
        P.op("act", lambda e: e.copy(out=A.Sstb[:], in_=A.Sst[:]), reads=(("Sst",),), writes=(("Sstb",),))
        for j in range(2):
            if ml is None:
                src = A.qT[:, j, tok]
            else:
                src = A.qi[:, j, :]
            P.op("dve", lambda e, j=j, src=src: e.tensor_tensor(
                out=A.qmx[:, j, :, :], in0=c["qmask"][:, g * 4:(g + 1) * 4, :],
                in1=src.unsqueeze(1).broadcast_to([128, 4, 128]), op=ALU.mult),
                reads=(A.k_qT, ("qi",), ("c", "qmask")), writes=(("qmx",),))
        fns = []
        for bb in range(4):
            for j in range(2):
                first = (g == 0 and bb == 0 and j == 0)
                last = (g == 3 and bb == 3 and j == 1)
                fns.append(lambda e, bb=bb, j=j, first=first, last=last: e.matmul(
                    b.pb[5][:, 0:ncol], lhsT=A.qmx[:, j, bb, :], rhs=A.Sstb[:, bb, j, 0:ncol], start=first, stop=last))
        P.group("pe", fns, reads=(("qmx",), ("Sstb",)), writes=(("pb", 5),))
        P.op("dve", lambda e: e.tensor_tensor(
            out=A.kdm[:], in0=A.kd[:].unsqueeze(1).broadcast_to([128, 4, 256]),
            in1=c["rowmask"][:, g * 4:(g + 1) * 4].unsqueeze(2).broadcast_to([128, 4, 256]), op=ALU.mult),
            reads=(("kd",), ("c", "rowmask")), writes=(("kdm",),))
        for bb in range(4):
            for j in range(2):
                bank = (bb * 2 + j) % 2
                P.op("pe", lambda e, bb=bb, j=j, bank=bank: e.matmul(
                    b.sreg(bank, ncol), lhsT=A.kdm[:, bb, j * 128:(j + 1) * 128], rhs=A.v[:, 8, 0:ncol],
                    start=True, stop=True), reads=(("kdm",), A.k_v, ("v1",)), writes=(b.skey(bank),))
                if ml is None:
                    P.op("dve", lambda e, bb=bb, j=j, bank=bank: e.scalar_tensor_tensor(
                        out=A.Sst[:, bb, j, 0:ncol], in0=A.Sst[:, bb, j, 0:ncol], scalar=decay,
                        in1=b.sreg(bank, ncol), op0=ALU.mult, op1=ALU.add),
                        reads=(("Sst",), b.skey(bank)), writes=(("Sst",),))
                else:
                    sq = g * 4 + bb
                    P.op("dve", lambda e, bb=bb, j=j, bank=bank, sq=sq: e.scalar_tensor_tensor(
                        out=A.Sst[:, bb, j, 0:ncol], in0=A.Sst[:, bb, j, 0:ncol], scalar=ml.carry_col(sq),
                        in1=b.sreg(bank, ncol), op0=ALU.mult, op1=ALU.add),
                        reads=(("Sst",), b.skey(bank), ("I",)), writes=(("Sst",),))
        for bb in range(4):
            P.dma("sp", s_out[g * 4 + bb].rearrange("(j p) e -> p j e", p=128), A.Sst[:, bb, :, 0:256],
                  reads=(("Sst",),), writes=(("dram", "sout"),), semkey=("Sst_out",))
        if ml is not None:
            for bb in range(4):
                for j in range(2):
                    P.op("act", lambda e, bb=bb, j=j: e.copy(out=A.ncol[:, j, g * 4 + bb:g * 4 + bb + 1],
                                                            in_=A.Sst[:, bb, j, 256:257]),
                         reads=(("Sst",),), writes=(("ncol",),))


class _ML:
    def __init__(self, A, TP):
        self.A, self.TP = A, TP

    def carry_col(self, sq):
        t = self.TP + 8 * sq + 7
        return self.A.I[:, t:t + 1]


def _ml_proj_gen(b, A, h):
    cfg, P, c = b.cfg, b.P, b.c
    T, TP, RW, MW, MH, KC = cfg.T, cfg.TP, cfg.RW, cfg.MW, cfg.MH, cfg.KC
    W = b.w_in_ab
    base = 4 * RW
    sets = [(0, 1, 2)]
    if h == 0:
        b.wi = 0
        b.NW = 1
        P.dma("pool", b.wb[1][:, 0:KC, 0:2 * MH],
              W[:, base + 4 * MW:base + 4 * MW + 2 * MH].rearrange("(k p) n -> p k n", p=128),
              reads=(("coll",),), writes=(("wb", 1),))
    yield from _proj_fm_gen(b, W, base + 0 * MW + h * 256, _evac_to(b, A.qT, A.k_qT), sets)
    yield from _proj_fm_gen(b, W, base + 1 * MW + h * 256, _evac_to(b, A.kT, A.k_kT), sets)
    yield from _proj_tm_gen(b, W, base + 2 * MW + h * 256, 256, A.v, A.k_v)
    yield from _proj_fm_gen(b, W, base + 3 * MW + h * 256, _evac_to(b, A.gT, A.k_gT, AF.Sigmoid), sets)


def _ml_rec(b, A, h):
    cfg, P, c = b.cfg, b.P, b.c
    T, TP, RW, MW, MH, KC = cfg.T, cfg.TP, cfg.RW, cfg.MW, cfg.MH, cfg.KC
    W = b.w_in_ab
    ti = b.ti
    base = 4 * RW
    xn_res = tuple(("xn", k) for k in range(KC))
    IG, F, M, BE, I, RST = A.raw[:, 0, :], A.raw[:, 1, :], A.rot[:, 0, :], A.rot[:, 1, :], A.t1[:], A.t2[:]
    A.I = A.t1
    rIG, rF, rM, rBE, rI = ("raw0",), ("raw1",), ("rot0",), ("rot1",), ("I",)
    if h == 0:
        P.barrier()
        P.dma("sp", A.t2[:], b.tbl["rst"][0], writes=(("rst",),))
        A.wgate = b.wb[1]
    for gi, (dst, dres) in enumerate(((IG, rIG), (BE, rBE))):
        col = gi * MH + h
        P.op("dve", lambda e, col=col: e.tensor_copy(
            out=A.wrep, in_=A.wgate[:, 0:KC, col:col + 1].broadcast_to([128, KC, 128])),
            reads=(("wb", 1),), writes=(("wrep",),))
        fns = []
        for k in range(KC):
            for j in range(3):
                fns.append(lambda e, k=k, j=j: e.matmul(b.pb[4 + j][:, 0:384], lhsT=A.wrep[:, k, :],
                                                        rhs=b.xn[:, k, j * 384:(j + 1) * 384],
                                                        start=(k == 0), stop=(k == KC - 1)))
        P.group("pe", fns, reads=(("wrep",),) + xn_res, writes=tuple(("pb", 4 + j) for j in range(3)))
        for j in range(3):
            P.op("dve", lambda e, j=j, dst=dst, col=col: e.tensor_scalar(
                out=dst[:, j * 384:(j + 1) * 384], in0=b.pb[4 + j][:, 0:384], scalar1=c["bif"][:, col:col + 1],
                scalar2=None, op0=ALU.add), reads=(("pb", 4 + j), ("c", "bif")), writes=(dres,))
    P.op("act", lambda e: e.activation(out=I, in_=BE, func=AF.Abs), reads=(rBE,), writes=(rI,))
    P.op("act", lambda e: e.activation(out=I, in_=I, func=AF.Exp, scale=-1.0), reads=(rI,), writes=(rI,))
    P.op("act", lambda e: e.activation(out=I, in_=I, func=AF.Ln, bias=c["one"][:], scale=1.0),
         reads=(rI, ("c", "one")), writes=(rI,))
    P.op("dve", lambda e: e.scalar_tensor_tensor(out=I, in0=BE, scalar=0.0, in1=I, op0=ALU.min, op1=ALU.subtract),
         reads=(rBE, rI), writes=(rI,))
    LF = I
    if ti == 0:
        P.op("dve", lambda e: e.memset(A.S[:], 0.0), writes=(("S",),))
        P.op("dve", lambda e: e.memset(A.sm[:, 10:11], 0.0), writes=(("sm", 10),))
    else:
        P.dma("sp", A.S[:, :, 0:256], b.o_mc_p[h].rearrange("(j p) e -> p j e", p=128),
              reads=(("dram", "mcp", h),), writes=(("S",),))
        P.dma("sp", A.S[:, :, 256], b.o_mn_p[h].rearrange("(j p) -> p j", p=128),
              reads=(("dram", "mnp", h),), writes=(("S",),), allow_slow_non_contiguous=True)
        P.dma("sp", A.sm[:, 10:11], b.o_mm_p[0:1, h:h + 1].broadcast_to([128, 1]) if False else
              b.o_mm_p[0, h:h + 1].partition_broadcast(128),
              reads=(("dram", "mmp", h),), writes=(("sm", 10),))
    P.op("act", lambda e: e.copy(out=A.Sbf[:], in_=A.S[:]), reads=(("S",),), writes=(("Sbf",),))
    P.dma("sp", A.mrow[:], b.s_mm[ti, h].partition_broadcast(128), writes=(("mrow",),))
    P.dma("sp", A.nrow[:], b.s_mn[ti, :, h, :], writes=(("nrow",),))
    for j in range(2):
        P.op("pe", lambda e, j=j: e.transpose(out=b.pb[5][:, 0:16], in_=A.nrow[:, j * 128:(j + 1) * 128],
                                              identity=c["ident_f"][0:16, 0:16]),
             reads=(("nrow",), ("c", "ident_f")), writes=(("pb", 5),))
        P.op("act", lambda e, j=j: e.copy(out=A.ncol[:, j, :], in_=b.pb[5][:, 0:16]), reads=(("pb", 5),),
             writes=(("ncol",),))
    P.op("dve", lambda e: e.tensor_tensor_scan(out=F, data0=RST, data1=LF, initial=0.0, op0=ALU.mult, op1=ALU.add),
         reads=(("rst",), rI), writes=(rF,))
    if cfg.xch:
        P.op("dve", lambda e: e.tensor_tensor_scan(out=M[:, 0:TP], data0=LF[:, 0:TP], data1=IG[:, 0:TP],
                                                   initial=0.0, op0=ALU.add, op1=ALU.max),
             reads=(rI, rIG), writes=(rM,))
        P.op("dve", lambda e: e.tensor_tensor(out=A.sm[:, 12:13], in0=F[:, TP - 1:TP], in1=M[:, TP - 1:TP], op=ALU.subtract),
             reads=(rF, rM), writes=(("sm", 12),))
        P.op("dve", lambda e: e.tensor_tensor(out=BE[:, 0:TP], in0=IG[:, 0:TP], in1=F[:, 0:TP], op=ALU.subtract),
             reads=(rIG, rF), writes=(rBE,))
        P.op("act", lambda e: e.activation(out=BE[:, 0:TP], in_=BE[:, 0:TP], func=AF.Exp, bias=A.sm[:, 12:13], scale=1.0),
             reads=(rBE, ("sm", 12)), writes=(rBE,))
        for cidx in range(8):
            tok = slice(cidx * 128, (cidx + 1) * 128)
            P.op("pe", lambda e: e.matmul(b.pb[3][:, 128:129], lhsT=BE[0:1, tok], rhs=c["one"][0:1, 0:1], start=True, stop=True),
                 reads=(rBE, ("c", "one")), writes=(("pb", 3),))
            P.op("act", lambda e: e.copy(out=A.cols[:, 0:1], in_=b.pb[3][:, 128:129]), reads=(("pb", 3),), writes=(("cols",),))
            _transpose_pair(b, lambda j: A.kT[:, j, tok], (A.k_kT,), 1)
            P.op("dve", lambda e: e.tensor_scalar(out=A.kd[:], in0=b.pt[0][:, 256:512], scalar1=A.cols[:, 0:1],
                                                  scalar2=1.0 / 16.0, op0=ALU.mult, op1=ALU.mult),
                 reads=(("pt", 0), ("cols",)), writes=(("kd",),))
            for j in range(2):
                bank = j
                P.op("pe", lambda e, j=j, bank=bank: e.matmul(b.sreg(bank, 257), lhsT=A.kd[:, j * 128:(j + 1) * 128],
                                                              rhs=A.v[:, cidx, 0:257], start=True, stop=True),
                     reads=(("kd",), A.k_v, ("v1",)), writes=(b.skey(bank),))
                P.op("dve", lambda e, j=j, bank=bank: e.tensor_tensor(out=A.S[:, j, 0:257], in0=A.S[:, j, 0:257],
                                                                      in1=b.sreg(bank, 257), op=ALU.add),
                     reads=(("S",), b.skey(bank)), writes=(("S",),))
        P.op("act", lambda e: e.copy(out=A.S[:, 0, 258:259], in_=M[:, TP - 1:TP]), reads=(rM,), writes=(("S",),))
        _xchg(b, A.S[:].rearrange("p j e -> p (j e)"), b.snd[:, :], b.rcv[:, :], ("S",))
        P.op("act", lambda e: e.copy(out=A.sm[:, 10:11], in_=A.S[:, 0, 258:259]), reads=(("S",),), writes=(("sm", 10),))
        P.op("act", lambda e: e.copy(out=A.Sbf[:], in_=A.S[:]), reads=(("S",),), writes=(("Sbf",),))
    P.op("dve", lambda e: e.tensor_tensor_scan(out=M[:, 0:TP], data0=LF[:, 0:TP], data1=IG[:, 0:TP],
                                               initial=A.sm[:, 10:11], op0=ALU.add, op1=ALU.max),
         reads=(rI, rIG, ("sm", 10)), writes=(rM,))
    for sq in range(16):
        sl = slice(TP + 8 * sq, TP + 8 * sq + 8)
        P.op("dve", lambda e, sl=sl, sq=sq: e.tensor_tensor_scan(out=M[:, sl], data0=LF[:, sl], data1=IG[:, sl],
                                                                initial=A.mrow[:, sq:sq + 1], op0=ALU.add, op1=ALU.max),
             reads=(rI, rIG, ("mrow",)), writes=(rM,))
    P.op("dve", lambda e: e.tensor_tensor(out=IG, in0=IG, in1=F, op=ALU.subtract), reads=(rIG, rF), writes=(rIG,))
    P.op("dve", lambda e: e.tensor_tensor(out=F, in0=F, in1=M, op=ALU.subtract), reads=(rF, rM), writes=(rF,))
    P.op("act", lambda e: e.copy(out=A.sm[:, 11:12], in_=M[:, TP - 1:TP]), reads=(rM,), writes=(("sm", 11),))
    P.op("act", lambda e: e.copy(out=A.mend[:], in_=M[:, TP:T].rearrange("p (b t) -> p b t", t=8)[:, :, 7]),
         reads=(rM,), writes=(("mend",),))
    P.op("act", lambda e: e.activation(out=M, in_=M, func=AF.Exp, scale=-1.0), reads=(rM,), writes=(rM,))
    P.op("dve", lambda e: e.tensor_tensor(
        out=BE[:, 0:TP].rearrange("p (c t) -> p c t", t=128), in0=IG[:, 0:TP].rearrange("p (c t) -> p c t", t=128),
        in1=F[:, 0:TP].rearrange("p (c t) -> p c t", t=128)[:, :, 127:128].broadcast_to([128, 8, 128]), op=ALU.add),
        reads=(rIG, rF), writes=(rBE,))
    P.op("dve", lambda e: e.tensor_tensor(
        out=BE[:, TP:T].rearrange("p (c t) -> p c t", t=8), in0=IG[:, TP:T].rearrange("p (c t) -> p c t", t=8),
        in1=F[:, TP:T].rearrange("p (c t) -> p c t", t=8)[:, :, 7:8].broadcast_to([128, 16, 8]), op=ALU.add),
        reads=(rIG, rF), writes=(rBE,))
    P.op("act", lambda e: e.activation(out=BE, in_=BE, func=AF.Exp), reads=(rBE,), writes=(rBE,))
    P.op("dve", lambda e: e.tensor_scalar(out=I[:, 0:128], in0=F[:, 0:128], scalar1=A.sm[:, 10:11], scalar2=None,
                                          op0=ALU.add), reads=(rF, ("sm", 10)), writes=(rI,))
    for cc in range(1, 8):
        P.op("dve", lambda e, cc=cc: e.tensor_scalar(out=I[:, cc * 128:(cc + 1) * 128], in0=F[:, cc * 128:(cc + 1) * 128],
                                                     scalar1=F[:, cc * 128 - 1:cc * 128], scalar2=None, op0=ALU.subtract),
             reads=(rF,), writes=(rI,))
    P.op("dve", lambda e: e.tensor_tensor(
        out=I[:, TP:T].rearrange("p (c t) -> p c t", t=8), in0=F[:, TP:T].rearrange("p (c t) -> p c t", t=8),
        in1=A.mrow[:].unsqueeze(2).broadcast_to([128, 16, 8]), op=ALU.add), reads=(rF, ("mrow",)), writes=(rI,))
    P.op("act", lambda e: e.activation(out=I, in_=I, func=AF.Exp), reads=(rI,), writes=(rI,))
    ml = _ML(A, TP)
    pending = None
    for cidx in range(9):
        smp = 1 if cidx == 8 else 0
        tok = slice(cidx * 128, (cidx + 1) * 128)
        otot, kot = A.otot_db[cidx % 2], ("otot", cidx % 2)
        fns = []
        for i, src in enumerate((IG, BE, M)):
            fns.append(lambda e, i=i, src=src: e.matmul(b.pb[3][:, 128 + i:129 + i], lhsT=src[0:1, tok], rhs=c["one"][0:1, 0:1],
                                                        start=True, stop=True))
        P.group("pe", fns, reads=(rIG, rBE, rM, ("c", "one")), writes=(("pb", 3),))
        P.op("act", lambda e: e.copy(out=A.cols[:, 0:3], in_=b.pb[3][:, 128:131]), reads=(("pb", 3),), writes=(("cols",),))
        P.group("pe", [lambda e, j=j: e.matmul(b.pb[3][:, 0:128], lhsT=A.kT[:, j, tok], rhs=A.qT[:, j, tok],
                                               start=(j == 0), stop=(j == 1)) for j in range(2)],
                reads=(A.k_kT, A.k_qT), writes=(("pb", 3),))
        P.op("dve", lambda e: e.tensor_tensor(out=A.D[:], in0=F[:, tok], in1=c["masks"][:, smp, :], op=ALU.add),
             reads=(rF, ("c", "masks")), writes=(("D",),))
        P.op("act", lambda e: e.activation(out=A.D[:], in_=A.D[:], func=AF.Exp, bias=A.cols[:, 0:1], scale=1.0),
             reads=(("D",), ("cols",)), writes=(("D",),))
        P.op("dve", lambda e: e.scalar_tensor_tensor(out=A.attm[:], in0=b.pb[3][:, 0:128], scalar=1.0 / 16.0, in1=A.D[:],
                                                     op0=ALU.mult, op1=ALU.mult),
             reads=(("pb", 3), ("D",)), writes=(("attm",),))
        P.op("pe", lambda e: e.matmul(b.pb[4][:, 0:257], lhsT=A.attm[:], rhs=A.v[:, cidx, 0:257], start=True, stop=True),
             reads=(("attm",), A.k_v, ("v1",)), writes=(("pb", 4),))
        P.op("act", lambda e: e.copy(out=A.osb[:, 0:257], in_=b.pb[4][:, 0:257]), reads=(("pb", 4),), writes=(("osb",),))
        for j in range(2):
            P.op("dve", lambda e, j=j: e.tensor_tensor(out=A.qi[:, j, :], in0=A.qT[:, j, tok], in1=I[:, tok], op=ALU.mult),
                 reads=(A.k_qT, rI), writes=(("qi",),))
        _transpose_pair(b, lambda j: A.kT[:, j, tok], (A.k_kT,), 1)
        P.op("dve", lambda e: e.tensor_scalar(out=A.kd[:], in0=b.pt[0][:, 256:512], scalar1=A.cols[:, 1:2],
                                              scalar2=1.0 / 16.0, op0=ALU.mult, op1=ALU.mult),
             reads=(("pt", 0), ("cols",)), writes=(("kd",),))
        if not smp:
            P.group("pe", [lambda e, j=j: e.matmul(b.pb[5][:, 0:257], lhsT=A.qi[:, j, :], rhs=A.Sbf[:, j, 0:257],
                                                   start=(j == 0), stop=(j == 1)) for j in range(2)],
                    reads=(("qi",), ("Sbf",)), writes=(("pb", 5),))
            end = cidx * 128 + 127
            for j in range(2):
                bank = j
                P.op("pe", lambda e, j=j, bank=bank: e.matmul(b.sreg(bank, 257), lhsT=A.kd[:, j * 128:(j + 1) * 128],
                                                              rhs=A.v[:, cidx, 0:257], start=True, stop=True),
                     reads=(("kd",), A.k_v, ("v1",)), writes=(b.skey(bank),))
                P.op("dve", lambda e, j=j, bank=bank: e.scalar_tensor_tensor(
                    out=A.S[:, j, 0:257], in0=A.S[:, j, 0:257], scalar=I[:, end:end + 1], in1=b.sreg(bank, 257),
                    op0=ALU.mult, op1=ALU.add), reads=(("S",), b.skey(bank), rI), writes=(("S",),))
            P.op("act", lambda e: e.copy(out=A.Sbf[:], in_=A.S[:]), reads=(("S",),), writes=(("Sbf",),))
        else:
            _sample_states(b, A, tok, b.s_mc[ti, :, h], b.o_mc_s[ti, :, h], 257, None, ml)
        P.op("dve", lambda e: e.tensor_tensor(out=otot[:, 0:257], in0=b.pb[5][:, 0:257], in1=A.osb[:, 0:257], op=ALU.add),
             reads=(("pb", 5), ("osb",)), writes=(kot,))
        P.op("act", lambda e: e.copy(out=otot[:, 258:259], in_=A.cols[:, 2:3]), reads=(("cols",), kot), writes=(kot,))

        def back(tok=tok, otot=otot, kot=kot):
            P.op("act", lambda e: e.activation(out=A.sm[:, 4:5], in_=otot[:, 256:257], func=AF.Abs),
                 reads=(kot,), writes=(("sm", 4),))
            P.op("dve", lambda e: e.tensor_tensor(out=A.sm[:, 5:6], in0=A.sm[:, 4:5], in1=otot[:, 258:259], op=ALU.max),
                 reads=(("sm", 4), kot), writes=(("sm", 5),))
            P.op("dve", lambda e: e.reciprocal(out=A.sm[:, 8:9], in_=A.sm[:, 5:6]), reads=(("sm", 5),), writes=(("sm", 8),))
            _norm_gate_store(b, A, tok, A.sm[:, 8:9], lambda j: c["mlw"][:, h * 2 + j:h * 2 + j + 1], otot, kot)

        if pending is not None:
            pending()
        pending = back
    pending()
    for j in range(2):
        P.dma("sp", b.o_mc_p[h, j * 128:(j + 1) * 128, :], A.S[:, j, 0:256], reads=(("S",),),
              writes=(("dram", "mcp", h),), semkey=("Sout",))
    P.dma("sp", b.o_mn_p[h].rearrange("(j p) -> p j", p=128), A.S[:, :, 256], reads=(("S",),),
          writes=(("dram", "mnp", h),), semkey=("Sout",), allow_slow_non_contiguous=True)
    P.dma("sp", b.o_mm_p[0:1, h:h + 1], A.sm[0:1, 11:12], reads=(("sm", 11),), writes=(("dram", "mmp", h),),
          semkey=("Sout",))
    P.dma("sp", b.o_mm_s[ti, h:h + 1, :], A.mend[0:1, :], reads=(("mend",),), writes=(("dram", "mms"),),
          semkey=("Sout",))
    for j in range(2):
        P.op("pe", lambda e, j=j: e.transpose(out=b.pb[5][0:16, 0:128], in_=A.ncol[:, j, :], identity=c["ident_f"][:]),
             reads=(("ncol",), ("c", "ident_f")), writes=(("pb", 5),))
        P.op("act", lambda e, j=j: e.copy(out=A.nrow[:, j * 128:(j + 1) * 128], in_=b.pb[5][0:16, 0:128]),
             reads=(("pb", 5),), writes=(("nrow",),))
    P.dma("sp", b.o_mn_s[ti, :, h, :], A.nrow[:], reads=(("nrow",),), writes=(("dram", "mns"),), semkey=("Sout",))
    _store_mixed(b, A, cfg.RH + h)


def mixer_c_real(b):
    cfg, P, c = b.cfg, b.P, b.c
    P.barrier()
    KC, T, TP, D, HH = cfg.KC, cfg.T, cfg.TP, cfg.D, cfg.HH
    W = b.w_in_c
    ti = b.ti
    sets = [(0, 1, 2)]
    with ExitStack() as es:
        A = _AB()
        A.Q32_db = [b.sb(es, "Q32", [128, 2, T], BF16) for _ in range(2)]
        A.F32_db = [b.sb(es, "F32", [128, 2, T], F32) for _ in range(2)]
        A.LN = b.sb(es, "LN", [128, T], F32)
        A.G = b.sb(es, "G", [128, T], F32)
        A.E = b.sb(es, "E", [128, T], F32)
        A.RST = b.sb(es, "RST", [128, T], F32)
        A.qT_db = [b.sb(es, "qT", [128, 2, T], BF16)] * 2
        A.kT_db = [b.sb(es, "kT", [128, 2, T], BF16)] * 2
        A.gT_db = [b.sb(es, "gT", [128, 2, T], BF16) for _ in range(2)]
        A.v_db = [b.sb(es, "v", [128, 9, 256], BF16) for _ in range(2)]
        A.mst = b.sb(es, "mst", [128, 2, T], BF16)
        A.S = b.sb(es, "S", [128, 128], F32)
        A.Sp = b.sb(es, "Sp", [128, 128], F32)
        A.Sbf = b.sb(es, "Sbf", [128, 128], BF16)
        A.Sst = b.sb(es, "Sst", [128, 8, 128], F32)
        A.Sstb = b.sb(es, "Sstb", [128, 8, 128], BF16)
        A.qmx = b.sb(es, "qmx", [128, 8, 128], BF16)
        A.kdm = b.sb(es, "kdm", [128, 8, 128], BF16)
        A.attm = b.sb(es, "attm", [128, 128], BF16)
        A.osb = b.sb(es, "osb", [128, 128], F32)
        A.otot_db = [b.sb(es, "otot", [128, 128], F32) for _ in range(2)]
        A.junk = b.sb(es, "junk", [128, 128], F32)
        A.on = b.sb(es, "on", [128, 128], BF16)
        A.kd = b.sb(es, "kd", [128, 128], BF16)
        A.sm = b.sb(es, "sm", [128, 40], F32)
        A.ss = b.sb(es, "ss", [128, 4], F32)
        P.dma("sp", A.RST[:], b.tbl["rst"][1], writes=(("rst",),))
        views = [_View(A, 0), _View(A, 1)]

        def proj_gen(hp):
            V = views[hp % 2]
            yield from _proj_fm_gen(b, W, 0 * D + hp * 256, _evac_to(b, V.Q32, V.k_Q32, AF.Silu), sets)
            yield from _proj_fm_gen(b, W, 1 * D + hp * 256, _evac_to(b, V.F32, V.k_F32, AF.Sigmoid), sets)
            yield from _proj_tm_gen(b, W, 2 * D + hp * 256, 256, V.v, V.k_v)
            yield from _proj_fm_gen(b, W, 3 * D + hp * 256, _evac_to(b, V.gT, V.k_gT, AF.Silu), sets)

        def rec(hp):
            V = views[hp % 2]
            for s in range(2):
                _hg_head(b, V, hp * 2 + s, s)
            for j in range(2):
                P.dma("sp", b.mT[hp * 2 + j], A.mst[:, j, :], reads=(("mst",),), writes=(("dram", "m", hp * 2 + j),),
                      semkey=("mst",))

        _pipeline(b, HH // 2, proj_gen, rec)
        P.barrier()


def _hg_head(b, A, hh, s):
    cfg, P, c = b.cfg, b.P, b.c
    T, TP = cfg.T, cfg.TP
    ti = b.ti
    FG = A.F32[:, s, :]
    Q = A.Q32[:, s, :]
    rFG, rQ = A.k_F32, A.k_Q32
    P.op("dve", lambda e: e.tensor_scalar(out=FG, in0=FG, scalar1=c["oml"][:, hh:hh + 1], scalar2=c["lb"][:, hh:hh + 1],
                                          op0=ALU.mult, op1=ALU.add), reads=(rFG, ("c", "oml"), ("c", "lb")), writes=(rFG,))
    P.op("act", lambda e: e.activation(out=A.LN[:], in_=FG, func=AF.Ln), reads=(rFG,), writes=(("LN",),))
    P.op("dve", lambda e: e.tensor_tensor_scan(out=A.G[:], data0=A.RST[:], data1=A.LN[:], initial=0.0,
                                               op0=ALU.mult, op1=ALU.add), reads=(("rst",), ("LN",)), writes=(("G",),))
    P.op("dve", lambda e: e.tensor_scalar(out=FG, in0=FG, scalar1=-1.0, scalar2=1.0, op0=ALU.mult, op1=ALU.add),
         reads=(rFG,), writes=(rFG,))
    for cc in range(8):
        P.op("dve", lambda e, cc=cc: e.tensor_scalar(out=A.sm[:, cc:cc + 1], in0=A.G[:, cc * 128 + 63:cc * 128 + 64],
                                                     scalar1=-1.0, scalar2=None, op0=ALU.mult),
             reads=(("G",),), writes=(("sm", "a"),))
    for cc in range(8):
        tok = slice(cc * 128, (cc + 1) * 128)
        P.op("act", lambda e, cc=cc, tok=tok: e.activation(out=A.E[:, tok], in_=A.G[:, tok], func=AF.Exp,
                                                          bias=A.sm[:, cc:cc + 1], scale=1.0),
             reads=(("G",), ("sm", "a")), writes=(("E",),))
    P.op("act", lambda e: e.activation(out=A.E[:, TP:T], in_=A.G[:, TP:T], func=AF.Exp), reads=(("G",),), writes=(("E",),))
    P.op("dve", lambda e: e.tensor_tensor(out=A.qT[:, s, :], in0=Q, in1=A.E[:], op=ALU.mult), reads=(rQ, ("E",)),
         writes=(A.k_qT,))
    P.op("act", lambda e: e.copy(out=A.sm[:, 8:16], in_=A.E[:, 0:TP].rearrange("p (c t) -> p c t", t=128)[:, :, 127]),
         reads=(("E",),), writes=(("sm", "b"),))
    P.op("act", lambda e: e.copy(out=A.sm[:, 16:32], in_=A.E[:, TP:T].rearrange("p (c t) -> p c t", t=8)[:, :, 7]),
         reads=(("E",),), writes=(("sm", "b"),))
    P.op("act", lambda e: e.activation(out=A.sm[:, 32:40], in_=A.sm[:, 0:8], func=AF.Exp, scale=-1.0),
         reads=(("sm", "a"),), writes=(("sm", "c"),))
    P.op("dve", lambda e: e.reciprocal(out=A.E[:], in_=A.E[:]), reads=(("E",),), writes=(("E",),))
    P.op("dve", lambda e: e.tensor_tensor(out=A.kT[:, s, :], in0=FG, in1=A.E[:], op=ALU.mult), reads=(rFG, ("E",)),
         writes=(A.k_kT,))
    if ti == 0:
        P.op("dve", lambda e: e.memset(A.S[:], 0.0), writes=(("S",),))
    else:
        P.dma("sp", A.S[:], b.o_hg_p[hh], reads=(("dram", "hgp", hh),), writes=(("S",),))
    vs = slice(s * 128, (s + 1) * 128)
    if cfg.xch:
        for cidx in range(8):
            tok = slice(cidx * 128, (cidx + 1) * 128)
            P.op("pe", lambda e: e.transpose(out=b.pt[0][:, 256:384], in_=A.kT[:, s, tok], identity=c["ident_bf"][:]),
                 reads=(A.k_kT, ("c", "ident_bf")), writes=(("pt", 0),))
            P.op("act", lambda e: e.copy(out=A.kd[:], in_=b.pt[0][:, 256:384]), reads=(("pt", 0),), writes=(("kd",),))
            P.op("dve", lambda e: e.tensor_scalar(out=A.Sp[:], in0=A.S[:], scalar1=A.sm[:, 32 + cidx:33 + cidx], scalar2=None,
                                                  op0=ALU.mult), reads=(("S",), ("sm", "c")), writes=(("Sp",),))
            P.op("pe", lambda e: e.matmul(b.sreg(0, 128), lhsT=A.kd[:], rhs=A.v[:, cidx, vs], start=True, stop=True),
                 reads=(("kd",), A.k_v), writes=(b.skey(0),))
            P.op("dve", lambda e: e.tensor_tensor(out=A.Sp[:], in0=b.sreg(0, 128), in1=A.Sp[:], op=ALU.add),
                 reads=(b.skey(0), ("Sp",)), writes=(("Sp",),))
            P.op("dve", lambda e: e.tensor_scalar(out=A.S[:], in0=A.Sp[:], scalar1=A.sm[:, 8 + cidx:9 + cidx], scalar2=None,
                                                  op0=ALU.mult), reads=(("Sp",), ("sm", "b")), writes=(("S",),))
        _xchg(b, A.S[:], b.snd_c[:, :], b.rcv_c[:, :], ("S",))
    pending = None
    for cidx in range(9):
        smp = 1 if cidx == 8 else 0
        tok = slice(cidx * 128, (cidx + 1) * 128)
        otot, kot = A.otot_db[cidx % 2], ("otot", cidx % 2)
        P.op("pe", lambda e: e.matmul(b.pb[3][:, 0:128], lhsT=A.kT[:, s, tok], rhs=A.qT[:, s, tok], start=True, stop=True),
             reads=(A.k_kT, A.k_qT), writes=(("pb", 3),))
        P.op("dve", lambda e: e.tensor_tensor(out=A.attm[:], in0=b.pb[3][:, 0:128], in1=c["masks"][:, 2 + smp, :], op=ALU.mult),
             reads=(("pb", 3), ("c", "masks")), writes=(("attm",),))
        P.op("pe", lambda e: e.matmul(b.pb[4][:, 0:128], lhsT=A.attm[:], rhs=A.v[:, cidx, vs], start=True, stop=True),
             reads=(("attm",), A.k_v), writes=(("pb", 4),))
        P.op("act", lambda e: e.copy(out=A.osb[:], in_=b.pb[4][:, 0:128]), reads=(("pb", 4),), writes=(("osb",),))
        P.op("pe", lambda e: e.transpose(out=b.pt[0][:, 256:384], in_=A.kT[:, s, tok], identity=c["ident_bf"][:]),
             reads=(A.k_kT, ("c", "ident_bf")), writes=(("pt", 0),))
        P.op("act", lambda e: e.copy(out=A.kd[:], in_=b.pt[0][:, 256:384]), reads=(("pt", 0),), writes=(("kd",),))
        if not smp:
            P.op("dve", lambda e: e.tensor_scalar(out=A.Sp[:], in0=A.S[:], scalar1=A.sm[:, 32 + cidx:33 + cidx], scalar2=None,
                                                  op0=ALU.mult), reads=(("S",), ("sm", "c")), writes=(("Sp",),))
            P.op("act", lambda e: e.copy(out=A.Sbf[:], in_=A.Sp[:]), reads=(("Sp",),), writes=(("Sbf",),))
            P.op("pe", lambda e: e.matmul(b.pb[5][:, 0:128], lhsT=A.qT[:, s, tok], rhs=A.Sbf[:], start=True, stop=True),
                 reads=(A.k_qT, ("Sbf",)), writes=(("pb", 5),))
            P.op("pe", lambda e: e.matmul(b.sreg(0, 128), lhsT=A.kd[:], rhs=A.v[:, cidx, vs], start=True, stop=True),
                 reads=(("kd",), A.k_v), writes=(b.skey(0),))
            P.op("dve", lambda e: e.tensor_tensor(out=A.Sp[:], in0=b.sreg(0, 128), in1=A.Sp[:], op=ALU.add),
                 reads=(b.skey(0), ("Sp",)), writes=(("Sp",),))
            P.op("dve", lambda e: e.tensor_scalar(out=A.S[:], in0=A.Sp[:], scalar1=A.sm[:, 8 + cidx:9 + cidx], scalar2=None,
                                                  op0=ALU.mult), reads=(("Sp",), ("sm", "b")), writes=(("S",),))
        else:
            for g in range(2):
                P.dma("sp", A.Sst[:], b.s_hg[ti, g * 8:(g + 1) * 8, hh].rearrange("b p e -> p b e"), writes=(("Sst",),))
                P.op("act", lambda e: e.copy(out=A.Sstb[:], in_=A.Sst[:]), reads=(("Sst",),), writes=(("Sstb",),))
                P.op("dve", lambda e, g=g: e.tensor_tensor(
                    out=A.qmx[:], in0=c["qmask"][:, g * 8:(g + 1) * 8, :],
                    in1=A.qT[:, s, tok].unsqueeze(1).broadcast_to([128, 8, 128]), op=ALU.mult),
                    reads=(A.k_qT, ("c", "qmask")), writes=(("qmx",),))
                fns = []
                for bb in range(8):
                    fns.append(lambda e, bb=bb, g=g: e.matmul(b.pb[5][:, 0:128], lhsT=A.qmx[:, bb, :], rhs=A.Sstb[:, bb, :],
                                                              start=(g == 0 and bb == 0), stop=(g == 1 and bb == 7)))
                P.group("pe", fns, reads=(("qmx",), ("Sstb",)), writes=(("pb", 5),))
                P.op("dve", lambda e, g=g: e.tensor_tensor(
                    out=A.kdm[:], in0=A.kd[:].unsqueeze(1).broadcast_to([128, 8, 128]),
                    in1=c["rowmask"][:, g * 8:(g + 1) * 8].unsqueeze(2).broadcast_to([128, 8, 128]), op=ALU.mult),
                    reads=(("kd",), ("c", "rowmask")), writes=(("kdm",),))
                for bb in range(8):
                    bank = bb % 2
                    sq = g * 8 + bb
                    P.op("pe", lambda e, bb=bb, bank=bank: e.matmul(b.sreg(bank, 128), lhsT=A.kdm[:, bb, :],
                                                                    rhs=A.v[:, 8, vs], start=True, stop=True),
                         reads=(("kdm",), A.k_v), writes=(b.skey(bank),))
                    P.op("dve", lambda e, bb=bb, bank=bank: e.tensor_tensor(out=A.Sst[:, bb, :], in0=b.sreg(bank, 128),
                                                                            in1=A.Sst[:, bb, :], op=ALU.add),
                         reads=(b.skey(bank), ("Sst",)), writes=(("Sst",),))
                    P.op("dve", lambda e, bb=bb, sq=sq: e.tensor_scalar(out=A.Sst[:, bb, :], in0=A.Sst[:, bb, :],
                                                                        scalar1=A.sm[:, 16 + sq:17 + sq], scalar2=None,
                                                                        op0=ALU.mult),
                         reads=(("Sst",), ("sm", "b")), writes=(("Sst",),))
                P.dma("sp", b.o_hg_s[ti, g * 8:(g + 1) * 8, hh].rearrange("b p e -> p b e"), A.Sst[:],
                      reads=(("Sst",),), writes=(("dram", "hgs"),), semkey=("Sst_out",))
        P.op("dve", lambda e: e.tensor_tensor(out=otot[:], in0=b.pb[5][:, 0:128], in1=A.osb[:], op=ALU.add),
             reads=(("pb", 5), ("osb",)), writes=(kot,))

        def back(tok=tok, otot=otot, kot=kot):
            P.op("act", lambda e: e.activation(out=A.junk[:], in_=otot[:], func=AF.Square, accum_out=A.ss[:, 0:1]),
                 reads=(kot,), writes=(("junk",), ("ss", 0)))
            P.op("act", lambda e: e.activation(out=A.ss[:, 1:2], in_=A.ss[:, 0:1], func=AF.Sqrt, scale=1.0 / 128.0,
                                               bias=c["eps"][:]), reads=(("ss", 0), ("c", "eps")), writes=(("ss", 1),))
            P.op("dve", lambda e: e.reciprocal(out=A.ss[:, 2:3], in_=A.ss[:, 1:2]), reads=(("ss", 1),), writes=(("ss", 2),))
            P.op("dve", lambda e: e.tensor_scalar(out=A.on[:], in0=otot[:], scalar1=A.ss[:, 2:3], scalar2=None, op0=ALU.mult),
                 reads=(kot, ("ss", 2)), writes=(("on",),))
            P.op("pe", lambda e: e.transpose(out=b.pt[0][:, 0:128], in_=A.on[:], identity=c["ident_bf"][:]),
                 reads=(("on",), ("c", "ident_bf")), writes=(("pt", 0),))
            P.op("dve", lambda e: e.scalar_tensor_tensor(out=A.mst[:, s, tok], in0=b.pt[0][:, 0:128], scalar=c["hgw"][:, 0:1],
                                                         in1=A.gT[:, s, tok], op0=ALU.mult, op1=ALU.mult),
                 reads=(("pt", 0), A.k_gT, ("c", "hgw")), writes=(("mst",),))

        if pending is not None:
            pending()
        pending = back
    pending()
    P.dma("sp", b.o_hg_p[hh], A.S[:], reads=(("S",),), writes=(("dram", "hgp", hh),), semkey=("Sout",))
```
